# Optimizing a Trainium2 kernel written in Bass

```python
import jax, jax.numpy as jnp
from jax import lax
import numpy as np

D_MODEL = 1024
BATCH = 16
SEQ = 4096
DEPTH = 1

CHUNK = 64
M_HEADS = 4
M_HEAD_DIM = 256
M_WIDTH = M_HEADS * M_HEAD_DIM
SB_HEADS = 8
SB_HEAD_DIM = 128
SB_WIDTH = SB_HEADS * SB_HEAD_DIM
CONV_WIDTH = 4
Q_BLOCK = 128
N_BRANCH = 2
EPS = 1e-6
SPLIT_SIZES = (M_WIDTH, M_WIDTH, M_WIDTH,
               M_HEADS, M_HEADS,
               M_WIDTH, M_WIDTH,
               SB_WIDTH, SB_WIDTH, SB_WIDTH,
               SB_WIDTH,
               N_BRANCH * D_MODEL)
IN_COLS = 5 * M_WIDTH + 2 * M_HEADS + 4 * SB_WIDTH + N_BRANCH * D_MODEL

kernel_name = "hybrid_mlstm_stickbreaking_gated_merge"


def rmsnorm(x, w):
    xf = x.astype(jnp.float32)
    y = xf * lax.rsqrt(jnp.mean(xf * xf, axis=-1, keepdims=True) + EPS)
    return (y * w.astype(jnp.float32)).astype(x.dtype)


def causal_depthwise_conv(x, w, b):
    k = w.shape[0]
    out = lax.conv_general_dilated(
        x, w[:, None, :].astype(x.dtype), window_strides=(1,), padding=[(k - 1, 0)],
        dimension_numbers=("NWC", "WIO", "NWC"), feature_group_count=x.shape[-1])
    return out + b


def to_heads(x, n_heads):
    b, s, _ = x.shape
    return x.reshape(b, s, n_heads, -1).transpose(0, 2, 1, 3)


def mlstm_chunkwise(q, k, v, log_i, log_f):
    b_, h_, s_, d_ = q.shape
    nc = s_ // CHUNK
    f32 = jnp.float32

    def chunked(a):
        a = a.reshape(b_, h_, nc, CHUNK, *a.shape[3:])
        return jnp.moveaxis(a, 2, 0)

    qc = chunked(q.astype(f32))
    kc = chunked(k.astype(f32) * (d_ ** -0.5))
    vc = chunked(v.astype(f32))
    ic = chunked(log_i.astype(f32))
    fc = chunked(log_f.astype(f32))
    tril = jnp.tril(jnp.ones((CHUNK, CHUNK), dtype=bool))

    def step(carry, xs):
        C, n, m = carry
        qb, kb, vb, ib, fb = xs
        bcum = jnp.cumsum(fb, axis=-1)
        dmat = bcum[..., :, None] - bcum[..., None, :] + ib[..., None, :]
        dmat = jnp.where(tril, dmat, -jnp.inf)
        inter = bcum + m[..., None]
        m_t = jnp.maximum(inter, jnp.max(dmat, axis=-1))
        w_intra = jnp.exp(dmat - m_t[..., None])
        w_inter = jnp.exp(inter - m_t)
        scores = jnp.einsum('bhtd,bhsd->bhts', qb, kb) * w_intra
        num = (jnp.einsum('bhts,bhse->bhte', scores, vb)
               + w_inter[..., None] * jnp.einsum('bhtd,bhde->bhte', qb, C))
        den = jnp.sum(scores, axis=-1) + w_inter * jnp.einsum('bhtd,bhd->bht', qb, n)
        h = num / jnp.maximum(jnp.abs(den), jnp.exp(-m_t))[..., None]
        b_last = bcum[..., -1]
        decay = b_last[..., None] - bcum + ib
        m_new = jnp.maximum(b_last + m, jnp.max(decay, axis=-1))
        ws = jnp.exp(decay - m_new[..., None])
        carry_scale = jnp.exp(b_last + m - m_new)
        kw = kb * ws[..., None]
        C_new = carry_scale[..., None, None] * C + jnp.einsum('bhsd,bhse->bhde', kw, vb)
        n_new = carry_scale[..., None] * n + jnp.sum(kw, axis=2)
        return (C_new, n_new, m_new), h

    init = (jnp.zeros((b_, h_, d_, d_), f32), jnp.zeros((b_, h_, d_), f32),
            jnp.zeros((b_, h_), f32))
    _, hs = lax.scan(step, init, (qc, kc, vc, ic, fc))
    return jnp.moveaxis(hs, 0, 2).reshape(b_, h_, s_, d_)


def stick_breaking_attention(q, k, v):
    _, _, s_, d_ = q.shape
    scale = d_ ** -0.5
    outs = []
    for blk in range(s_ // Q_BLOCK):
        t0, t1 = blk * Q_BLOCK, (blk + 1) * Q_BLOCK
        qb, kb, vb = q[:, :, t0:t1], k[:, :, :t1], v[:, :, :t1]
        z = jnp.einsum('bhtd,bhsd->bhts', qb, kb).astype(jnp.float32) * scale
        t_idx = t0 + jnp.arange(Q_BLOCK)[:, None]
        s_idx = jnp.arange(t1)[None, :]
        causal = s_idx < t_idx
        log_beta = jax.nn.log_sigmoid(z)
        log_1mb = jnp.where(causal, jax.nn.log_sigmoid(-z), 0.0)
        between = lax.cumsum(log_1mb, axis=3, reverse=True) - log_1mb
        a = jnp.where(causal, jnp.exp(log_beta + between), 0.0)
        outs.append(jnp.einsum('bhts,bhsd->bhtd', a.astype(v.dtype), vb))
    return jnp.concatenate(outs, axis=2)


def setup_inputs(seed: int = 0) -> dict:
    key = jax.random.key(seed)
    ks = jax.random.split(key, 13)
    f32 = jnp.float32
    x = jax.random.normal(ks[0], (BATCH, SEQ, D_MODEL), f32)
    norm_w = 1.0 + 0.02 * jax.random.normal(ks[1], (D_MODEL,), f32)
    w_in = jax.random.normal(ks[2], (D_MODEL, IN_COLS), f32) * D_MODEL ** -0.5
    b_in = 0.02 * jax.random.normal(ks[3], (IN_COLS,), f32)
    f_start = 3 * M_WIDTH + M_HEADS
    b_in = b_in.at[f_start:f_start + M_HEADS].add(jnp.linspace(3.0, 6.0, M_HEADS, dtype=f32))
    conv_w = jax.random.normal(ks[4], (CONV_WIDTH, 2 * M_WIDTH), f32) * CONV_WIDTH ** -0.5
    conv_b = 0.02 * jax.random.normal(ks[5], (2 * M_WIDTH,), f32)
    mlstm_norm_w = 1.0 + 0.02 * jax.random.normal(ks[6], (M_WIDTH,), f32)
    sb_q_norm_w = 1.0 + 0.02 * jax.random.normal(ks[7], (SB_HEAD_DIM,), f32)
    sb_k_norm_w = 1.0 + 0.02 * jax.random.normal(ks[8], (SB_HEAD_DIM,), f32)
    w_proj_m = jax.random.normal(ks[9], (M_WIDTH, D_MODEL), f32) * M_WIDTH ** -0.5
    w_proj_s = jax.random.normal(ks[10], (SB_WIDTH, D_MODEL), f32) * SB_WIDTH ** -0.5
    w_out = jax.random.normal(ks[11], (D_MODEL, D_MODEL), f32) * D_MODEL ** -0.5
    return {"x": x, "norm_w": norm_w, "w_in": w_in, "b_in": b_in,
            "conv_w": conv_w, "conv_b": conv_b, "mlstm_norm_w": mlstm_norm_w,
            "sb_q_norm_w": sb_q_norm_w, "sb_k_norm_w": sb_k_norm_w,
            "w_proj_m": w_proj_m, "w_proj_s": w_proj_s, "w_out": w_out}


def reference(x, norm_w, w_in, b_in, conv_w, conv_b, mlstm_norm_w,
              sb_q_norm_w, sb_k_norm_w, w_proj_m, w_proj_s, w_out):
    b_, s_, _ = x.shape
    split_at = [int(c) for c in np.cumsum(SPLIT_SIZES)[:-1]]
    for _layer in range(DEPTH):
        h = rmsnorm(x, norm_w)
        proj = jnp.einsum('bsd,de->bse', h, w_in) + b_in
        (mq, mk, mv, mi, mf, mo, mz, sq, sk, sv, sz, gates) = jnp.split(proj, split_at, axis=-1)

        qk = jax.nn.silu(causal_depthwise_conv(jnp.concatenate([mq, mk], axis=-1), conv_w, conv_b))
        mq_c, mk_c = jnp.split(qk, 2, axis=-1)
        log_i = mi.astype(jnp.float32).transpose(0, 2, 1)
        log_f = jax.nn.log_sigmoid(mf.astype(jnp.float32)).transpose(0, 2, 1)
        hm = mlstm_chunkwise(to_heads(mq_c, M_HEADS), to_heads(mk_c, M_HEADS),
                             to_heads(mv, M_HEADS), log_i, log_f)
        hm = rmsnorm(hm.transpose(0, 2, 1, 3), mlstm_norm_w.reshape(M_HEADS, M_HEAD_DIM))
        y_m = (hm.reshape(b_, s_, M_WIDTH).astype(x.dtype)
               * jax.nn.sigmoid(mo) * jax.nn.silu(mz))

        qs = rmsnorm(to_heads(sq, SB_HEADS), sb_q_norm_w)
        ks_ = rmsnorm(to_heads(sk, SB_HEADS), sb_k_norm_w)
        os_ = stick_breaking_attention(qs, ks_, to_heads(sv, SB_HEADS))
        y_s = os_.transpose(0, 2, 1, 3).reshape(b_, s_, SB_WIDTH) * jax.nn.silu(sz)

        g_m, g_s = jnp.split(jax.nn.sigmoid(gates), N_BRANCH, axis=-1)
        merged = (g_m * jnp.einsum('bse,ed->bsd', y_m, w_proj_m)
                  + g_s * jnp.einsum('bse,ed->bsd', y_s, w_proj_s))
        x = x + jnp.einsum('bsd,de->bse', merged, w_out)
    return x
```

```python
import math
import numpy as np
import concourse.bass as bass
import concourse.mybir as mybir
from concourse.bass_utils import run_bass_kernel_spmd

F32 = mybir.dt.float32
BF16 = mybir.dt.bfloat16
AF = mybir.ActivationFunctionType
ALU = mybir.AluOpType
AX = mybir.AxisListType

D = 1024
IN_COLS = 11272
EPS = 1e-6
OFF = dict(mq=0, mk=1024, mv=2048, mi=3072, mf=3076, mo=3080, mz=4104,
           sq=5128, sk=6152, sv=7176, sz=8200, g=9224)
LN16 = math.log(16.0)


class Buf:
    __slots__ = ("name", "w", "r", "excl")

    def __init__(self, name="", excl=False):
        self.name = name
        self.w = None
        self.r = {}
        self.excl = excl


class Op:
    __slots__ = ("eng", "tl", "idx", "fn", "waits", "signal", "clock", "dma", "semval")


ENGS = ["pe", "act", "dve", "pool", "sp"]


class Prog:
    def __init__(self, nc, ndma=8):
        self.nc = nc
        self.ops = {e: [] for e in ENGS}
        self.tl_ops = {}
        self.known = {e: {} for e in ENGS}
        self.ndma = ndma
        self.dma_count = {e: 0 for e in ENGS}
        self.bar_ops = []
        self.bar_gen = 0
        self.eng_gen = {e: 0 for e in ENGS}

    def barrier(self):
        self.bar_ops = [lst[-1] for lst in self.tl_ops.values() if lst]
        self.bar_gen += 1

    def add(self, eng, fn, r=(), w=(), dma=False):
        op = Op()
        op.eng = eng
        op.fn = fn
        op.dma = dma
        op.signal = dma
        deps = []
        for b in r:
            if b.w is not None:
                deps.append((b.w, True))
            if b.excl:
                for o in b.r.values():
                    deps.append((o, False))
        for b in w:
            if b.w is not None:
                deps.append((b.w, False))
            for o in b.r.values():
                deps.append((o, False))
        if self.eng_gen[eng] < self.bar_gen:
            self.eng_gen[eng] = self.bar_gen
            for o in self.bar_ops:
                deps.append((o, True))
        if dma:
            j = self.dma_count[eng]
            self.dma_count[eng] += 1
            tl = (eng, j % self.ndma)
            prev = self.tl_ops.get(tl)
            if prev:
                deps.append((prev[-1], True))
        else:
            tl = eng
        lst = self.tl_ops.setdefault(tl, [])
        op.tl = tl
        op.idx = len(lst)
        kn = self.known[eng]
        best = {}
        for d, raw in deps:
            if d.tl == eng:
                if eng == "pe" or not raw:
                    continue
            if kn.get(d.tl, -1) >= d.idx:
                continue
            if d.tl not in best or best[d.tl].idx < d.idx:
                best[d.tl] = d
        waits = []
        for d in best.values():
            if kn.get(d.tl, -1) >= d.idx:
                continue
            waits.append(d)
            d.signal = True
            for t, i in d.clock.items():
                if kn.get(t, -1) < i:
                    kn[t] = i
            if kn.get(d.tl, -1) < d.idx:
                kn[d.tl] = d.idx
        op.waits = waits
        op.clock = dict(kn)
        lst.append(op)
        self.ops[eng].append(op)
        for b in r:
            b.r[tl] = op
        for b in w:
            b.w = op
            b.r = {}
        return op

    def mm(self, out, lhsT, rhs, start, stop, r, w):
        return self.add("pe", lambda e: e.matmul(out, lhsT, rhs, start=start, stop=stop,
                                                 skip_group_check=True), r, w)

    def tr(self, out, in_, ident, r, w):
        return self.add("pe", lambda e: e.transpose(out, in_, ident), r, w)

    def act(self, out, in_, func, r, w, bias=None, scale=None, accum_out=None):
        kw = {}
        if bias is not None:
            kw["bias"] = bias
        if scale is not None:
            kw["scale"] = scale
        if accum_out is not None:
            kw["accum_out"] = accum_out
        return self.add("act", lambda e: e.activation(out, in_, func, **kw), r, w)

    def v(self, eng, name, *args, r=(), w=(), **kw):
        return self.add(eng, lambda e: getattr(e, name)(*args, **kw), r, w)

    def dma(self, eng, out, in_, r, w, slow=False):
        if slow:
            return self.add(eng, lambda e: e.dma_start(out=out, in_=in_, allow_slow_non_contiguous=True),
                            r, w, dma=True)
        return self.add(eng, lambda e: e.dma_start(out=out, in_=in_), r, w, dma=True)

    def emit(self):
        nc = self.nc
        for tl, lst in self.tl_ops.items():
            if isinstance(tl, tuple):
                for o in lst:
                    o.semval = 16 * (o.idx + 1)
            else:
                c = 0
                for o in lst:
                    if o.signal:
                        c += 1
                    o.semval = c
        sems = {}
        import contextlib
        with contextlib.ExitStack() as st:
            for tl in self.tl_ops:
                nm = tl if isinstance(tl, str) else "%s_d%d" % tl
                sems[tl] = st.enter_context(nc.semaphore("s_" + nm))
            block = st.enter_context(nc.Block())
            handles = {"pe": block.tensor, "act": block.scalar, "dve": block.vector,
                       "pool": block.gpsimd, "sp": block.sync}

            def make(engname):
                def body(e):
                    for o in self.ops[engname]:
                        for d in o.waits:
                            e.wait_ge(sems[d.tl], d.semval)
                        ins = o.fn(e)
                        if o.signal:
                            ins.then_inc(sems[o.tl], 16 if o.dma else 1)
                    for tl, lst in self.tl_ops.items():
                        if isinstance(tl, tuple) and tl[0] == engname and lst:
                            e.wait_ge(sems[tl], lst[-1].semval)
                return body

            for en in ENGS:
                if self.ops[en]:
                    handles[en](make(en))


def _consts():
    c = np.zeros((128, 1536), np.float32)
    j = np.arange(128)[:, None]
    s = np.arange(128)[None, :]
    c[:, 0:128] = (j == s)
    c[:, 128:256] = -(j >= s).astype(np.float32)
    c[:, 256:384] = -1.0
    c[:, 384:512] = 1.0
    c[:, 512:640] = (j <= s)
    cc = np.arange(896)[None, :]
    c[:, 640:1536] = ((cc - 384) > j)
    c4 = np.zeros((4, 520), np.float32)
    c4[:, 0:4] = np.eye(4)
    for h in range(4):
        c4[h, 8 + h * 128: 8 + (h + 1) * 128] = 1.0
    return c, c4


def build(S, NSEQ, dbg=None):
    dbg = dbg or set()
    nc = bass.Bass("TRN2", target_bir_lowering=False)
    NT = S // 512
    NCH = S // 128
    dt = nc.dram_tensor
    x = dt("x", [NSEQ, S, D], F32, kind="ExternalInput").ap()
    norm_w = dt("norm_w", [D], F32, kind="ExternalInput").ap()
    w_in = dt("w_in", [D, IN_COLS], F32, kind="ExternalInput").ap()
    b_in = dt("b_in", [IN_COLS], F32, kind="ExternalInput").ap()
    conv_w = dt("conv_w", [4, 2048], F32, kind="ExternalInput").ap()
    conv_b = dt("conv_b", [2048], F32, kind="ExternalInput").ap()
    mnw = dt("mlstm_norm_w", [1024], F32, kind="ExternalInput").ap()
    sqw = dt("sb_q_norm_w", [128], F32, kind="ExternalInput").ap()
    skw = dt("sb_k_norm_w", [128], F32, kind="ExternalInput").ap()
    w_pm = dt("w_proj_m", [D, D], F32, kind="ExternalInput").ap()
    w_ps = dt("w_proj_s", [D, D], F32, kind="ExternalInput").ap()
    w_out = dt("w_out", [D, D], F32, kind="ExternalInput").ap()
    cst = dt("cst", [128, 1536], F32, kind="ExternalInput").ap()
    cst4 = dt("cst4", [4, 520], F32, kind="ExternalInput").ap()
    out = dt("out", [NSEQ, S, D], F32, kind="ExternalOutput").ap()
    ys_d = dt("ys_scr", [NSEQ, 8, 128, S], BF16, kind="Internal").ap()
    ym_d = dt("ym_scr", [NSEQ, 8, 128, S], BF16, kind="Internal").ap()
    dbg_t = {}
    if "hT" in dbg:
        dbg_t["hT"] = dt("dbg_hT", [128, 8, S], BF16, kind="ExternalOutput").ap()
    if "ys" in dbg:
        dbg_t["ys"] = dt("dbg_ys", [8, 128, S], BF16, kind="ExternalOutput").ap()
    if "ym" in dbg:
        dbg_t["ym"] = dt("dbg_ym", [8, 128, S], BF16, kind="ExternalOutput").ap()
    if "gates" in dbg:
        dbg_t["colv"] = dt("dbg_colv", [128, 3 * NCH * 4], F32, kind="ExternalOutput").ap()
        dbg_t["carry"] = dt("dbg_carry", [128, 4 * NCH], F32, kind="ExternalOutput").ap()

    w3 = w_in.rearrange("(kc p) n -> p kc n", p=128)

    import contextlib
    with contextlib.ExitStack() as st:
        TOTAL = 104000
        big = st.enter_context(nc.sbuf_tensor("big", [128, TOTAL], BF16))
        psum = st.enter_context(nc.psum_tensor("psum_all", [128, 4096], F32))
        banks = [psum[:, i * 512:(i + 1) * 512] for i in range(8)]
        bankB = [Buf("bank%d" % i, excl=True) for i in range(8)]

        class Arena:
            def __init__(self, lo, hi):
                self.lo, self.hi, self.p = lo, hi, lo

            def reset(self):
                self.p = self.lo

            def alloc(self, shape, dtype, parts=128):
                n = 1
                for d_ in shape:
                    n *= d_
                nb = n * (2 if dtype == BF16 else 4)
                nb = (nb + 63) // 64 * 64
                ne = nb // 2
                assert self.p + ne <= self.hi, ("arena overflow", self.p, ne, self.hi)
                v = big[0:parts, self.p:self.p + ne]
                self.p += ne
                if dtype != BF16:
                    v = v.bitcast(dtype)
                v = v[:, 0:n]
                if len(shape) == 2:
                    v = v.rearrange("p (a b) -> p a b", b=shape[1])
                elif len(shape) == 3:
                    v = v.rearrange("p (a b c) -> p a b c", b=shape[1], c=shape[2])
                return v

        pers = Arena(0, 40000)
        ar = Arena(40000, TOTAL)

        P = Prog(nc)

        hT = pers.alloc([8, S], BF16)
        hTB = [Buf("hT%d" % i) for i in range(NT)]
        cb = pers.alloc([1536], BF16)
        cbB = Buf("cb")
        c4 = pers.alloc([520], F32, parts=4)
        c4B = Buf("c4")
        normw_bc = pers.alloc([1024], F32)
        nwB = Buf("normw")
        cols = pers.alloc([64], F32)
        colsB = Buf("cols")
        colv = pers.alloc([3 * NCH * 4], F32)
        colvB = Buf("colv")
        carry_bc = pers.alloc([4 * NCH], F32)
        carryB = Buf("carry")
        mcols = pers.alloc([16 * 6 + 8], F32)
        mcolsB = Buf("mcols")
        gb = pers.alloc([4], F32, parts=4)
        gbB = Buf("gb")

        mhalf = pers.alloc([2], F32)
        mhB = Buf("mhalf")
        P.v("pool", "memset", mhalf, -0.5, r=[], w=[mhB])
        epsq = pers.alloc([2], F32)
        P.v("pool", "memset", epsq, 128.0 * EPS, r=[], w=[mhB])
        onec = pers.alloc([2], F32)
        P.v("pool", "memset", onec, 1.0, r=[], w=[mhB])
        ml16 = pers.alloc([2], F32)
        P.v("pool", "memset", ml16, -LN16, r=[], w=[gbB])
        ysB = [[[Buf() for _ in range(NT)] for _ in range(8)] for _ in range(NSEQ)]
        ymB = [[[Buf() for _ in range(NT)] for _ in range(8)] for _ in range(NSEQ)]
        ident = cb[:, 0:128]
        negU = cb[:, 128:256]
        negones = cb[:, 256:384]
        ones_b = cb[:, 384:512]
        trimask = cb[:, 512:640]
        sbmask = cb[:, 640:1536]
        ident4 = c4[:, 0:4]

        P.dma("pool", cb, cst, r=[], w=[cbB])
        P.dma("sp", c4, cst4, r=[], w=[c4B])
        P.dma("sp", normw_bc, norm_w.partition_broadcast(128), r=[], w=[nwB])
        with nc.allow_non_contiguous_dma(reason="tiny one-time bias/gain column loads"):
            for i, key in enumerate(["sq", "sk", "sz"]):
                P.dma("sp", cols[:, 8 * i:8 * i + 8],
                      b_in[OFF[key]:OFF[key] + 1024].rearrange("(h p) -> p h", p=128), r=[], w=[colsB], slow=True)
            P.dma("sp", cols[:, 24:25], sqw.unsqueeze(1), r=[], w=[colsB], slow=True)
            P.dma("sp", cols[:, 25:26], skw.unsqueeze(1), r=[], w=[colsB], slow=True)
            for j in range(4):
                P.dma("sp", mcols[:, j * 16:(j + 1) * 16], conv_w[j].rearrange("(c p) -> p c", p=128),
                      r=[], w=[mcolsB], slow=True)
            P.dma("sp", mcols[:, 64:80], conv_b.rearrange("(c p) -> p c", p=128), r=[], w=[mcolsB], slow=True)
            P.dma("sp", mcols[:, 80:96], b_in[0:2048].rearrange("(c p) -> p c", p=128), r=[], w=[mcolsB], slow=True)
            P.dma("sp", gb[:, 0:1], b_in[OFF["mi"]:OFF["mi"] + 4].unsqueeze(1), r=[], w=[gbB], slow=True)
            P.dma("sp", gb[:, 1:2], b_in[OFF["mf"]:OFF["mf"] + 4].unsqueeze(1), r=[], w=[gbB], slow=True)
        P.v("dve", "tensor_scalar", cols[:, 25:26], cols[:, 25:26], math.sqrt(128.0), None, ALU.mult,
            r=[colsB], w=[colsB])
        P.v("dve", "tensor_scalar", cols[:, 32:40], cols[:, 0:8], cols[:, 24:25], None, ALU.mult,
            r=[colsB], w=[colsB])
        P.v("dve", "tensor_scalar", cols[:, 40:48], cols[:, 8:16], cols[:, 25:26], None, ALU.mult,
            r=[colsB], w=[colsB])
        P.v("dve", "tensor_scalar", gb[:, 2:3], gb[:, 1:2], -1.0, None, ALU.mult, r=[gbB], w=[gbB])

        def load_w(eng, tile, col0, ncols, bufs):
            return P.dma(eng, tile, w3[:, :, col0:col0 + ncols], r=[], w=bufs)

        for b in range(NSEQ):
            P.barrier()
            ar.reset()
            xb = [ar.alloc([1024], F32) for _ in range(2)]
            xbB = [Buf("xb") for _ in range(2)]
            junk = ar.alloc([1024], BF16)
            junkB = Buf("junk")
            ssq = [ar.alloc([2], F32) for _ in range(2)]
            ssqB = [Buf("ssq") for _ in range(2)]
            xn = [ar.alloc([1024], BF16) for _ in range(2)]
            xnB = [Buf("xn") for _ in range(2)]
            tp = banks[7][:, :].bitcast(BF16).rearrange("p (a b) -> p a b", b=128)
            for i in range(NCH):
                k = i % 2
                P.dma("sp", xb[k], x[b, i * 128:(i + 1) * 128, :], r=[], w=[xbB[k]])
                P.act(junk, xb[k], AF.Square, r=[xbB[k]], w=[junkB, ssqB[k]], accum_out=ssq[k][:, 0:1])
                P.v("dve", "tensor_scalar", ssq[k][:, 1:2], ssq[k][:, 0:1], 1.0 / D, EPS, ALU.mult, ALU.add,
                    r=[ssqB[k]], w=[ssqB[k]])
                P.v("pool", "tensor_tensor", ssq[k][:, 1:2], ssq[k][:, 1:2], mhalf[:, 0:1], ALU.pow,
                    r=[ssqB[k], mhB], w=[ssqB[k]])
                P.v("dve", "scalar_tensor_tensor", xn[k], xb[k], ssq[k][:, 1:2], normw_bc, ALU.mult, ALU.mult,
                    r=[xbB[k], ssqB[k], nwB], w=[xnB[k]])
                for kc in range(8):
                    P.tr(tp[:, kc, :], xn[k][:, kc * 128:(kc + 1) * 128], ident,
                         r=[xnB[k], cbB], w=[bankB[7]])
                P.act(hT[:, :, i * 128:(i + 1) * 128], tp, AF.Copy, r=[bankB[7]], w=[hTB[i // 4]])
            if "hT" in dbg:
                P.dma("sp", dbg_t["hT"], hT, r=hTB, w=[Buf()])

            if "noA" not in dbg:
                P.barrier()
                ar.reset()
                Wt = [[ar.alloc([8, 128], BF16) for _ in range(4)] for _ in range(2)]
                WtB = [[Buf("Wt") for _ in range(4)] for _ in range(2)]
                qT2 = [ar.alloc([S], BF16) for _ in range(2)]
                kT2 = [ar.alloc([S], BF16) for _ in range(2)]
                szT2 = [ar.alloc([S], BF16) for _ in range(2)]
                vv2 = [ar.alloc([NCH, 128], BF16) for _ in range(2)]
                sqv = [ar.alloc([512], BF16) for _ in range(2)]
                qb = [ar.alloc([512], F32) for _ in range(2)]
                lnr = [ar.alloc([512], F32) for _ in range(2)]
                sg = [ar.alloc([512], F32) for _ in range(2)]
                ebuf = [ar.alloc([1024], F32) for _ in range(2)]
                Lb = [ar.alloc([1024], BF16) for _ in range(3)]
                Ab = [ar.alloc([1024], BF16) for _ in range(3)]
                Lacc = [ar.alloc([512], BF16) for _ in range(2)]
                yt = [ar.alloc([512], BF16) for _ in range(2)]
                sqvB = [Buf() for _ in range(2)]
                qbB = [Buf() for _ in range(2)]
                lnrB = [Buf() for _ in range(2)]
                sgB = [Buf() for _ in range(2)]
                eB = [Buf() for _ in range(2)]
                LB = [Buf() for _ in range(3)]
                AB = [Buf() for _ in range(3)]
                LaccB = [Buf() for _ in range(2)]
                ytB = [Buf() for _ in range(2)]
                SEG = ["sq", "sk", "sv", "sz"]
                brow = ar.alloc([1024], BF16, parts=1)
                browB = Buf("brow")
                P.dma("pool", brow, b_in[OFF["sv"]:OFF["sv"] + 1024].unsqueeze(0), r=[], w=[browB])

                def loadA(h):
                    for i_, key in enumerate(SEG):
                        load_w("pool", Wt[h % 2][i_], OFF[key] + h * 128, 128, [WtB[h % 2][i_]])

                loadA(0)
                NHA = 1 if "h1" in dbg else 8
                if NHA > 1:
                    loadA(1)
                cnt = [0]
                TB = {}

                def make_pieces(h):
                    hp = h % 2
                    Wq, Wk, Wv, Wz = Wt[hp]
                    WqB, WkB, WvB, WzB = WtB[hp]
                    qT, kT, szT, vv = qT2[hp], kT2[hp], szT2[hp], vv2[hp]
                    TB[h] = dict(q=[Buf() for _ in range(NT)], k=[Buf() for _ in range(NT)],
                                 z=[Buf() for _ in range(NT)], v=[Buf() for _ in range(NT)])
                    qTB, kTB, szB, vB = TB[h]["q"], TB[h]["k"], TB[h]["z"], TB[h]["v"]

                    def proj_fm(Wtile, WB, tt, pb):
                        for kc in range(8):
                            P.mm(banks[pb][:, :], Wtile[:, kc, :], hT[:, kc, tt * 512:(tt + 1) * 512],
                                 kc == 0, kc == 7, r=[WB, hTB[tt]], w=[bankB[pb]])

                    def z_piece(tt):
                        pb = 6 + (cnt[0] % 2)
                        k2 = cnt[0] % 2
                        cnt[0] += 1
                        proj_fm(Wz, WzB, tt, pb)
                        bz = cols[:, 16 + h:17 + h]
                        P.act(sg[k2], banks[pb][:, :], AF.Sigmoid, bias=bz, r=[bankB[pb], colsB], w=[sgB[k2]])
                        P.v("dve", "scalar_tensor_tensor", szT[:, tt * 512:(tt + 1) * 512], banks[pb][:, :], bz,
                            sg[k2], ALU.add, ALU.mult, r=[bankB[pb], sgB[k2], colsB], w=[szB[tt]])

                    def v_piece(g):
                        pb = 6 + (cnt[0] % 2)
                        cnt[0] += 1
                        for c4_ in range(4):
                            j = g * 4 + c4_
                            o_ = banks[pb][:, c4_ * 128:(c4_ + 1) * 128]
                            for kc in range(8):
                                P.mm(o_, hT[:, kc, j * 128:(j + 1) * 128], Wv[:, kc, :], kc == 0, False,
                                     r=[WvB, hTB[g]], w=[bankB[pb]])
                            P.mm(o_, ones_b[0:1, 0:128], brow[0:1, h * 128:(h + 1) * 128],
                                 False, True, r=[cbB, browB], w=[bankB[pb]])
                        P.v("dve", "tensor_copy", vv[:, g * 4:(g + 1) * 4, :],
                            banks[pb][:, :].rearrange("p (a b) -> p a b", b=128),
                            r=[bankB[pb]], w=[vB[g]])

                    def qk_piece(which, tt):
                        if which == 0:
                            Wx, WxB, dest, destB = Wq, WqB, qT, qTB
                            wc, bc = cols[:, 24:25], cols[:, h:h + 1]
                        else:
                            Wx, WxB, dest, destB = Wk, WkB, kT, kTB
                            wc, bc = cols[:, 25:26], cols[:, 8 + h:9 + h]
                        k2 = cnt[0] % 2
                        cnt[0] += 1
                        proj_fm(Wx, WxB, tt, 6)
                        P.v("dve", "tensor_scalar", qb[k2], banks[6][:, :], bc, None, ALU.add,
                            r=[bankB[6], colsB], w=[qbB[k2]])
                        P.v("pool", "tensor_tensor", sqv[k2], qb[k2], qb[k2], ALU.mult, r=[qbB[k2]], w=[sqvB[k2]])
                        P.mm(banks[7][:, :], ones_b, sqv[k2], True, True, r=[cbB, sqvB[k2]], w=[bankB[7]])
                        P.act(lnr[k2], banks[7][:, :], AF.Ln, bias=epsq[:, 0:1], r=[bankB[7], mhB], w=[lnrB[k2]])
                        P.act(lnr[k2], lnr[k2], AF.Exp, scale=-0.5, r=[lnrB[k2]], w=[lnrB[k2]])
                        P.v("dve", "scalar_tensor_tensor", dest[:, tt * 512:(tt + 1) * 512], qb[k2], wc, lnr[k2],
                            ALU.mult, ALU.mult, r=[qbB[k2], colsB, lnrB[k2]], w=[destB[tt]])

                    pcs = []
                    for tt in range(NT):
                        pcs.append(lambda tt=tt: qk_piece(1, tt))
                        pcs.append(lambda tt=tt: v_piece(tt))
                        pcs.append(lambda tt=tt: qk_piece(0, tt))
                        pcs.append(lambda tt=tt: z_piece(tt))
                    return pcs

                for pc in make_pieces(0):
                    pc()
                for h in range(NHA):
                    if 1 <= h and h + 1 < NHA:
                        loadA(h + 1)
                    nxt = make_pieces(h + 1) if h + 1 < NHA else []
                    hp = h % 2
                    qT, kT, szT, vv = qT2[hp], kT2[hp], szT2[hp], vv2[hp]
                    qTB, kTB, szB, vB = TB[h]["q"], TB[h]["k"], TB[h]["z"], TB[h]["v"]
                    steps = [(qt, kbH) for qt in range(NT) for kbH in range(4 * qt + 3, 0, -2)]
                    NS = len(steps)
                    cur = [0]

                    def mask_of(kb, qt):
                        i_ = kb - 4 * qt
                        if i_ < 0:
                            return None
                        return sbmask[:, 384 - 128 * i_:384 - 128 * i_ + 512]

                    def stA1_mm(n):
                        qt, kbH = steps[n]
                        z = n % 2
                        for j_, kb in enumerate((kbH, kbH - 1)):
                            P.mm(banks[2 * z + j_][:, :], kT[:, kb * 128:(kb + 1) * 128],
                                 qT[:, qt * 512:(qt + 1) * 512], True, True,
                                 r=[kTB[kb // 4], qTB[qt]], w=[bankB[2 * z + j_]])

                    def stA1_act(n):
                        z = n % 2
                        P.act(ebuf[n % 2], psum[:, 2 * z * 512:(2 * z + 2) * 512], AF.Exp,
                              r=[bankB[2 * z], bankB[2 * z + 1]], w=[eB[n % 2]])

                    def stA2(n):
                        qt, kbH = steps[n]
                        P.act(Lb[n % 3], ebuf[n % 2], AF.Ln, bias=onec[:, 0:1], r=[eB[n % 2], mhB], w=[LB[n % 3]])
                        for j_, kb in enumerate((kbH, kbH - 1)):
                            m = mask_of(kb, qt)
                            if m is not None:
                                hv = Lb[n % 3][:, j_ * 512:(j_ + 1) * 512]
                                P.v("dve", "tensor_tensor", hv, hv, m, ALU.mult, r=[LB[n % 3], cbB], w=[LB[n % 3]])

                    def stB(n):
                        qt, kbH = steps[n]
                        z = n % 2
                        first = (kbH == 4 * qt + 3)
                        last = (kbH == 1)
                        LH = Lb[n % 3][:, 0:512]
                        LL = Lb[n % 3][:, 512:1024]
                        c_ = cur[0]
                        bH, bL = 2 * z, 2 * z + 1
                        P.mm(banks[bH][:, :], negU, LH, False, True, r=[cbB, LB[n % 3]], w=[bankB[bH]])
                        if not first:
                            P.mm(banks[bH][:, :], negones, Lacc[c_], False, True, r=[cbB, LaccB[c_]], w=[bankB[bH]])
                        P.mm(banks[bL][:, :], negU, LL, False, True, r=[cbB, LB[n % 3]], w=[bankB[bL]])
                        P.mm(banks[bL][:, :], negones, LH, False, True, r=[cbB, LB[n % 3]], w=[bankB[bL]])
                        if not first:
                            P.mm(banks[bL][:, :], negones, Lacc[c_], False, True, r=[cbB, LaccB[c_]], w=[bankB[bL]])
                        if not last:
                            if first:
                                P.v("dve", "tensor_tensor", Lacc[0], LH, LL, ALU.add, r=[LB[n % 3]], w=[LaccB[0]])
                                cur[0] = 0
                            else:
                                P.v("dve", "tensor_tensor", Lacc[1 - c_], Lacc[c_], LH, ALU.add,
                                    r=[LaccB[c_], LB[n % 3]], w=[LaccB[1 - c_]])
                                P.v("dve", "tensor_tensor", Lacc[1 - c_], Lacc[1 - c_], LL, ALU.add,
                                    r=[LaccB[1 - c_], LB[n % 3]], w=[LaccB[1 - c_]])
                                cur[0] = 1 - c_

                    def stB2(n):
                        qt, kbH = steps[n]
                        z = n % 2
                        P.act(Ab[n % 3], psum[:, 2 * z * 512:(2 * z + 2) * 512], AF.Exp,
                              r=[bankB[2 * z], bankB[2 * z + 1]], w=[AB[n % 3]])
                        for j_, kb in enumerate((kbH, kbH - 1)):
                            m = mask_of(kb, qt)
                            if m is not None:
                                hv = Ab[n % 3][:, j_ * 512:(j_ + 1) * 512]
                                P.v("dve", "tensor_tensor", hv, hv, m, ALU.mult, r=[AB[n % 3], cbB], w=[AB[n % 3]])

                    def stC(n):
                        qt, kbH = steps[n]
                        first = (kbH == 4 * qt + 3)
                        last = (kbH == 1)
                        ob = 4 + (qt % 2)
                        P.mm(banks[ob][:, :], vv[:, kbH, :], Ab[n % 3][:, 0:512], first, False,
                             r=[vB[kbH // 4], AB[n % 3]], w=[bankB[ob]])
                        P.mm(banks[ob][:, :], vv[:, kbH - 1, :], Ab[n % 3][:, 512:1024], False, last,
                             r=[vB[(kbH - 1) // 4], AB[n % 3]], w=[bankB[ob]])
                        if last:
                            y2 = qt % 2
                            P.v("dve", "tensor_tensor", yt[y2], banks[ob][:, :], szT[:, qt * 512:(qt + 1) * 512],
                                ALU.mult, r=[bankB[ob], szB[qt]], w=[ytB[y2]])
                            P.dma("sp", ys_d[b, h, :, qt * 512:(qt + 1) * 512], yt[y2], r=[ytB[y2]],
                                  w=[ysB[b][h][qt]])

                    if "noattn" in dbg:
                        NS = -3
                    for n in range(NS + 2):
                        if 0 <= n - 2 < NS:
                            stB2(n - 2)
                        if n < NS:
                            stA1_mm(n)
                        if 0 <= n - 2 < NS:
                            stC(n - 2)
                        if 0 <= n - 1 < NS:
                            stA2(n - 1)
                            stB(n - 1)
                        if n < NS:
                            stA1_act(n)
                        if nxt and n % 2 == 1:
                            nxt.pop(0)()
                    while nxt:
                        nxt.pop(0)()
            if "ys" in dbg:
                P.barrier()
                P.dma("sp", dbg_t["ys"], ys_d[b], r=[], w=[Buf()])

            if "noB" not in dbg:
                P.barrier()
                ar.reset()
                Wg8 = ar.alloc([8, 8], BF16)
                Wg8B = Buf()
                load_w("pool", Wg8, OFF["mi"], 8, [Wg8B])
                A1 = ar.alloc([S], F32, parts=4)
                A2 = ar.alloc([S], F32, parts=4)
                A3 = ar.alloc([S], F32, parts=4)
                A1B, A2B, A3B = Buf(), Buf(), Buf()
                Pt = ar.alloc([NCH + 1], F32, parts=4)
                PtB = Buf()
                cr = ar.alloc([NCH], F32, parts=4)
                crB = Buf()
                one4 = ar.alloc([2], F32, parts=4)
                one4B = Buf()
                P.v("dve", "memset", one4, 1.0, r=[], w=[one4B])
                for tt in range(NT):
                    sl = slice(tt * 512, (tt + 1) * 512)
                    for kc in range(8):
                        P.mm(banks[5][0:4, :], Wg8[:, kc, 0:4], hT[:, kc, sl], kc == 0, kc == 7,
                             r=[Wg8B, hTB[tt]], w=[bankB[5]])
                    P.act(A1[:, sl], banks[5][0:4, :], AF.Identity, bias=gb[:, 0:1], r=[bankB[5], gbB], w=[A1B])
                    for kc in range(8):
                        P.mm(banks[6][0:4, :], Wg8[:, kc, 4:8], hT[:, kc, sl], kc == 0, kc == 7,
                             r=[Wg8B, hTB[tt]], w=[bankB[6]])
                    P.act(A2[:, sl], banks[6][0:4, :], AF.Exp, scale=-1.0, bias=gb[:, 2:3],
                          r=[bankB[6], gbB], w=[A2B])
                P.act(A2, A2, AF.Ln, bias=one4[:, 0:1], r=[A2B, one4B], w=[A2B])
                P.v("dve", "tensor_tensor_scan", A3, one4[:, 0:1].to_broadcast([4, S]), A2, 0.0,
                    ALU.mult, ALU.subtract, r=[A2B, one4B], w=[A3B])
                P.v("dve", "tensor_tensor", A1, A1, A3, ALU.subtract, r=[A1B, A3B], w=[A1B])
                P.v("dve", "tensor_tensor_scan", A2, A1, A1, 0.0, ALU.max, ALU.max, r=[A1B], w=[A2B])
                P.v("dve", "memset", Pt[:, 0:1], 0.0, r=[], w=[PtB])
                P.v("dve", "tensor_copy", Pt[:, 1:NCH + 1],
                    A2.rearrange("p (c t) -> p c t", t=128)[:, :, 127], r=[A2B], w=[PtB])
                A1v = A1.rearrange("p (c t) -> p c t", t=128)
                A2v = A2.rearrange("p (c t) -> p c t", t=128)
                A3v = A3.rearrange("p (c t) -> p c t", t=128)
                Plo = Pt[:, 0:NCH].unsqueeze(2).to_broadcast([4, NCH, 128])
                Phi = Pt[:, 1:NCH + 1].unsqueeze(2).to_broadcast([4, NCH, 128])
                colps = banks[7]
                for slot in range(3):
                    if slot == 0:
                        P.v("dve", "tensor_tensor", A2v, A1v, Plo, ALU.subtract, r=[A1B, PtB], w=[A2B])
                        P.act(A2, A2, AF.Exp, bias=ml16[0:4, 0:1], r=[A2B, gbB], w=[A2B])
                    elif slot == 1:
                        P.v("dve", "tensor_tensor", A2v, A1v, Phi, ALU.subtract, r=[A1B, PtB], w=[A2B])
                        P.act(A2, A2, AF.Exp, bias=ml16[0:4, 0:1], r=[A2B, gbB], w=[A2B])
                    else:
                        P.v("dve", "tensor_tensor", A2v, A3v, Plo, ALU.add, r=[A3B, PtB], w=[A2B])
                        P.act(A2, A2, AF.Exp, scale=-1.0, r=[A2B], w=[A2B])
                    for c in range(NCH):
                        o0 = slot * NCH * 4 + c * 4
                        P.mm(colps[:, o0:o0 + 4], A2[:, c * 128:(c + 1) * 128], ident4, True, True,
                             r=[A2B, c4B], w=[bankB[7]])
                P.act(colv, colps[:, 0:3 * NCH * 4], AF.Copy, r=[bankB[7]], w=[colvB])
                P.v("dve", "tensor_tensor", cr, Pt[:, 0:NCH], Pt[:, 1:NCH + 1], ALU.subtract, r=[PtB], w=[crB])
                P.act(cr, cr, AF.Exp, r=[crB], w=[crB])
                for h in range(4):
                    P.mm(banks[6][:, h * NCH:(h + 1) * NCH], c4[:, 8 + h * 128:8 + (h + 1) * 128], cr, True, True,
                         r=[c4B, crB], w=[bankB[6]])
                P.act(carry_bc, banks[6][:, 0:4 * NCH], AF.Copy, r=[bankB[6]], w=[carryB])
                if "gates" in dbg:
                    P.dma("sp", dbg_t["colv"], colv, r=[colvB], w=[Buf()])
                    P.dma("sp", dbg_t["carry"], carry_bc, r=[carryB], w=[Buf()])

                P.barrier()
                ar.reset()
                Wm = [[ar.alloc([8, 256], BF16) for _ in range(5)] for _ in range(2)]
                WmB = [[Buf() for _ in range(5)] for _ in range(2)]
                raw = [ar.alloc([516], BF16) for _ in range(4)]
                rawB = [Buf() for _ in range(4)]
                dwc = ar.alloc([16, 128], BF16)
                dwcB = Buf()
                acc = [ar.alloc([512], F32) for _ in range(2)]
                accB = [Buf() for _ in range(2)]
                sgm = [ar.alloc([512], F32) for _ in range(2)]
                sgmB = [Buf() for _ in range(2)]
                qTt = [ar.alloc([2, 512], BF16) for _ in range(2)]
                kTt = [ar.alloc([2, 512], BF16) for _ in range(2)]
                qTtB = [Buf() for _ in range(2)]
                kTtB = [Buf() for _ in range(2)]
                vaug = [ar.alloc([4, 264], BF16) for _ in range(2)]
                vaugB = [Buf() for _ in range(2)]
                ogt = [ar.alloc([4, 256], BF16) for _ in range(2)]
                ogtB = [Buf() for _ in range(2)]
                sgo = [ar.alloc([512], F32) for _ in range(2)]
                sgoB = [Buf() for _ in range(2)]
                t1 = [ar.alloc([256], F32) for _ in range(2)]
                t1B = [Buf() for _ in range(2)]
                Cf = ar.alloc([2, 257], F32)
                CfB = Buf()
                Cbf = ar.alloc([2, 258], BF16)
                CbfB = Buf()
                scT = [ar.alloc([128], BF16) for _ in range(2)]
                scTB = [Buf() for _ in range(2)]
                kwt = [ar.alloc([256], BF16) for _ in range(2)]
                kwtB = [Buf() for _ in range(2)]
                ymt = [ar.alloc([256], BF16) for _ in range(2)]
                ymtB = [Buf() for _ in range(2)]
                ymT = [ar.alloc([2, 512], BF16) for _ in range(2)]
                ymTB = [Buf() for _ in range(2)]
                sc = [ar.alloc([8], F32) for _ in range(2)]
                scB = [Buf() for _ in range(2)]
                junk2 = ar.alloc([256], BF16)
                junk2B = Buf()
                for k_ in range(2):
                    P.v("dve", "memset", vaug[k_][:, :, 256:257], 1.0, r=[], w=[vaugB[k_]])
                MSEG = ["mq", "mk", "mv", "mo", "mz"]
                brow = ar.alloc([3072], BF16, parts=1)
                browB = Buf("brow")
                for i_, key in enumerate(["mv", "mo", "mz"]):
                    P.dma("pool", brow[:, i_ * 1024:(i_ + 1) * 1024],
                          b_in[OFF[key]:OFF[key] + 1024].unsqueeze(0), r=[], w=[browB])

                def loadB(h):
                    for i_, key in enumerate(MSEG):
                        load_w("pool", Wm[h % 2][i_], OFF[key] + h * 256, 256, [WmB[h % 2][i_]])

                loadB(0)
                pcnt = [0]
                NHB = 1 if "h1" in dbg else 4
                tpm = banks[3][:, :].bitcast(BF16).rearrange("p (a b) -> p a b", b=128)
                tpk = banks[0][:, 256:512].bitcast(BF16).rearrange("p (a b) -> p a b", b=128)
                for h in range(NHB):
                    if h + 1 < NHB:
                        loadB(h + 1)
                    Wq_, Wk_, Wv_, Wo_, Wz_ = Wm[h % 2]
                    WqB_, WkB_, WvB_, WoB_, WzB_ = WmB[h % 2]
                    P.v("dve", "memset", Cf, 0.0, r=[], w=[CfB])
                    for i_ in range(4):
                        P.v("dve", "memset", raw[i_][:, 0:3], 0.0, r=[], w=[rawB[i_]])
                    for qk_ in range(2):
                        for dc_ in range(2):
                            ci_ = qk_ * 8 + h * 2 + dc_
                            for j_ in range(4):
                                P.v("dve", "tensor_scalar", dwc[:, (qk_ * 2 + dc_) * 4 + j_, :], ident,
                                    mcols[:, j_ * 16 + ci_:j_ * 16 + ci_ + 1], None, ALU.mult,
                                    r=[cbB, mcolsB], w=[dwcB])

                    def proj_pieces(tt, h=h, Wq_=Wq_, Wk_=Wk_, Wv_=Wv_, Wo_=Wo_, Wz_=Wz_,
                                    WqB_=WqB_, WkB_=WkB_, WvB_=WvB_, WoB_=WoB_, WzB_=WzB_):
                        k2 = tt % 2
                        sl = slice(tt * 512, (tt + 1) * 512)
                        pieces = []

                        def qk_piece(qk, dc):
                            ci = qk * 8 + h * 2 + dc
                            ri = qk * 2 + dc
                            Wx, WxB = (Wq_, WqB_) if qk == 0 else (Wk_, WkB_)
                            a2 = pcnt[0] % 2
                            pcnt[0] += 2
                            for kc in range(8):
                                P.mm(banks[6][:, :], Wx[:, kc, dc * 128:(dc + 1) * 128], hT[:, kc, sl],
                                     kc == 0, kc == 7, r=[WxB, hTB[tt]], w=[bankB[6]])
                            if tt > 0:
                                P.v("dve", "tensor_copy", raw[ri][:, 0:3], raw[ri][:, 512:515],
                                    r=[rawB[ri]], w=[rawB[ri]])
                            P.act(raw[ri][:, 3:515], banks[6][:, :], AF.Identity, bias=mcols[:, 80 + ci:81 + ci],
                                  r=[bankB[6], mcolsB], w=[rawB[ri]])
                            for j in range(4):
                                P.mm(banks[7][:, :], dwc[:, ri * 4 + j, :], raw[ri][:, j:j + 512], j == 0, j == 3,
                                     r=[dwcB, rawB[ri]], w=[bankB[7]])
                            P.act(sgm[a2], banks[7][:, :], AF.Sigmoid, bias=mcols[:, 64 + ci:65 + ci],
                                  r=[bankB[7], mcolsB], w=[sgmB[a2]])
                            dst, dstB = (qTt, qTtB) if qk == 0 else (kTt, kTtB)
                            P.v("dve", "scalar_tensor_tensor", dst[k2][:, dc, :], banks[7][:, :],
                                mcols[:, 64 + ci:65 + ci], sgm[a2], ALU.add, ALU.mult,
                                r=[bankB[7], sgmB[a2], mcolsB], w=[dstB[k2]])

                        def v_piece(g2):
                            pb = 6 + (pcnt[0] % 2)
                            pcnt[0] += 1
                            for c2 in range(2):
                                c4_ = g2 * 2 + c2
                                j = tt * 4 + c4_
                                o_ = banks[pb][:, c2 * 256:(c2 + 1) * 256]
                                for kc in range(8):
                                    P.mm(o_, hT[:, kc, j * 128:(j + 1) * 128], Wv_[:, kc, :], kc == 0, False,
                                         r=[WvB_, hTB[tt]], w=[bankB[pb]])
                                P.mm(o_, ones_b[0:1, 0:128], brow[0:1, h * 256:(h + 1) * 256], False, True,
                                     r=[cbB, browB], w=[bankB[pb]])
                            P.act(vaug[k2][:, g2 * 2:g2 * 2 + 2, 0:256],
                                  banks[pb][:, :].rearrange("p (a b) -> p a b", b=256), AF.Copy,
                                  r=[bankB[pb]], w=[vaugB[k2]])

                        def og_piece(c4_):
                            j = tt * 4 + c4_
                            pb = 6 + (pcnt[0] % 2)
                            a2 = pcnt[0] % 2
                            pcnt[0] += 1
                            for half, (Wx, WxB, boff) in enumerate(((Wo_, WoB_, 1024), (Wz_, WzB_, 2048))):
                                o_ = banks[pb][:, half * 256:(half + 1) * 256]
                                for kc in range(8):
                                    P.mm(o_, hT[:, kc, j * 128:(j + 1) * 128], Wx[:, kc, :], kc == 0, False,
                                         r=[WxB, hTB[tt]], w=[bankB[pb]])
                                P.mm(o_, ones_b[0:1, 0:128], brow[0:1, boff + h * 256:boff + (h + 1) * 256],
                                     False, True, r=[cbB, browB], w=[bankB[pb]])
                            P.act(sgo[a2], banks[pb][:, :], AF.Sigmoid, r=[bankB[pb]], w=[sgoB[a2]])
                            P.v("dve", "tensor_tensor", t1[a2], sgo[a2][:, 0:256], sgo[a2][:, 256:512], ALU.mult,
                                r=[sgoB[a2]], w=[t1B[a2]])
                            P.v("dve", "tensor_tensor", ogt[k2][:, c4_, :], t1[a2], banks[pb][:, 256:512], ALU.mult,
                                r=[t1B[a2], bankB[pb]], w=[ogtB[k2]])

                        for qk in range(2):
                            for dc in range(2):
                                pieces.append(lambda qk=qk, dc=dc: qk_piece(qk, dc))
                        for g2 in range(2):
                            pieces.append(lambda g2=g2: v_piece(g2))
                        for c4_ in range(4):
                            pieces.append(lambda c4_=c4_: og_piece(c4_))
                        return pieces

                    def chunk_parts(g, h=h):
                        tt, c4_ = divmod(g, 4)
                        c = g
                        k2 = tt % 2
                        bl = slice(c4_ * 128, (c4_ + 1) * 128)
                        s2 = g % 2
                        hb = 1 + (g % 2)
                        es = colv[:, c * 4 + h:c * 4 + h + 1]
                        es2 = colv[:, NCH * 4 + c * 4 + h:NCH * 4 + c * 4 + h + 1]
                        dnm = colv[:, 2 * NCH * 4 + c * 4 + h:2 * NCH * 4 + c * 4 + h + 1]
                        car = carry_bc[:, h * NCH + c:h * NCH + c + 1]
                        upd = c < NCH - 1

                        def early():
                            for dc in range(2):
                                P.mm(banks[0][:, 0:128], kTt[k2][:, dc, bl], qTt[k2][:, dc, bl], dc == 0, dc == 1,
                                     r=[kTtB[k2], qTtB[k2]], w=[bankB[0]])
                            P.v("dve", "scalar_tensor_tensor", scT[s2], banks[0][:, 0:128], es, trimask,
                                ALU.mult, ALU.mult, r=[bankB[0], colvB, cbB], w=[scTB[s2]])
                            if upd:
                                for dc in range(2):
                                    P.tr(tpk[:, dc, :], kTt[k2][:, dc, bl], ident, r=[kTtB[k2], cbB], w=[bankB[0]])
                                P.act(kwt[s2].rearrange("p (a b) -> p a b", b=128), tpk[:, 0:2, :], AF.Copy,
                                      scale=es2, r=[bankB[0], colvB], w=[kwtB[s2]])
                            P.mm(banks[hb][:, 0:257], scT[s2], vaug[k2][:, c4_, 0:257], True, c == 0,
                                 r=[scTB[s2], vaugB[k2]], w=[bankB[hb]])
                            if c > 0:
                                for dc in range(2):
                                    P.mm(banks[hb][:, 0:257], qTt[k2][:, dc, bl], Cbf[:, dc, 0:257], False, dc == 1,
                                         r=[qTtB[k2], CbfB], w=[bankB[hb]])
                            if upd:
                                for dc in range(2):
                                    P.mm(banks[4 + dc][:, 0:257], kwt[s2][:, dc * 128:(dc + 1) * 128],
                                         vaug[k2][:, c4_, 0:257], True, True, r=[kwtB[s2], vaugB[k2]],
                                         w=[bankB[4 + dc]])
                                    P.v("dve", "scalar_tensor_tensor", Cf[:, dc, :], Cf[:, dc, :], car,
                                        banks[4 + dc][:, 0:257], ALU.mult, ALU.add,
                                        r=[CfB, carryB, bankB[4 + dc]], w=[CfB])
                                P.act(Cbf[:, :, 0:257], Cf, AF.Copy, r=[CfB], w=[CbfB])

                        def epi1():
                            P.act(junk2, banks[hb][:, 0:256], AF.Square, r=[bankB[hb]], w=[junk2B, scB[s2]],
                                  accum_out=sc[s2][:, 0:1])
                            P.act(sc[s2][:, 6:7], banks[hb][:, 256:257], AF.Abs, r=[bankB[hb]], w=[scB[s2]])
                            P.v("dve", "tensor_scalar", sc[s2][:, 1:2], sc[s2][:, 6:7], dnm, None,
                                ALU.max, r=[scB[s2], colvB], w=[scB[s2]])
                            P.v("dve", "reciprocal", sc[s2][:, 2:3], sc[s2][:, 1:2], r=[scB[s2]], w=[scB[s2]])
                            P.v("dve", "scalar_tensor_tensor", sc[s2][:, 3:4], sc[s2][:, 0:1], sc[s2][:, 2:3],
                                sc[s2][:, 2:3], ALU.mult, ALU.mult, r=[scB[s2]], w=[scB[s2]])
                            P.v("dve", "tensor_scalar", sc[s2][:, 3:4], sc[s2][:, 3:4], 1.0 / 256.0, EPS,
                                ALU.mult, ALU.add, r=[scB[s2]], w=[scB[s2]])
                            P.v("pool", "tensor_tensor", sc[s2][:, 4:5], sc[s2][:, 3:4], mhalf[:, 0:1], ALU.pow,
                                r=[scB[s2], mhB], w=[scB[s2]])

                        def epi2():
                            P.v("dve", "tensor_tensor", sc[s2][:, 5:6], sc[s2][:, 4:5], sc[s2][:, 2:3], ALU.mult,
                                r=[scB[s2]], w=[scB[s2]])
                            P.v("dve", "scalar_tensor_tensor", ymt[s2], banks[hb][:, 0:256], sc[s2][:, 5:6],
                                ogt[k2][:, c4_, :], ALU.mult, ALU.mult, r=[bankB[hb], scB[s2], ogtB[k2]],
                                w=[ymtB[s2]])

                        def fin():
                            for ec in range(2):
                                P.tr(tpm[:, ec, :], ymt[s2][:, ec * 128:(ec + 1) * 128], ident,
                                     r=[ymtB[s2], cbB], w=[bankB[3]])
                            P.act(ymT[k2][:, :, bl], tpm[:, 0:2, :], AF.Copy, r=[bankB[3]], w=[ymTB[k2]])
                            if c4_ == 3:
                                sl = slice(tt * 512, (tt + 1) * 512)
                                P.dma("sp", ym_d[b, 2 * h:2 * h + 2, :, sl].rearrange("c p t -> p c t"), ymT[k2],
                                      r=[ymTB[k2]], w=[ymB[b][2 * h][tt], ymB[b][2 * h + 1][tt]])

                        return early, epi1, epi2, fin

                    for pc in proj_pieces(0):
                        pc()
                    parts = {}
                    nxt = []
                    for g in range(NCH + 2):
                        if g < NCH:
                            tt, c4_ = divmod(g, 4)
                            if c4_ == 0:
                                nxt = proj_pieces(tt + 1) if tt + 1 < NT else []
                            lo = (len(nxt) * c4_) // 4
                            hi = (len(nxt) * (c4_ + 1)) // 4
                            for pc in nxt[lo:hi]:
                                pc()
                            parts[g] = chunk_parts(g)
                            parts[g][0]()
                        if 0 <= g - 1 < NCH:
                            parts[g - 1][2]()
                        if g < NCH:
                            parts[g][1]()
                        if 0 <= g - 2 < NCH:
                            parts[g - 2][3]()
            if "ym" in dbg:
                P.barrier()
                P.dma("sp", dbg_t["ym"], ym_d[b], r=[], w=[Buf()])

            if "noC" not in dbg:
                P.barrier()
                ar.reset()
                TC = 256
                NTC = S // TC
                Wpm_t = ar.alloc([8, 1024], BF16)
                Wps_t = ar.alloc([8, 1024], BF16)
                Wo_t = ar.alloc([8, 1024], BF16)
                Wg_t = ar.alloc([8, 2048], BF16)
                WpmB, WpsB, WoB, WgB = Buf(), Buf(), Buf(), Buf()
                mnwc = ar.alloc([8], F32)
                bgc = ar.alloc([16], F32)
                smB = Buf()
                ymt_c = [ar.alloc([8, TC], BF16) for _ in range(2)]
                yst_c = [ar.alloc([8, TC], BF16) for _ in range(2)]
                ymcB = [Buf() for _ in range(2)]
                yscB = [Buf() for _ in range(2)]
                sga = [ar.alloc([TC], F32) for _ in range(2)]
                sgb = [ar.alloc([TC], F32) for _ in range(2)]
                m1 = [ar.alloc([TC], F32) for _ in range(2)]
                m2 = [ar.alloc([TC], F32) for _ in range(2)]
                sgaB = [Buf() for _ in range(2)]
                sgbB = [Buf() for _ in range(2)]
                m1B = [Buf() for _ in range(2)]
                m2B = [Buf() for _ in range(2)]
                mrg = [ar.alloc([8, TC], BF16) for _ in range(2)]
                mrgB = [Buf() for _ in range(2)]
                xt = [ar.alloc([1024], F32) for _ in range(2)]
                xtB = [Buf() for _ in range(2)]
                for wt_, wb_, src in ((Wpm_t, WpmB, w_pm), (Wps_t, WpsB, w_ps), (Wo_t, WoB, w_out)):
                    s3 = src.rearrange("(kc p) n -> p kc n", p=128)
                    for hf in range(2):
                        P.dma("pool", wt_[:, :, hf * 512:(hf + 1) * 512], s3[:, :, hf * 512:(hf + 1) * 512],
                              r=[], w=[wb_])
                for hf in range(4):
                    P.dma("pool", Wg_t[:, :, hf * 512:(hf + 1) * 512],
                          w3[:, :, OFF["g"] + hf * 512:OFF["g"] + (hf + 1) * 512], r=[], w=[WgB])
                P.dma("sp", mnwc, mnw.rearrange("(c p) -> p c", p=128), r=[], w=[smB], slow=True)
                P.dma("sp", bgc, b_in[OFF["g"]:OFF["g"] + 2048].rearrange("(c p) -> p c", p=128), r=[], w=[smB],
                      slow=True)
                for kc in range(8):
                    P.v("dve", "tensor_scalar", Wpm_t[:, kc, :], Wpm_t[:, kc, :], mnwc[:, kc:kc + 1], None, ALU.mult,
                        r=[WpmB, smB], w=[WpmB])
                ymv = ym_d[b].rearrange("c p t -> p c t")
                ysv = ys_d[b].rearrange("c p t -> p c t")
                gcnt = 0
                for tc in range(NTC):
                    k2 = tc % 2
                    sl = slice(tc * TC, (tc + 1) * TC)
                    t512 = (tc * TC) // 512
                    P.dma("sp", ymt_c[k2], ymv[:, :, sl], r=[ymB[b][f_][t512] for f_ in range(8)], w=[ymcB[k2]])
                    P.dma("sp", yst_c[k2], ysv[:, :, sl], r=[ysB[b][f_][t512] for f_ in range(8)], w=[yscB[k2]])
                    for cc in range(8):
                        g2 = gcnt % 2
                        gcnt += 1
                        cs = slice(cc * 128, (cc + 1) * 128)
                        cs2 = slice(1024 + cc * 128, 1024 + (cc + 1) * 128)
                        bPm, bPs, bgm, bgs = banks[0 + g2], banks[2 + g2], banks[4 + g2], banks[6 + g2]
                        BPm, BPs, Bgm, Bgs = bankB[0 + g2], bankB[2 + g2], bankB[4 + g2], bankB[6 + g2]
                        for kc in range(8):
                            P.mm(bgm[:, 0:TC], Wg_t[:, kc, cs], hT[:, kc, sl], kc == 0, kc == 7,
                                 r=[WgB, hTB[t512]], w=[Bgm])
                        for kc in range(8):
                            P.mm(bgs[:, 0:TC], Wg_t[:, kc, cs2], hT[:, kc, sl], kc == 0, kc == 7,
                                 r=[WgB, hTB[t512]], w=[Bgs])
                        for kc in range(8):
                            P.mm(bPm[:, 0:TC], Wpm_t[:, kc, cs], ymt_c[k2][:, kc, :], kc == 0, kc == 7,
                                 r=[WpmB, ymcB[k2]], w=[BPm])
                        for kc in range(8):
                            P.mm(bPs[:, 0:TC], Wps_t[:, kc, cs], yst_c[k2][:, kc, :], kc == 0, kc == 7,
                                 r=[WpsB, yscB[k2]], w=[BPs])
                        P.act(sga[g2], bgm[:, 0:TC], AF.Sigmoid, bias=bgc[:, cc:cc + 1], r=[Bgm, smB], w=[sgaB[g2]])
                        P.act(sgb[g2], bgs[:, 0:TC], AF.Sigmoid, bias=bgc[:, 8 + cc:9 + cc], r=[Bgs, smB],
                              w=[sgbB[g2]])
                        P.v("dve", "tensor_tensor", m1[g2], bPm[:, 0:TC], sga[g2], ALU.mult, r=[BPm, sgaB[g2]],
                            w=[m1B[g2]])
                        P.v("dve", "tensor_tensor", m2[g2], bPs[:, 0:TC], sgb[g2], ALU.mult, r=[BPs, sgbB[g2]],
                            w=[m2B[g2]])
                        P.v("pool", "tensor_tensor", mrg[k2][:, cc, :], m1[g2], m2[g2], ALU.add,
                            r=[m1B[g2], m2B[g2]], w=[mrgB[k2]])
                    for sub in range(TC // 128):
                        row0 = tc * TC + sub * 128
                        x2 = (tc * (TC // 128) + sub) % 2
                        P.dma("sp", xt[x2], x[b, row0:row0 + 128, :], r=[], w=[xtB[x2]])
                        for eh in range(2):
                            g2 = gcnt % 2
                            gcnt += 1
                            bo, BO = banks[0 + g2], bankB[0 + g2]
                            for cc in range(8):
                                P.mm(bo[:, :], mrg[k2][:, cc, sub * 128:(sub + 1) * 128],
                                     Wo_t[:, cc, eh * 512:(eh + 1) * 512], cc == 0, cc == 7,
                                     r=[mrgB[k2], WoB], w=[BO])
                            P.v("dve", "tensor_tensor", xt[x2][:, eh * 512:(eh + 1) * 512], bo[:, :],
                                xt[x2][:, eh * 512:(eh + 1) * 512], ALU.add, r=[BO, xtB[x2]], w=[xtB[x2]])
                        P.dma("sp", out[b, row0:row0 + 128, :], xt[x2], r=[xtB[x2]], w=[Buf()])

        P.emit()
    return nc


_NC_CACHE = {}


def kernel(x, norm_w, w_in, b_in, conv_w, conv_b, mlstm_norm_w, sb_q_norm_w, sb_k_norm_w,
           w_proj_m, w_proj_s, w_out):
    n = 8
    B, S, _ = x.shape
    nseq = B // n
    nc = build(S, nseq)
    c, c4 = _consts()
    f = lambda a: np.ascontiguousarray(np.asarray(a, dtype=np.float32))
    shared = dict(norm_w=f(norm_w), w_in=f(w_in), b_in=f(b_in), conv_w=f(conv_w), conv_b=f(conv_b),
                  mlstm_norm_w=f(mlstm_norm_w), sb_q_norm_w=f(sb_q_norm_w), sb_k_norm_w=f(sb_k_norm_w),
                  w_proj_m=f(w_proj_m), w_proj_s=f(w_proj_s), w_out=f(w_out), cst=c, cst4=c4)
    xs = f(x)
    in_maps = [dict(shared, x=xs[i * nseq:(i + 1) * nseq]) for i in range(n)]
    res = run_bass_kernel_spmd(nc, in_maps, core_ids=list(range(n)))
    return np.concatenate([r["out"] for r in res.results], axis=0)
```

```python
import math
import numpy as np
import concourse.bass as bass
import concourse.mybir as mybir
from concourse.bass_utils import run_bass_kernel_spmd

F32 = mybir.dt.float32
BF16 = mybir.dt.bfloat16
AF = mybir.ActivationFunctionType
ALU = mybir.AluOpType
AX = mybir.AxisListType

D = 1024
IN_COLS = 11272
EPS = 1e-6
OFF = dict(mq=0, mk=1024, mv=2048, mi=3072, mf=3076, mo=3080, mz=4104,
           sq=5128, sk=6152, sv=7176, sz=8200, g=9224)
LN16 = math.log(16.0)


class Buf:
    __slots__ = ("name", "w", "r", "excl")

    def __init__(self, name="", excl=False):
        self.name = name
        self.w = None
        self.r = {}
        self.excl = excl


class Op:
    __slots__ = ("eng", "tl", "idx", "fn", "waits", "signal", "clock", "dma", "semval")


ENGS = ["pe", "act", "dve", "pool", "sp"]


class Prog:
    def __init__(self, nc, ndma=8):
        self.nc = nc
        self.ops = {e: [] for e in ENGS}
        self.tl_ops = {}
        self.known = {e: {} for e in ENGS}
        self.ndma = ndma
        self.dma_count = {e: 0 for e in ENGS}
        self.bar_ops = []
        self.bar_gen = 0
        self.eng_gen = {e: 0 for e in ENGS}

    def barrier(self):
        self.bar_ops = [lst[-1] for lst in self.tl_ops.values() if lst]
        self.bar_gen += 1

    def add(self, eng, fn, r=(), w=(), dma=False):
        op = Op()
        op.eng = eng
        op.fn = fn
        op.dma = dma
        op.signal = dma
        deps = []
        for b in r:
            if b.w is not None:
                deps.append((b.w, True))
            if b.excl:
                for o in b.r.values():
                    deps.append((o, False))
        for b in w:
            if b.w is not None:
                deps.append((b.w, False))
            for o in b.r.values():
                deps.append((o, False))
        if self.eng_gen[eng] < self.bar_gen:
            self.eng_gen[eng] = self.bar_gen
            for o in self.bar_ops:
                deps.append((o, True))
        if dma:
            j = self.dma_count[eng]
            self.dma_count[eng] += 1
            tl = (eng, j % self.ndma)
            prev = self.tl_ops.get(tl)
            if prev:
                deps.append((prev[-1], True))
        else:
            tl = eng
        lst = self.tl_ops.setdefault(tl, [])
        op.tl = tl
        op.idx = len(lst)
        kn = self.known[eng]
        best = {}
        for d, raw in deps:
            if d.tl == eng:
                if eng == "pe" or not raw:
                    continue
            if kn.get(d.tl, -1) >= d.idx:
                continue
            if d.tl not in best or best[d.tl].idx < d.idx:
                best[d.tl] = d
        waits = []
        for d in best.values():
            if kn.get(d.tl, -1) >= d.idx:
                continue
            waits.append(d)
            d.signal = True
            for t, i in d.clock.items():
                if kn.get(t, -1) < i:
                    kn[t] = i
            if kn.get(d.tl, -1) < d.idx:
                kn[d.tl] = d.idx
        op.waits = waits
        op.clock = dict(kn)
        lst.append(op)
        self.ops[eng].append(op)
        for b in r:
            b.r[tl] = op
        for b in w:
            b.w = op
            b.r = {}
        return op

    def mm(self, out, lhsT, rhs, start, stop, r, w):
        return self.add("pe", lambda e: e.matmul(out, lhsT, rhs, start=start, stop=stop,
                                                 skip_group_check=True), r, w)

    def tr(self, out, in_, ident, r, w):
        return self.add("pe", lambda e: e.transpose(out, in_, ident), r, w)

    def act(self, out, in_, func, r, w, bias=None, scale=None, accum_out=None):
        kw = {}
        if bias is not None:
            kw["bias"] = bias
        if scale is not None:
            kw["scale"] = scale
        if accum_out is not None:
            kw["accum_out"] = accum_out
        return self.add("act", lambda e: e.activation(out, in_, func, **kw), r, w)

    def v(self, eng, name, *args, r=(), w=(), **kw):
        return self.add(eng, lambda e: getattr(e, name)(*args, **kw), r, w)

    def dma(self, eng, out, in_, r, w, slow=False):
        if slow:
            return self.add(eng, lambda e: e.dma_start(out=out, in_=in_, allow_slow_non_contiguous=True),
                            r, w, dma=True)
        return self.add(eng, lambda e: e.dma_start(out=out, in_=in_), r, w, dma=True)

    def emit(self):
        nc = self.nc
        for tl, lst in self.tl_ops.items():
            if isinstance(tl, tuple):
                for o in lst:
                    o.semval = 16 * (o.idx + 1)
            else:
                c = 0
                for o in lst:
                    if o.signal:
                        c += 1
                    o.semval = c
        sems = {}
        import contextlib
        with contextlib.ExitStack() as st:
            for tl in self.tl_ops:
                nm = tl if isinstance(tl, str) else "%s_d%d" % tl
                sems[tl] = st.enter_context(nc.semaphore("s_" + nm))
            block = st.enter_context(nc.Block())
            handles = {"pe": block.tensor, "act": block.scalar, "dve": block.vector,
                       "pool": block.gpsimd, "sp": block.sync}

            def make(engname):
                def body(e):
                    for o in self.ops[engname]:
                        for d in o.waits:
                            e.wait_ge(sems[d.tl], d.semval)
                        ins = o.fn(e)
                        if o.signal:
                            ins.then_inc(sems[o.tl], 16 if o.dma else 1)
                    for tl, lst in self.tl_ops.items():
                        if isinstance(tl, tuple) and tl[0] == engname and lst:
                            e.wait_ge(sems[tl], lst[-1].semval)
                return body

            for en in ENGS:
                if self.ops[en]:
                    handles[en](make(en))


def _consts():
    c = np.zeros((128, 1536), np.float32)
    j = np.arange(128)[:, None]
    s = np.arange(128)[None, :]
    c[:, 0:128] = (j == s)
    c[:, 128:256] = -(j >= s).astype(np.float32)
    c[:, 256:384] = -1.0
    c[:, 384:512] = 1.0
    c[:, 512:640] = (j <= s)
    cc = np.arange(896)[None, :]
    c[:, 640:1536] = ((cc - 384) > j)
    c4 = np.zeros((4, 520), np.float32)
    c4[:, 0:4] = np.eye(4)
    for h in range(4):
        c4[h, 8 + h * 128: 8 + (h + 1) * 128] = 1.0
    return c, c4


def build(S, NSEQ, dbg=None):
    dbg = dbg or set()
    nc = bass.Bass("TRN2", target_bir_lowering=False)
    NT = S // 512
    NCH = S // 128
    dt = nc.dram_tensor
    x = dt("x", [NSEQ, S, D], F32, kind="ExternalInput").ap()
    norm_w = dt("norm_w", [D], F32, kind="ExternalInput").ap()
    w_in = dt("w_in", [D, IN_COLS], F32, kind="ExternalInput").ap()
    b_in = dt("b_in", [IN_COLS], F32, kind="ExternalInput").ap()
    conv_w = dt("conv_w", [4, 2048], F32, kind="ExternalInput").ap()
    conv_b = dt("conv_b", [2048], F32, kind="ExternalInput").ap()
    mnw = dt("mlstm_norm_w", [1024], F32, kind="ExternalInput").ap()
    sqw = dt("sb_q_norm_w", [128], F32, kind="ExternalInput").ap()
    skw = dt("sb_k_norm_w", [128], F32, kind="ExternalInput").ap()
    w_pm = dt("w_proj_m", [D, D], F32, kind="ExternalInput").ap()
    w_ps = dt("w_proj_s", [D, D], F32, kind="ExternalInput").ap()
    w_out = dt("w_out", [D, D], F32, kind="ExternalInput").ap()
    cst = dt("cst", [128, 1536], F32, kind="ExternalInput").ap()
    cst4 = dt("cst4", [4, 520], F32, kind="ExternalInput").ap()
    out = dt("out", [NSEQ, S, D], F32, kind="ExternalOutput").ap()
    ys_d = dt("ys_scr", [NSEQ, 8, 128, S], BF16, kind="Internal").ap()
    ym_d = dt("ym_scr", [NSEQ, 8, 128, S], BF16, kind="Internal").ap()
    dbg_t = {}
    if "hT" in dbg:
        dbg_t["hT"] = dt("dbg_hT", [128, 8, S], BF16, kind="ExternalOutput").ap()
    if "ys" in dbg:
        dbg_t["ys"] = dt("dbg_ys", [8, 128, S], BF16, kind="ExternalOutput").ap()
    if "ym" in dbg:
        dbg_t["ym"] = dt("dbg_ym", [8, 128, S], BF16, kind="ExternalOutput").ap()
    if "gates" in dbg:
        dbg_t["colv"] = dt("dbg_colv", [128, 3 * NCH * 4], F32, kind="ExternalOutput").ap()
        dbg_t["carry"] = dt("dbg_carry", [128, 4 * NCH], F32, kind="ExternalOutput").ap()

    w3 = w_in.rearrange("(kc p) n -> p kc n", p=128)

    import contextlib
    with contextlib.ExitStack() as st:
        TOTAL = 104000
        big = st.enter_context(nc.sbuf_tensor("big", [128, TOTAL], BF16))
        psum = st.enter_context(nc.psum_tensor("psum_all", [128, 4096], F32))
        banks = [psum[:, i * 512:(i + 1) * 512] for i in range(8)]
        bankB = [Buf("bank%d" % i, excl=True) for i in range(8)]

        class Arena:
            def __init__(self, lo, hi):
                self.lo, self.hi, self.p = lo, hi, lo

            def reset(self):
                self.p = self.lo

            def alloc(self, shape, dtype, parts=128):
                n = 1
                for d_ in shape:
                    n *= d_
                nb = n * (2 if dtype == BF16 else 4)
                nb = (nb + 63) // 64 * 64
                ne = nb // 2
                assert self.p + ne <= self.hi, ("arena overflow", self.p, ne, self.hi)
                v = big[0:parts, self.p:self.p + ne]
                self.p += ne
                if dtype != BF16:
                    v = v.bitcast(dtype)
                v = v[:, 0:n]
                if len(shape) == 2:
                    v = v.rearrange("p (a b) -> p a b", b=shape[1])
                elif len(shape) == 3:
                    v = v.rearrange("p (a b c) -> p a b c", b=shape[1], c=shape[2])
                return v

        pers = Arena(0, 40000)
        ar = Arena(40000, TOTAL)

        P = Prog(nc)

        hT = pers.alloc([8, S], BF16)
        hTB = [Buf("hT%d" % i) for i in range(NT)]
        cb = pers.alloc([1536], BF16)
        cbB = Buf("cb")
        c4 = pers.alloc([520], F32, parts=4)
        c4B = Buf("c4")
        normw_bc = pers.alloc([1024], F32)
        nwB = Buf("normw")
        cols = pers.alloc([64], F32)
        colsB = Buf("cols")
        colv = pers.alloc([3 * NCH * 4], F32)
        colvB = Buf("colv")
        carry_bc = pers.alloc([4 * NCH], F32)
        carryB = Buf("carry")
        mcols = pers.alloc([16 * 6 + 8], F32)
        mcolsB = Buf("mcols")
        gb = pers.alloc([4], F32, parts=4)
        gbB = Buf("gb")

        mhalf = pers.alloc([2], F32)
        mhB = Buf("mhalf")
        P.v("pool", "memset", mhalf, -0.5, r=[], w=[mhB])
        epsq = pers.alloc([2], F32)
        P.v("pool", "memset", epsq, 128.0 * EPS, r=[], w=[mhB])
        onec = pers.alloc([2], F32)
        P.v("pool", "memset", onec, 1.0, r=[], w=[mhB])
        ml16 = pers.alloc([2], F32)
        P.v("pool", "memset", ml16, -LN16, r=[], w=[gbB])
        ysB = [[[Buf() for _ in range(NT)] for _ in range(8)] for _ in range(NSEQ)]
        ymB = [[[Buf() for _ in range(NT)] for _ in range(8)] for _ in range(NSEQ)]
        ident = cb[:, 0:128]
        negU = cb[:, 128:256]
        negones = cb[:, 256:384]
        ones_b = cb[:, 384:512]
        trimask = cb[:, 512:640]
        sbmask = cb[:, 640:1536]
        ident4 = c4[:, 0:4]

        P.dma("pool", cb, cst, r=[], w=[cbB])
        P.dma("sp", c4, cst4, r=[], w=[c4B])
        P.dma("sp", normw_bc, norm_w.partition_broadcast(128), r=[], w=[nwB])
        with nc.allow_non_contiguous_dma(reason="tiny one-time bias/gain column loads"):
            for i, key in enumerate(["sq", "sk", "sz"]):
                P.dma("sp", cols[:, 8 * i:8 * i + 8],
                      b_in[OFF[key]:OFF[key] + 1024].rearrange("(h p) -> p h", p=128), r=[], w=[colsB], slow=True)
            P.dma("sp", cols[:, 24:25], sqw.unsqueeze(1), r=[], w=[colsB], slow=True)
            P.dma("sp", cols[:, 25:26], skw.unsqueeze(1), r=[], w=[colsB], slow=True)
            for j in range(4):
                P.dma("sp", mcols[:, j * 16:(j + 1) * 16], conv_w[j].rearrange("(c p) -> p c", p=128),
                      r=[], w=[mcolsB], slow=True)
            P.dma("sp", mcols[:, 64:80], conv_b.rearrange("(c p) -> p c", p=128), r=[], w=[mcolsB], slow=True)
            P.dma("sp", mcols[:, 80:96], b_in[0:2048].rearrange("(c p) -> p c", p=128), r=[], w=[mcolsB], slow=True)
            P.dma("sp", gb[:, 0:1], b_in[OFF["mi"]:OFF["mi"] + 4].unsqueeze(1), r=[], w=[gbB], slow=True)
            P.dma("sp", gb[:, 1:2], b_in[OFF["mf"]:OFF["mf"] + 4].unsqueeze(1), r=[], w=[gbB], slow=True)
        P.v("dve", "tensor_scalar", cols[:, 25:26], cols[:, 25:26], math.sqrt(128.0), None, ALU.mult,
            r=[colsB], w=[colsB])
        P.v("dve", "tensor_scalar", cols[:, 32:40], cols[:, 0:8], cols[:, 24:25], None, ALU.mult,
            r=[colsB], w=[colsB])
        P.v("dve", "tensor_scalar", cols[:, 40:48], cols[:, 8:16], cols[:, 25:26], None, ALU.mult,
            r=[colsB], w=[colsB])
        P.v("dve", "tensor_scalar", gb[:, 2:3], gb[:, 1:2], -1.0, None, ALU.mult, r=[gbB], w=[gbB])

        def load_w(eng, tile, col0, ncols, bufs):
            return P.dma(eng, tile, w3[:, :, col0:col0 + ncols], r=[], w=bufs)

        for b in range(NSEQ):
            P.barrier()
            ar.reset()
            NXB = 4
            xb = [ar.alloc([1024], F32) for _ in range(NXB)]
            xbB = [Buf("xb") for _ in range(NXB)]
            junk = ar.alloc([1024], BF16)
            junkB = Buf("junk")
            ssq = [ar.alloc([2], F32) for _ in range(NXB)]
            ssqB = [Buf("ssq") for _ in range(NXB)]
            xn = [ar.alloc([1024], BF16) for _ in range(NXB)]
            xnB = [Buf("xn") for _ in range(NXB)]
            tps = [banks[6 + j_][:, :].bitcast(BF16).rearrange("p (a b) -> p a b", b=128) for j_ in range(2)]
            for i in range(NCH):
                k = i % NXB
                tp = tps[i % 2]
                tpB = bankB[6 + (i % 2)]
                P.dma("sp", xb[k], x[b, i * 128:(i + 1) * 128, :], r=[], w=[xbB[k]])
                P.act(junk, xb[k], AF.Square, r=[xbB[k]], w=[junkB, ssqB[k]], accum_out=ssq[k][:, 0:1])
                P.v("dve", "tensor_scalar", ssq[k][:, 1:2], ssq[k][:, 0:1], 1.0 / D, EPS, ALU.mult, ALU.add,
                    r=[ssqB[k]], w=[ssqB[k]])
                P.v("pool", "tensor_tensor", ssq[k][:, 1:2], ssq[k][:, 1:2], mhalf[:, 0:1], ALU.pow,
                    r=[ssqB[k], mhB], w=[ssqB[k]])
                P.v("dve", "scalar_tensor_tensor", xn[k], xb[k], ssq[k][:, 1:2], normw_bc, ALU.mult, ALU.mult,
                    r=[xbB[k], ssqB[k], nwB], w=[xnB[k]])
                for kc in range(8):
                    P.tr(tp[:, kc, :], xn[k][:, kc * 128:(kc + 1) * 128], ident,
                         r=[xnB[k], cbB], w=[tpB])
                P.act(hT[:, :, i * 128:(i + 1) * 128], tp, AF.Copy, r=[tpB], w=[hTB[i // 4]])
            if "hT" in dbg:
                P.dma("sp", dbg_t["hT"], hT, r=hTB, w=[Buf()])

            if "noA" not in dbg:
                P.barrier()
                ar.reset()
                Wt = [[ar.alloc([8, 128], BF16) for _ in range(4)] for _ in range(2)]
                WtB = [[Buf("Wt") for _ in range(4)] for _ in range(2)]
                qT = ar.alloc([S], BF16)
                kT = ar.alloc([S], BF16)
                szT = ar.alloc([S], BF16)
                vv = ar.alloc([NCH, 128], BF16)
                sqv = [ar.alloc([512], BF16) for _ in range(2)]
                qb = [ar.alloc([512], F32) for _ in range(2)]
                lnr = [ar.alloc([512], F32) for _ in range(2)]
                sg = [ar.alloc([512], F32) for _ in range(2)]
                ebuf = [ar.alloc([1024], F32) for _ in range(2)]
                Lb = [ar.alloc([1024], BF16) for _ in range(3)]
                Ab = [ar.alloc([1024], BF16) for _ in range(3)]
                Lacc = [ar.alloc([512], BF16) for _ in range(2)]
                yt = [ar.alloc([512], BF16) for _ in range(2)]
                sqvB = [Buf() for _ in range(2)]
                qbB = [Buf() for _ in range(2)]
                lnrB = [Buf() for _ in range(2)]
                sgB = [Buf() for _ in range(2)]
                eB = [Buf() for _ in range(2)]
                LB = [Buf() for _ in range(3)]
                AB = [Buf() for _ in range(3)]
                LaccB = [Buf() for _ in range(2)]
                ytB = [Buf() for _ in range(2)]
                SEG = ["sq", "sk", "sv", "sz"]
                brow = ar.alloc([1024], BF16, parts=1)
                browB = Buf("brow")
                P.dma("pool", brow, b_in[OFF["sv"]:OFF["sv"] + 1024].unsqueeze(0), r=[], w=[browB])

                def loadA(h):
                    for i_, key in enumerate(SEG):
                        load_w("pool", Wt[h % 2][i_], OFF[key] + h * 128, 128, [WtB[h % 2][i_]])

                loadA(0)
                cnt = 0
                NHA = 1 if "h1" in dbg else 8
                for h in range(NHA):
                    if h + 1 < NHA:
                        loadA(h + 1)
                    Wq, Wk, Wv, Wz = Wt[h % 2]
                    WqB, WkB, WvB, WzB = WtB[h % 2]
                    qTB = [Buf() for _ in range(NT)]
                    kTB = [Buf() for _ in range(NT)]
                    szB = [Buf() for _ in range(NT)]
                    vB = [Buf() for _ in range(NT)]

                    def proj_fm(Wtile, WB, tt, pb):
                        for kc in range(8):
                            P.mm(banks[pb][:, :], Wtile[:, kc, :], hT[:, kc, tt * 512:(tt + 1) * 512],
                                 kc == 0, kc == 7, r=[WB, hTB[tt]], w=[bankB[pb]])

                    for tt in range(NT if "A0" not in dbg else 0):
                        pb = 4 + (cnt % 2)
                        k2 = cnt % 2
                        cnt += 1
                        proj_fm(Wz, WzB, tt, pb)
                        bz = cols[:, 16 + h:17 + h]
                        P.act(sg[k2], banks[pb][:, :], AF.Sigmoid, bias=bz, r=[bankB[pb], colsB], w=[sgB[k2]])
                        P.v("dve", "scalar_tensor_tensor", szT[:, tt * 512:(tt + 1) * 512], banks[pb][:, :], bz,
                            sg[k2], ALU.add, ALU.mult, r=[bankB[pb], sgB[k2], colsB], w=[szB[tt]])
                    for g in range(NT if ("A0" not in dbg and "A1" not in dbg) else 0):
                        pb = 4 + (cnt % 2)
                        cnt += 1
                        for c4_ in range(4):
                            j = g * 4 + c4_
                            o_ = banks[pb][:, c4_ * 128:(c4_ + 1) * 128]
                            for kc in range(8):
                                P.mm(o_, hT[:, kc, j * 128:(j + 1) * 128], Wv[:, kc, :], kc == 0, False,
                                     r=[WvB, hTB[g]], w=[bankB[pb]])
                            P.mm(o_, ones_b[0:1, 0:128], brow[0:1, h * 128:(h + 1) * 128],
                                 False, True, r=[cbB, browB], w=[bankB[pb]])
                        P.act(vv[:, g * 4:(g + 1) * 4, :],
                              banks[pb][:, :].rearrange("p (a b) -> p a b", b=128), AF.Copy,
                              r=[bankB[pb]], w=[vB[g]])
                    for (Wx, WxB, dest, destB, wc, bwc, bc) in (
                            (Wq, WqB, qT, qTB, cols[:, 24:25], cols[:, 32 + h:33 + h], cols[:, h:h + 1]),
                            (Wk, WkB, kT, kTB, cols[:, 25:26], cols[:, 40 + h:41 + h], cols[:, 8 + h:9 + h])):
                        pbs = {}
                        for tt in range(NT if not (dbg & {"A0", "A1", "A2"}) else 0):
                            if tt not in pbs:
                                pbs[tt] = (4 + (cnt % 2), cnt % 2)
                                cnt += 1
                                proj_fm(Wx, WxB, tt, pbs[tt][0])
                            pb, k2 = pbs[tt]
                            if tt + 1 < NT:
                                pbs[tt + 1] = (4 + (cnt % 2), cnt % 2)
                                cnt += 1
                                proj_fm(Wx, WxB, tt + 1, pbs[tt + 1][0])
                            P.act(sqv[k2], banks[pb][:, :], AF.Square, bias=bc, r=[bankB[pb], colsB], w=[sqvB[k2]])
                            if "Q0" in dbg:
                                continue
                            P.v("dve", "tensor_scalar", qb[k2], banks[pb][:, :], wc, bwc, ALU.mult, ALU.add,
                                r=[bankB[pb], colsB] + ([sqvB[k2]] if "SER" in dbg else []), w=[qbB[k2]])
                            if "Q1" in dbg:
                                continue
                            P.mm(banks[3][:, :], ones_b, sqv[k2], True, True, r=[cbB, sqvB[k2]], w=[bankB[3]])
                            if "Q2" in dbg:
                                continue
                            P.act(lnr[k2], banks[3][:, :], AF.Ln, bias=epsq[:, 0:1], r=[bankB[3], mhB], w=[lnrB[k2]])
                            if "Q3" in dbg:
                                continue
                            P.act(lnr[k2], lnr[k2], AF.Exp, scale=-0.5, r=[lnrB[k2]], w=[lnrB[k2]])
                            if "Q4" in dbg:
                                continue
                            P.v("dve", "tensor_tensor", dest[:, tt * 512:(tt + 1) * 512], qb[k2], lnr[k2], ALU.mult,
                                r=[qbB[k2], lnrB[k2]], w=[destB[tt]])
                    steps = [(qt, kbH) for qt in range(NT) for kbH in range(4 * qt + 3, 0, -2)]
                    NS = len(steps)
                    cur = [0]

                    def mask_of(kb, qt):
                        i_ = kb - 4 * qt
                        if i_ < 0:
                            return None
                        return sbmask[:, 384 - 128 * i_:384 - 128 * i_ + 512]

                    def stA1(n):
                        qt, kbH = steps[n]
                        z = n % 3
                        for j_, kb in enumerate((kbH, kbH - 1)):
                            P.mm(banks[2 * z + j_][:, :], kT[:, kb * 128:(kb + 1) * 128],
                                 qT[:, qt * 512:(qt + 1) * 512], True, True,
                                 r=[kTB[kb // 4], qTB[qt]], w=[bankB[2 * z + j_]])
                        P.act(ebuf[n % 2], psum[:, 2 * z * 512:(2 * z + 2) * 512], AF.Exp,
                              r=[bankB[2 * z], bankB[2 * z + 1]], w=[eB[n % 2]])

                    def stA2(n):
                        qt, kbH = steps[n]
                        P.act(Lb[n % 3], ebuf[n % 2], AF.Ln, bias=onec[:, 0:1], r=[eB[n % 2], mhB], w=[LB[n % 3]])
                        for j_, kb in enumerate((kbH, kbH - 1)):
                            m = mask_of(kb, qt)
                            if m is not None:
                                hv = Lb[n % 3][:, j_ * 512:(j_ + 1) * 512]
                                P.v("dve", "tensor_tensor", hv, hv, m, ALU.mult, r=[LB[n % 3], cbB], w=[LB[n % 3]])

                    def stB(n):
                        qt, kbH = steps[n]
                        z = n % 3
                        first = (kbH == 4 * qt + 3)
                        last = (kbH == 1)
                        LH = Lb[n % 3][:, 0:512]
                        LL = Lb[n % 3][:, 512:1024]
                        c_ = cur[0]
                        bH, bL = 2 * z, 2 * z + 1
                        P.mm(banks[bH][:, :], negU, LH, False, True, r=[cbB, LB[n % 3]], w=[bankB[bH]])
                        if not first:
                            P.mm(banks[bH][:, :], negones, Lacc[c_], False, True, r=[cbB, LaccB[c_]], w=[bankB[bH]])
                        P.mm(banks[bL][:, :], negU, LL, False, True, r=[cbB, LB[n % 3]], w=[bankB[bL]])
                        P.mm(banks[bL][:, :], negones, LH, False, True, r=[cbB, LB[n % 3]], w=[bankB[bL]])
                        if not first:
                            P.mm(banks[bL][:, :], negones, Lacc[c_], False, True, r=[cbB, LaccB[c_]], w=[bankB[bL]])
                        if not last:
                            if first:
                                P.v("dve", "tensor_tensor", Lacc[0], LH, LL, ALU.add, r=[LB[n % 3]], w=[LaccB[0]])
                                cur[0] = 0
                            else:
                                P.v("dve", "tensor_tensor", Lacc[1 - c_], Lacc[c_], LH, ALU.add,
                                    r=[LaccB[c_], LB[n % 3]], w=[LaccB[1 - c_]])
                                P.v("dve", "tensor_tensor", Lacc[1 - c_], Lacc[1 - c_], LL, ALU.add,
                                    r=[LaccB[1 - c_], LB[n % 3]], w=[LaccB[1 - c_]])
                                cur[0] = 1 - c_

                    def stB2(n):
                        qt, kbH = steps[n]
                        z = n % 3
                        P.act(Ab[n % 3], psum[:, 2 * z * 512:(2 * z + 2) * 512], AF.Exp,
                              r=[bankB[2 * z], bankB[2 * z + 1]], w=[AB[n % 3]])
                        for j_, kb in enumerate((kbH, kbH - 1)):
                            m = mask_of(kb, qt)
                            if m is not None:
                                hv = Ab[n % 3][:, j_ * 512:(j_ + 1) * 512]
                                P.v("dve", "tensor_tensor", hv, hv, m, ALU.mult, r=[AB[n % 3], cbB], w=[AB[n % 3]])

                    def stC(n):
                        qt, kbH = steps[n]
                        first = (kbH == 4 * qt + 3)
                        last = (kbH == 1)
                        ob = 6 + (qt % 2)
                        P.mm(banks[ob][:, :], vv[:, kbH, :], Ab[n % 3][:, 0:512], first, False,
                             r=[vB[kbH // 4], AB[n % 3]], w=[bankB[ob]])
                        P.mm(banks[ob][:, :], vv[:, kbH - 1, :], Ab[n % 3][:, 512:1024], False, last,
                             r=[vB[(kbH - 1) // 4], AB[n % 3]], w=[bankB[ob]])
                        if last:
                            y2 = qt % 2
                            P.v("dve", "tensor_tensor", yt[y2], banks[ob][:, :], szT[:, qt * 512:(qt + 1) * 512],
                                ALU.mult, r=[bankB[ob], szB[qt]], w=[ytB[y2]])
                            P.dma("sp", ys_d[b, h, :, qt * 512:(qt + 1) * 512], yt[y2], r=[ytB[y2]],
                                  w=[ysB[b][h][qt]])

                    if "noattn" in dbg:
                        NS = -3
                    for n in range(NS + 3):
                        if n < NS:
                            stA1(n)
                        if 0 <= n - 1 < NS:
                            stB(n - 1)
                        if 0 <= n - 2 < NS:
                            stB2(n - 2)
                        if n < NS:
                            stA2(n)
                        if 0 <= n - 3 < NS:
                            stC(n - 3)
            if "ys" in dbg:
                P.barrier()
                P.dma("sp", dbg_t["ys"], ys_d[b], r=[], w=[Buf()])

            if "noB" not in dbg:
                P.barrier()
                ar.reset()
                Wg8 = ar.alloc([8, 8], BF16)
                Wg8B = Buf()
                load_w("pool", Wg8, OFF["mi"], 8, [Wg8B])
                A1 = ar.alloc([S], F32, parts=4)
                A2 = ar.alloc([S], F32, parts=4)
                A3 = ar.alloc([S], F32, parts=4)
                A1B, A2B, A3B = Buf(), Buf(), Buf()
                Pt = ar.alloc([NCH + 1], F32, parts=4)
                PtB = Buf()
                cr = ar.alloc([NCH], F32, parts=4)
                crB = Buf()
                one4 = ar.alloc([2], F32, parts=4)
                one4B = Buf()
                P.v("dve", "memset", one4, 1.0, r=[], w=[one4B])
                for tt in range(NT):
                    sl = slice(tt * 512, (tt + 1) * 512)
                    for kc in range(8):
                        P.mm(banks[5][0:4, :], Wg8[:, kc, 0:4], hT[:, kc, sl], kc == 0, kc == 7,
                             r=[Wg8B, hTB[tt]], w=[bankB[5]])
                    P.act(A1[:, sl], banks[5][0:4, :], AF.Identity, bias=gb[:, 0:1], r=[bankB[5], gbB], w=[A1B])
                    for kc in range(8):
                        P.mm(banks[6][0:4, :], Wg8[:, kc, 4:8], hT[:, kc, sl], kc == 0, kc == 7,
                             r=[Wg8B, hTB[tt]], w=[bankB[6]])
                    P.act(A2[:, sl], banks[6][0:4, :], AF.Exp, scale=-1.0, bias=gb[:, 2:3],
                          r=[bankB[6], gbB], w=[A2B])
                P.act(A2, A2, AF.Ln, bias=one4[:, 0:1], r=[A2B, one4B], w=[A2B])
                P.v("dve", "tensor_tensor_scan", A3, one4[:, 0:1].to_broadcast([4, S]), A2, 0.0,
                    ALU.mult, ALU.subtract, r=[A2B, one4B], w=[A3B])
                P.v("dve", "tensor_tensor", A1, A1, A3, ALU.subtract, r=[A1B, A3B], w=[A1B])
                P.v("dve", "tensor_tensor_scan", A2, A1, A1, 0.0, ALU.max, ALU.max, r=[A1B], w=[A2B])
                P.v("dve", "memset", Pt[:, 0:1], 0.0, r=[], w=[PtB])
                P.v("dve", "tensor_copy", Pt[:, 1:NCH + 1],
                    A2.rearrange("p (c t) -> p c t", t=128)[:, :, 127], r=[A2B], w=[PtB])
                A1v = A1.rearrange("p (c t) -> p c t", t=128)
                A2v = A2.rearrange("p (c t) -> p c t", t=128)
                A3v = A3.rearrange("p (c t) -> p c t", t=128)
                Plo = Pt[:, 0:NCH].unsqueeze(2).to_broadcast([4, NCH, 128])
                Phi = Pt[:, 1:NCH + 1].unsqueeze(2).to_broadcast([4, NCH, 128])
                colps = banks[7]
                for slot in range(3):
                    if slot == 0:
                        P.v("dve", "tensor_tensor", A2v, A1v, Plo, ALU.subtract, r=[A1B, PtB], w=[A2B])
                        P.act(A2, A2, AF.Exp, bias=ml16[0:4, 0:1], r=[A2B, gbB], w=[A2B])
                    elif slot == 1:
                        P.v("dve", "tensor_tensor", A2v, A1v, Phi, ALU.subtract, r=[A1B, PtB], w=[A2B])
                        P.act(A2, A2, AF.Exp, bias=ml16[0:4, 0:1], r=[A2B, gbB], w=[A2B])
                    else:
                        P.v("dve", "tensor_tensor", A2v, A3v, Plo, ALU.add, r=[A3B, PtB], w=[A2B])
                        P.act(A2, A2, AF.Exp, scale=-1.0, r=[A2B], w=[A2B])
                    for c in range(NCH):
                        o0 = slot * NCH * 4 + c * 4
                        P.mm(colps[:, o0:o0 + 4], A2[:, c * 128:(c + 1) * 128], ident4, True, True,
                             r=[A2B, c4B], w=[bankB[7]])
                P.act(colv, colps[:, 0:3 * NCH * 4], AF.Copy, r=[bankB[7]], w=[colvB])
                P.v("dve", "tensor_tensor", cr, Pt[:, 0:NCH], Pt[:, 1:NCH + 1], ALU.subtract, r=[PtB], w=[crB])
                P.act(cr, cr, AF.Exp, r=[crB], w=[crB])
                for h in range(4):
                    P.mm(banks[6][:, h * NCH:(h + 1) * NCH], c4[:, 8 + h * 128:8 + (h + 1) * 128], cr, True, True,
                         r=[c4B, crB], w=[bankB[6]])
                P.act(carry_bc, banks[6][:, 0:4 * NCH], AF.Copy, r=[bankB[6]], w=[carryB])
                if "gates" in dbg:
                    P.dma("sp", dbg_t["colv"], colv, r=[colvB], w=[Buf()])
                    P.dma("sp", dbg_t["carry"], carry_bc, r=[carryB], w=[Buf()])

                P.barrier()
                ar.reset()
                Wm = [[ar.alloc([8, 256], BF16) for _ in range(5)] for _ in range(2)]
                WmB = [[Buf() for _ in range(5)] for _ in range(2)]
                raw = [ar.alloc([516], BF16) for _ in range(4)]
                rawB = [Buf() for _ in range(4)]
                dwc = ar.alloc([16, 128], BF16)
                dwcB = Buf()
                acc = [ar.alloc([512], F32) for _ in range(2)]
                accB = [Buf() for _ in range(2)]
                sgm = [ar.alloc([512], F32) for _ in range(2)]
                sgmB = [Buf() for _ in range(2)]
                qTt = [ar.alloc([2, 512], BF16) for _ in range(2)]
                kTt = [ar.alloc([2, 512], BF16) for _ in range(2)]
                qTtB = [Buf() for _ in range(2)]
                kTtB = [Buf() for _ in range(2)]
                vaug = [ar.alloc([4, 264], BF16) for _ in range(2)]
                vaugB = [Buf() for _ in range(2)]
                ogt = [ar.alloc([4, 256], BF16) for _ in range(2)]
                ogtB = [Buf() for _ in range(2)]
                sgo = [ar.alloc([512], F32) for _ in range(2)]
                sgoB = [Buf() for _ in range(2)]
                t1 = [ar.alloc([256], F32) for _ in range(2)]
                t1B = [Buf() for _ in range(2)]
                Cf = ar.alloc([2, 257], F32)
                CfB = Buf()
                Cbf = ar.alloc([2, 258], BF16)
                CbfB = Buf()
                scT = [ar.alloc([128], BF16) for _ in range(2)]
                scTB = [Buf() for _ in range(2)]
                kwt = [ar.alloc([256], BF16) for _ in range(2)]
                kwtB = [Buf() for _ in range(2)]
                ymt = [ar.alloc([256], BF16) for _ in range(2)]
                ymtB = [Buf() for _ in range(2)]
                ymT = [ar.alloc([2, 512], BF16) for _ in range(2)]
                ymTB = [Buf() for _ in range(2)]
                sc = [ar.alloc([8], F32) for _ in range(2)]
                scB = [Buf() for _ in range(2)]
                junk2 = ar.alloc([256], BF16)
                junk2B = Buf()
                for k_ in range(2):
                    P.v("dve", "memset", vaug[k_][:, :, 256:257], 1.0, r=[], w=[vaugB[k_]])
                MSEG = ["mq", "mk", "mv", "mo", "mz"]
                brow = ar.alloc([3072], BF16, parts=1)
                browB = Buf("brow")
                for i_, key in enumerate(["mv", "mo", "mz"]):
                    P.dma("pool", brow[:, i_ * 1024:(i_ + 1) * 1024],
                          b_in[OFF[key]:OFF[key] + 1024].unsqueeze(0), r=[], w=[browB])

                def loadB(h):
                    for i_, key in enumerate(MSEG):
                        load_w("pool", Wm[h % 2][i_], OFF[key] + h * 256, 256, [WmB[h % 2][i_]])

                loadB(0)
                pcnt = [0]
                NHB = 1 if "h1" in dbg else 4
                tpm = banks[3][:, :].bitcast(BF16).rearrange("p (a b) -> p a b", b=128)
                tpk = banks[0][:, 256:512].bitcast(BF16).rearrange("p (a b) -> p a b", b=128)
                for h in range(NHB):
                    if h + 1 < NHB:
                        loadB(h + 1)
                    Wq_, Wk_, Wv_, Wo_, Wz_ = Wm[h % 2]
                    WqB_, WkB_, WvB_, WoB_, WzB_ = WmB[h % 2]
                    P.v("dve", "memset", Cf, 0.0, r=[], w=[CfB])
                    for i_ in range(4):
                        P.v("dve", "memset", raw[i_][:, 0:3], 0.0, r=[], w=[rawB[i_]])
                    for qk_ in range(2):
                        for dc_ in range(2):
                            ci_ = qk_ * 8 + h * 2 + dc_
                            for j_ in range(4):
                                P.v("dve", "tensor_scalar", dwc[:, (qk_ * 2 + dc_) * 4 + j_, :], ident,
                                    mcols[:, j_ * 16 + ci_:j_ * 16 + ci_ + 1], None, ALU.mult,
                                    r=[cbB, mcolsB], w=[dwcB])

                    def proj_pieces(tt, h=h, Wq_=Wq_, Wk_=Wk_, Wv_=Wv_, Wo_=Wo_, Wz_=Wz_,
                                    WqB_=WqB_, WkB_=WkB_, WvB_=WvB_, WoB_=WoB_, WzB_=WzB_):
                        k2 = tt % 2
                        sl = slice(tt * 512, (tt + 1) * 512)
                        pieces = []

                        def qk_piece(qk, dc):
                            ci = qk * 8 + h * 2 + dc
                            ri = qk * 2 + dc
                            Wx, WxB = (Wq_, WqB_) if qk == 0 else (Wk_, WkB_)
                            a2 = pcnt[0] % 2
                            pcnt[0] += 2
                            for kc in range(8):
                                P.mm(banks[6][:, :], Wx[:, kc, dc * 128:(dc + 1) * 128], hT[:, kc, sl],
                                     kc == 0, kc == 7, r=[WxB, hTB[tt]], w=[bankB[6]])
                            if tt > 0:
                                P.v("dve", "tensor_copy", raw[ri][:, 0:3], raw[ri][:, 512:515],
                                    r=[rawB[ri]], w=[rawB[ri]])
                            P.act(raw[ri][:, 3:515], banks[6][:, :], AF.Identity, bias=mcols[:, 80 + ci:81 + ci],
                                  r=[bankB[6], mcolsB], w=[rawB[ri]])
                            for j in range(4):
                                P.mm(banks[7][:, :], dwc[:, ri * 4 + j, :], raw[ri][:, j:j + 512], j == 0, j == 3,
                                     r=[dwcB, rawB[ri]], w=[bankB[7]])
                            P.act(sgm[a2], banks[7][:, :], AF.Sigmoid, bias=mcols[:, 64 + ci:65 + ci],
                                  r=[bankB[7], mcolsB], w=[sgmB[a2]])
                            dst, dstB = (qTt, qTtB) if qk == 0 else (kTt, kTtB)
                            P.v("dve", "scalar_tensor_tensor", dst[k2][:, dc, :], banks[7][:, :],
                                mcols[:, 64 + ci:65 + ci], sgm[a2], ALU.add, ALU.mult,
                                r=[bankB[7], sgmB[a2], mcolsB], w=[dstB[k2]])

                        def v_piece(g2):
                            pb = 6 + (pcnt[0] % 2)
                            pcnt[0] += 1
                            for c2 in range(2):
                                c4_ = g2 * 2 + c2
                                j = tt * 4 + c4_
                                o_ = banks[pb][:, c2 * 256:(c2 + 1) * 256]
                                for kc in range(8):
                                    P.mm(o_, hT[:, kc, j * 128:(j + 1) * 128], Wv_[:, kc, :], kc == 0, False,
                                         r=[WvB_, hTB[tt]], w=[bankB[pb]])
                                P.mm(o_, ones_b[0:1, 0:128], brow[0:1, h * 256:(h + 1) * 256], False, True,
                                     r=[cbB, browB], w=[bankB[pb]])
                            P.act(vaug[k2][:, g2 * 2:g2 * 2 + 2, 0:256],
                                  banks[pb][:, :].rearrange("p (a b) -> p a b", b=256), AF.Copy,
                                  r=[bankB[pb]], w=[vaugB[k2]])

                        def og_piece(c4_):
                            j = tt * 4 + c4_
                            pb = 6 + (pcnt[0] % 2)
                            a2 = pcnt[0] % 2
                            pcnt[0] += 1
                            for half, (Wx, WxB, boff) in enumerate(((Wo_, WoB_, 1024), (Wz_, WzB_, 2048))):
                                o_ = banks[pb][:, half * 256:(half + 1) * 256]
                                for kc in range(8):
                                    P.mm(o_, hT[:, kc, j * 128:(j + 1) * 128], Wx[:, kc, :], kc == 0, False,
                                         r=[WxB, hTB[tt]], w=[bankB[pb]])
                                P.mm(o_, ones_b[0:1, 0:128], brow[0:1, boff + h * 256:boff + (h + 1) * 256],
                                     False, True, r=[cbB, browB], w=[bankB[pb]])
                            P.act(sgo[a2], banks[pb][:, :], AF.Sigmoid, r=[bankB[pb]], w=[sgoB[a2]])
                            P.v("dve", "tensor_tensor", t1[a2], sgo[a2][:, 0:256], sgo[a2][:, 256:512], ALU.mult,
                                r=[sgoB[a2]], w=[t1B[a2]])
                            P.v("dve", "tensor_tensor", ogt[k2][:, c4_, :], t1[a2], banks[pb][:, 256:512], ALU.mult,
                                r=[t1B[a2], bankB[pb]], w=[ogtB[k2]])

                        for qk in range(2):
                            for dc in range(2):
                                pieces.append(lambda qk=qk, dc=dc: qk_piece(qk, dc))
                        for g2 in range(2):
                            pieces.append(lambda g2=g2: v_piece(g2))
                        for c4_ in range(4):
                            pieces.append(lambda c4_=c4_: og_piece(c4_))
                        return pieces

                    def chunk_parts(g, h=h):
                        tt, c4_ = divmod(g, 4)
                        c = g
                        k2 = tt % 2
                        bl = slice(c4_ * 128, (c4_ + 1) * 128)
                        s2 = g % 2
                        hb = 1 + (g % 2)
                        es = colv[:, c * 4 + h:c * 4 + h + 1]
                        es2 = colv[:, NCH * 4 + c * 4 + h:NCH * 4 + c * 4 + h + 1]
                        dnm = colv[:, 2 * NCH * 4 + c * 4 + h:2 * NCH * 4 + c * 4 + h + 1]
                        car = carry_bc[:, h * NCH + c:h * NCH + c + 1]
                        upd = c < NCH - 1

                        def early():
                            for dc in range(2):
                                P.mm(banks[0][:, 0:128], kTt[k2][:, dc, bl], qTt[k2][:, dc, bl], dc == 0, dc == 1,
                                     r=[kTtB[k2], qTtB[k2]], w=[bankB[0]])
                            P.v("dve", "scalar_tensor_tensor", scT[s2], banks[0][:, 0:128], es, trimask,
                                ALU.mult, ALU.mult, r=[bankB[0], colvB, cbB], w=[scTB[s2]])
                            if upd:
                                for dc in range(2):
                                    P.tr(tpk[:, dc, :], kTt[k2][:, dc, bl], ident, r=[kTtB[k2], cbB], w=[bankB[0]])
                                P.act(kwt[s2].rearrange("p (a b) -> p a b", b=128), tpk[:, 0:2, :], AF.Copy,
                                      scale=es2, r=[bankB[0], colvB], w=[kwtB[s2]])
                            P.mm(banks[hb][:, 0:257], scT[s2], vaug[k2][:, c4_, 0:257], True, c == 0,
                                 r=[scTB[s2], vaugB[k2]], w=[bankB[hb]])
                            if c > 0:
                                for dc in range(2):
                                    P.mm(banks[hb][:, 0:257], qTt[k2][:, dc, bl], Cbf[:, dc, 0:257], False, dc == 1,
                                         r=[qTtB[k2], CbfB], w=[bankB[hb]])
                            if upd:
                                for dc in range(2):
                                    P.mm(banks[4 + dc][:, 0:257], kwt[s2][:, dc * 128:(dc + 1) * 128],
                                         vaug[k2][:, c4_, 0:257], True, True, r=[kwtB[s2], vaugB[k2]],
                                         w=[bankB[4 + dc]])
                                    P.v("dve", "scalar_tensor_tensor", Cf[:, dc, :], Cf[:, dc, :], car,
                                        banks[4 + dc][:, 0:257], ALU.mult, ALU.add,
                                        r=[CfB, carryB, bankB[4 + dc]], w=[CfB])
                                P.act(Cbf[:, :, 0:257], Cf, AF.Copy, r=[CfB], w=[CbfB])

                        def epi1():
                            P.act(junk2, banks[hb][:, 0:256], AF.Square, r=[bankB[hb]], w=[junk2B, scB[s2]],
                                  accum_out=sc[s2][:, 0:1])
                            P.act(sc[s2][:, 6:7], banks[hb][:, 256:257], AF.Abs, r=[bankB[hb]], w=[scB[s2]])
                            P.v("dve", "tensor_scalar", sc[s2][:, 1:2], sc[s2][:, 6:7], dnm, None,
                                ALU.max, r=[scB[s2], colvB], w=[scB[s2]])
                            P.v("dve", "reciprocal", sc[s2][:, 2:3], sc[s2][:, 1:2], r=[scB[s2]], w=[scB[s2]])
                            P.v("dve", "scalar_tensor_tensor", sc[s2][:, 3:4], sc[s2][:, 0:1], sc[s2][:, 2:3],
                                sc[s2][:, 2:3], ALU.mult, ALU.mult, r=[scB[s2]], w=[scB[s2]])
                            P.v("dve", "tensor_scalar", sc[s2][:, 3:4], sc[s2][:, 3:4], 1.0 / 256.0, EPS,
                                ALU.mult, ALU.add, r=[scB[s2]], w=[scB[s2]])
                            P.v("pool", "tensor_tensor", sc[s2][:, 4:5], sc[s2][:, 3:4], mhalf[:, 0:1], ALU.pow,
                                r=[scB[s2], mhB], w=[scB[s2]])

                        def epi2():
                            P.v("dve", "tensor_tensor", sc[s2][:, 5:6], sc[s2][:, 4:5], sc[s2][:, 2:3], ALU.mult,
                                r=[scB[s2]], w=[scB[s2]])
                            P.v("dve", "scalar_tensor_tensor", ymt[s2], banks[hb][:, 0:256], sc[s2][:, 5:6],
                                ogt[k2][:, c4_, :], ALU.mult, ALU.mult, r=[bankB[hb], scB[s2], ogtB[k2]],
                                w=[ymtB[s2]])

                        def fin():
                            for ec in range(2):
                                P.tr(tpm[:, ec, :], ymt[s2][:, ec * 128:(ec + 1) * 128], ident,
                                     r=[ymtB[s2], cbB], w=[bankB[3]])
                            P.act(ymT[k2][:, :, bl], tpm[:, 0:2, :], AF.Copy, r=[bankB[3]], w=[ymTB[k2]])
                            if c4_ == 3:
                                sl = slice(tt * 512, (tt + 1) * 512)
                                P.dma("sp", ym_d[b, 2 * h:2 * h + 2, :, sl].rearrange("c p t -> p c t"), ymT[k2],
                                      r=[ymTB[k2]], w=[ymB[b][2 * h][tt], ymB[b][2 * h + 1][tt]])

                        return early, epi1, epi2, fin

                    for pc in proj_pieces(0):
                        pc()
                    parts = {}
                    nxt = []
                    for g in range(NCH + 2):
                        if g < NCH:
                            tt, c4_ = divmod(g, 4)
                            if c4_ == 0:
                                nxt = proj_pieces(tt + 1) if tt + 1 < NT else []
                            lo = (len(nxt) * c4_) // 4
                            hi = (len(nxt) * (c4_ + 1)) // 4
                            for pc in nxt[lo:hi]:
                                pc()
                            parts[g] = chunk_parts(g)
                            parts[g][0]()
                        if 0 <= g - 1 < NCH:
                            parts[g - 1][2]()
                        if g < NCH:
                            parts[g][1]()
                        if 0 <= g - 2 < NCH:
                            parts[g - 2][3]()
            if "ym" in dbg:
                P.barrier()
                P.dma("sp", dbg_t["ym"], ym_d[b], r=[], w=[Buf()])

            if "noC" not in dbg:
                P.barrier()
                ar.reset()
                TC = 256
                NTC = S // TC
                Wpm_t = ar.alloc([8, 1024], BF16)
                Wps_t = ar.alloc([8, 1024], BF16)
                Wo_t = ar.alloc([8, 1024], BF16)
                Wg_t = ar.alloc([8, 2048], BF16)
                WpmB, WpsB, WoB, WgB = Buf(), Buf(), Buf(), Buf()
                mnwc = ar.alloc([8], F32)
                bgc = ar.alloc([16], F32)
                smB = Buf()
                ymt_c = [ar.alloc([8, TC], BF16) for _ in range(2)]
                yst_c = [ar.alloc([8, TC], BF16) for _ in range(2)]
                ymcB = [Buf() for _ in range(2)]
                yscB = [Buf() for _ in range(2)]
                sga = [ar.alloc([TC], F32) for _ in range(2)]
                sgb = [ar.alloc([TC], F32) for _ in range(2)]
                m1 = [ar.alloc([TC], F32) for _ in range(2)]
                m2 = [ar.alloc([TC], F32) for _ in range(2)]
                sgaB = [Buf() for _ in range(2)]
                sgbB = [Buf() for _ in range(2)]
                m1B = [Buf() for _ in range(2)]
                m2B = [Buf() for _ in range(2)]
                mrg = [ar.alloc([8, TC], BF16) for _ in range(2)]
                mrgB = [Buf() for _ in range(2)]
                xt = [ar.alloc([1024], F32) for _ in range(2)]
                xtB = [Buf() for _ in range(2)]
                for wt_, wb_, src in ((Wpm_t, WpmB, w_pm), (Wps_t, WpsB, w_ps), (Wo_t, WoB, w_out)):
                    s3 = src.rearrange("(kc p) n -> p kc n", p=128)
                    for hf in range(2):
                        P.dma("pool", wt_[:, :, hf * 512:(hf + 1) * 512], s3[:, :, hf * 512:(hf + 1) * 512],
                              r=[], w=[wb_])
                for hf in range(4):
                    P.dma("pool", Wg_t[:, :, hf * 512:(hf + 1) * 512],
                          w3[:, :, OFF["g"] + hf * 512:OFF["g"] + (hf + 1) * 512], r=[], w=[WgB])
                P.dma("sp", mnwc, mnw.rearrange("(c p) -> p c", p=128), r=[], w=[smB], slow=True)
                P.dma("sp", bgc, b_in[OFF["g"]:OFF["g"] + 2048].rearrange("(c p) -> p c", p=128), r=[], w=[smB],
                      slow=True)
                for kc in range(8):
                    P.v("dve", "tensor_scalar", Wpm_t[:, kc, :], Wpm_t[:, kc, :], mnwc[:, kc:kc + 1], None, ALU.mult,
                        r=[WpmB, smB], w=[WpmB])
                ymv = ym_d[b].rearrange("c p t -> p c t")
                ysv = ys_d[b].rearrange("c p t -> p c t")
                gcnt = 0
                for tc in range(NTC):
                    k2 = tc % 2
                    sl = slice(tc * TC, (tc + 1) * TC)
                    t512 = (tc * TC) // 512
                    P.dma("sp", ymt_c[k2], ymv[:, :, sl], r=[ymB[b][f_][t512] for f_ in range(8)], w=[ymcB[k2]])
                    P.dma("sp", yst_c[k2], ysv[:, :, sl], r=[ysB[b][f_][t512] for f_ in range(8)], w=[yscB[k2]])
                    for cc in range(8):
                        g2 = gcnt % 2
                        gcnt += 1
                        cs = slice(cc * 128, (cc + 1) * 128)
                        cs2 = slice(1024 + cc * 128, 1024 + (cc + 1) * 128)
                        bPm, bPs, bgm, bgs = banks[0 + g2], banks[2 + g2], banks[4 + g2], banks[6 + g2]
                        BPm, BPs, Bgm, Bgs = bankB[0 + g2], bankB[2 + g2], bankB[4 + g2], bankB[6 + g2]
                        for kc in range(8):
                            P.mm(bgm[:, 0:TC], Wg_t[:, kc, cs], hT[:, kc, sl], kc == 0, kc == 7,
                                 r=[WgB, hTB[t512]], w=[Bgm])
                        for kc in range(8):
                            P.mm(bgs[:, 0:TC], Wg_t[:, kc, cs2], hT[:, kc, sl], kc == 0, kc == 7,
                                 r=[WgB, hTB[t512]], w=[Bgs])
                        for kc in range(8):
                            P.mm(bPm[:, 0:TC], Wpm_t[:, kc, cs], ymt_c[k2][:, kc, :], kc == 0, kc == 7,
                                 r=[WpmB, ymcB[k2]], w=[BPm])
                        for kc in range(8):
                            P.mm(bPs[:, 0:TC], Wps_t[:, kc, cs], yst_c[k2][:, kc, :], kc == 0, kc == 7,
                                 r=[WpsB, yscB[k2]], w=[BPs])
                        P.act(sga[g2], bgm[:, 0:TC], AF.Sigmoid, bias=bgc[:, cc:cc + 1], r=[Bgm, smB], w=[sgaB[g2]])
                        P.act(sgb[g2], bgs[:, 0:TC], AF.Sigmoid, bias=bgc[:, 8 + cc:9 + cc], r=[Bgs, smB],
                              w=[sgbB[g2]])
                        P.v("dve", "tensor_tensor", m1[g2], bPm[:, 0:TC], sga[g2], ALU.mult, r=[BPm, sgaB[g2]],
                            w=[m1B[g2]])
                        P.v("dve", "tensor_tensor", m2[g2], bPs[:, 0:TC], sgb[g2], ALU.mult, r=[BPs, sgbB[g2]],
                            w=[m2B[g2]])
                        P.v("pool", "tensor_tensor", mrg[k2][:, cc, :], m1[g2], m2[g2], ALU.add,
                            r=[m1B[g2], m2B[g2]], w=[mrgB[k2]])
                    for sub in range(TC // 128):
                        row0 = tc * TC + sub * 128
                        x2 = (tc * (TC // 128) + sub) % 2
                        P.dma("sp", xt[x2], x[b, row0:row0 + 128, :], r=[], w=[xtB[x2]])
                        for eh in range(2):
                            g2 = gcnt % 2
                            gcnt += 1
                            bo, BO = banks[0 + g2], bankB[0 + g2]
                            for cc in range(8):
                                P.mm(bo[:, :], mrg[k2][:, cc, sub * 128:(sub + 1) * 128],
                                     Wo_t[:, cc, eh * 512:(eh + 1) * 512], cc == 0, cc == 7,
                                     r=[mrgB[k2], WoB], w=[BO])
                            P.v("dve", "tensor_tensor", xt[x2][:, eh * 512:(eh + 1) * 512], bo[:, :],
                                xt[x2][:, eh * 512:(eh + 1) * 512], ALU.add, r=[BO, xtB[x2]], w=[xtB[x2]])
                        P.dma("sp", out[b, row0:row0 + 128, :], xt[x2], r=[xtB[x2]], w=[Buf()])

        P.emit()
    return nc


_NC_CACHE = {}


def kernel(x, norm_w, w_in, b_in, conv_w, conv_b, mlstm_norm_w, sb_q_norm_w, sb_k_norm_w,
           w_proj_m, w_proj_s, w_out):
    n = 8
    B, S, _ = x.shape
    nseq = B // n
    nc = build(S, nseq)
    c, c4 = _consts()
    f = lambda a: np.ascontiguousarray(np.asarray(a, dtype=np.float32))
    shared = dict(norm_w=f(norm_w), w_in=f(w_in), b_in=f(b_in), conv_w=f(conv_w), conv_b=f(conv_b),
                  mlstm_norm_w=f(mlstm_norm_w), sb_q_norm_w=f(sb_q_norm_w), sb_k_norm_w=f(sb_k_norm_w),
                  w_proj_m=f(w_proj_m), w_proj_s=f(w_proj_s), w_out=f(w_out), cst=c, cst4=c4)
    xs = f(x)
    in_maps = [dict(shared, x=xs[i * nseq:(i + 1) * nseq]) for i in range(n)]
    res = run_bass_kernel_spmd(nc, in_maps, core_ids=list(range(n)))
    return np.concatenate([r["out"] for r in res.results], axis=0)
```

```python
import math
import numpy as np
import concourse.bass as bass
import concourse.mybir as mybir
from concourse.bass_utils import run_bass_kernel_spmd

F32 = mybir.dt.float32
BF16 = mybir.dt.bfloat16
AF = mybir.ActivationFunctionType
ALU = mybir.AluOpType
AX = mybir.AxisListType

D = 1024
IN_COLS = 11272
EPS = 1e-6
OFF = dict(mq=0, mk=1024, mv=2048, mi=3072, mf=3076, mo=3080, mz=4104,
           sq=5128, sk=6152, sv=7176, sz=8200, g=9224)
LN16 = math.log(16.0)


class Buf:
    __slots__ = ("name", "w", "r", "excl")

    def __init__(self, name="", excl=False):
        self.name = name
        self.w = None
        self.r = {}
        self.excl = excl


class Op:
    __slots__ = ("eng", "tl", "idx", "fn", "waits", "signal", "clock", "dma", "semval")


ENGS = ["pe", "act", "dve", "pool", "sp"]


class Prog:
    def __init__(self, nc, ndma=8):
        self.nc = nc
        self.ops = {e: [] for e in ENGS}
        self.tl_ops = {}
        self.known = {e: {} for e in ENGS}
        self.ndma = ndma
        self.dma_count = {e: 0 for e in ENGS}
        self.bar_ops = []
        self.bar_gen = 0
        self.eng_gen = {e: 0 for e in ENGS}

    def barrier(self):
        self.bar_ops = [lst[-1] for lst in self.tl_ops.values() if lst]
        self.bar_gen += 1

    def add(self, eng, fn, r=(), w=(), dma=False):
        op = Op()
        op.eng = eng
        op.fn = fn
        op.dma = dma
        op.signal = dma
        deps = []
        for b in r:
            if b.w is not None:
                deps.append((b.w, True))
            if b.excl:
                for o in b.r.values():
                    deps.append((o, False))
        for b in w:
            if b.w is not None:
                deps.append((b.w, False))
            for o in b.r.values():
                deps.append((o, False))
        if self.eng_gen[eng] < self.bar_gen:
            self.eng_gen[eng] = self.bar_gen
            for o in self.bar_ops:
                deps.append((o, True))
        if dma:
            j = self.dma_count[eng]
            self.dma_count[eng] += 1
            tl = (eng, j % self.ndma)
            prev = self.tl_ops.get(tl)
            if prev:
                deps.append((prev[-1], True))
        else:
            tl = eng
        lst = self.tl_ops.setdefault(tl, [])
        op.tl = tl
        op.idx = len(lst)
        kn = self.known[eng]
        best = {}
        for d, raw in deps:
            if d.tl == eng:
                if eng == "pe" or not raw:
                    continue
            if kn.get(d.tl, -1) >= d.idx:
                continue
            if d.tl not in best or best[d.tl].idx < d.idx:
                best[d.tl] = d
        waits = []
        for d in best.values():
            if kn.get(d.tl, -1) >= d.idx:
                continue
            waits.append(d)
            d.signal = True
            for t, i in d.clock.items():
                if kn.get(t, -1) < i:
                    kn[t] = i
            if kn.get(d.tl, -1) < d.idx:
                kn[d.tl] = d.idx
        op.waits = waits
        op.clock = dict(kn)
        lst.append(op)
        self.ops[eng].append(op)
        for b in r:
            b.r[tl] = op
        for b in w:
            b.w = op
            b.r = {}
        return op

    def mm(self, out, lhsT, rhs, start, stop, r, w):
        return self.add("pe", lambda e: e.matmul(out, lhsT, rhs, start=start, stop=stop,
                                                 skip_group_check=True), r, w)

    def tr(self, out, in_, ident, r, w):
        return self.add("pe", lambda e: e.transpose(out, in_, ident), r, w)

    def act(self, out, in_, func, r, w, bias=None, scale=None, accum_out=None):
        kw = {}
        if bias is not None:
            kw["bias"] = bias
        if scale is not None:
            kw["scale"] = scale
        if accum_out is not None:
            kw["accum_out"] = accum_out
        return self.add("act", lambda e: e.activation(out, in_, func, **kw), r, w)

    def v(self, eng, name, *args, r=(), w=(), **kw):
        return self.add(eng, lambda e: getattr(e, name)(*args, **kw), r, w)

    def dma(self, eng, out, in_, r, w, slow=False):
        if slow:
            return self.add(eng, lambda e: e.dma_start(out=out, in_=in_, allow_slow_non_contiguous=True),
                            r, w, dma=True)
        return self.add(eng, lambda e: e.dma_start(out=out, in_=in_), r, w, dma=True)

    def emit(self):
        nc = self.nc
        for tl, lst in self.tl_ops.items():
            if isinstance(tl, tuple):
                for o in lst:
                    o.semval = 16 * (o.idx + 1)
            else:
                c = 0
                for o in lst:
                    if o.signal:
                        c += 1
                    o.semval = c
        sems = {}
        import contextlib
        with contextlib.ExitStack() as st:
            for tl in self.tl_ops:
                nm = tl if isinstance(tl, str) else "%s_d%d" % tl
                sems[tl] = st.enter_context(nc.semaphore("s_" + nm))
            block = st.enter_context(nc.Block())
            handles = {"pe": block.tensor, "act": block.scalar, "dve": block.vector,
                       "pool": block.gpsimd, "sp": block.sync}

            def make(engname):
                def body(e):
                    for o in self.ops[engname]:
                        for d in o.waits:
                            e.wait_ge(sems[d.tl], d.semval)
                        ins = o.fn(e)
                        if o.signal:
                            ins.then_inc(sems[o.tl], 16 if o.dma else 1)
                    for tl, lst in self.tl_ops.items():
                        if isinstance(tl, tuple) and tl[0] == engname and lst:
                            e.wait_ge(sems[tl], lst[-1].semval)
                return body

            for en in ENGS:
                if self.ops[en]:
                    handles[en](make(en))


def _consts():
    c = np.zeros((128, 1536), np.float32)
    j = np.arange(128)[:, None]
    s = np.arange(128)[None, :]
    c[:, 0:128] = (j == s)
    c[:, 128:256] = -(j >= s).astype(np.float32)
    c[:, 256:384] = -1.0
    c[:, 384:512] = 1.0
    c[:, 512:640] = (j <= s)
    cc = np.arange(896)[None, :]
    c[:, 640:1536] = ((cc - 384) > j)
    c4 = np.zeros((4, 520), np.float32)
    c4[:, 0:4] = np.eye(4)
    for h in range(4):
        c4[h, 8 + h * 128: 8 + (h + 1) * 128] = 1.0
    return c, c4


def build(S, NSEQ, dbg=None):
    dbg = dbg or set()
    nc = bass.Bass("TRN2", target_bir_lowering=False)
    NT = S // 512
    NCH = S // 128
    dt = nc.dram_tensor
    x = dt("x", [NSEQ, S, D], F32, kind="ExternalInput").ap()
    norm_w = dt("norm_w", [D], F32, kind="ExternalInput").ap()
    w_in = dt("w_in", [D, IN_COLS], F32, kind="ExternalInput").ap()
    b_in = dt("b_in", [IN_COLS], F32, kind="ExternalInput").ap()
    conv_w = dt("conv_w", [4, 2048], F32, kind="ExternalInput").ap()
    conv_b = dt("conv_b", [2048], F32, kind="ExternalInput").ap()
    mnw = dt("mlstm_norm_w", [1024], F32, kind="ExternalInput").ap()
    sqw = dt("sb_q_norm_w", [128], F32, kind="ExternalInput").ap()
    skw = dt("sb_k_norm_w", [128], F32, kind="ExternalInput").ap()
    w_pm = dt("w_proj_m", [D, D], F32, kind="ExternalInput").ap()
    w_ps = dt("w_proj_s", [D, D], F32, kind="ExternalInput").ap()
    w_out = dt("w_out", [D, D], F32, kind="ExternalInput").ap()
    cst = dt("cst", [128, 1536], F32, kind="ExternalInput").ap()
    cst4 = dt("cst4", [4, 520], F32, kind="ExternalInput").ap()
    out = dt("out", [NSEQ, S, D], F32, kind="ExternalOutput").ap()
    ys_d = dt("ys_scr", [NSEQ, 8, 128, S], BF16, kind="Internal").ap()
    ym_d = dt("ym_scr", [NSEQ, 8, 128, S], BF16, kind="Internal").ap()
    dbg_t = {}
    if "hT" in dbg:
        dbg_t["hT"] = dt("dbg_hT", [128, 8, S], BF16, kind="ExternalOutput").ap()
    if "ys" in dbg:
        dbg_t["ys"] = dt("dbg_ys", [8, 128, S], BF16, kind="ExternalOutput").ap()
    if "ym" in dbg:
        dbg_t["ym"] = dt("dbg_ym", [8, 128, S], BF16, kind="ExternalOutput").ap()
    if "gates" in dbg:
        dbg_t["colv"] = dt("dbg_colv", [128, 3 * NCH * 4], F32, kind="ExternalOutput").ap()
        dbg_t["carry"] = dt("dbg_carry", [128, 4 * NCH], F32, kind="ExternalOutput").ap()

    w3 = w_in.rearrange("(kc p) n -> p kc n", p=128)

    import contextlib
    with contextlib.ExitStack() as st:
        TOTAL = 104000
        big = st.enter_context(nc.sbuf_tensor("big", [128, TOTAL], BF16))
        psum = st.enter_context(nc.psum_tensor("psum_all", [128, 4096], F32))
        banks = [psum[:, i * 512:(i + 1) * 512] for i in range(8)]
        bankB = [Buf("bank%d" % i, excl=True) for i in range(8)]

        class Arena:
            def __init__(self, lo, hi):
                self.lo, self.hi, self.p = lo, hi, lo

            def reset(self):
                self.p = self.lo

            def alloc(self, shape, dtype, parts=128):
                n = 1
                for d_ in shape:
                    n *= d_
                nb = n * (2 if dtype == BF16 else 4)
                nb = (nb + 63) // 64 * 64
                ne = nb // 2
                assert self.p + ne <= self.hi, ("arena overflow", self.p, ne, self.hi)
                v = big[0:parts, self.p:self.p + ne]
                self.p += ne
                if dtype != BF16:
                    v = v.bitcast(dtype)
                v = v[:, 0:n]
                if len(shape) == 2:
                    v = v.rearrange("p (a b) -> p a b", b=shape[1])
                elif len(shape) == 3:
                    v = v.rearrange("p (a b c) -> p a b c", b=shape[1], c=shape[2])
                return v

        pers = Arena(0, 40000)
        ar = Arena(40000, TOTAL)

        P = Prog(nc)

        hT = pers.alloc([8, S], BF16)
        hTB = [Buf("hT%d" % i) for i in range(NT)]
        cb = pers.alloc([1536], BF16)
        cbB = Buf("cb")
        c4 = pers.alloc([520], F32, parts=4)
        c4B = Buf("c4")
        normw_bc = pers.alloc([1024], F32)
        nwB = Buf("normw")
        cols = pers.alloc([64], F32)
        colsB = Buf("cols")
        colv = pers.alloc([3 * NCH * 4], F32)
        colvB = Buf("colv")
        carry_bc = pers.alloc([4 * NCH], F32)
        carryB = Buf("carry")
        mcols = pers.alloc([16 * 6 + 8], F32)
        mcolsB = Buf("mcols")
        gb = pers.alloc([4], F32, parts=4)
        gbB = Buf("gb")

        mhalf = pers.alloc([2], F32)
        mhB = Buf("mhalf")
        P.v("pool", "memset", mhalf, -0.5, r=[], w=[mhB])
        epsq = pers.alloc([2], F32)
        P.v("pool", "memset", epsq, 128.0 * EPS, r=[], w=[mhB])
        onec = pers.alloc([2], F32)
        P.v("pool", "memset", onec, 1.0, r=[], w=[mhB])
        ml16 = pers.alloc([2], F32)
        P.v("pool", "memset", ml16, -LN16, r=[], w=[gbB])
        ysB = [[[Buf() for _ in range(NT)] for _ in range(8)] for _ in range(NSEQ)]
        ymB = [[[Buf() for _ in range(NT)] for _ in range(8)] for _ in range(NSEQ)]
        ident = cb[:, 0:128]
        negU = cb[:, 128:256]
        negones = cb[:, 256:384]
        ones_b = cb[:, 384:512]
        trimask = cb[:, 512:640]
        sbmask = cb[:, 640:1536]
        ident4 = c4[:, 0:4]

        P.dma("pool", cb, cst, r=[], w=[cbB])
        P.dma("sp", c4, cst4, r=[], w=[c4B])
        P.dma("sp", normw_bc, norm_w.partition_broadcast(128), r=[], w=[nwB])
        with nc.allow_non_contiguous_dma(reason="tiny one-time bias/gain column loads"):
            for i, key in enumerate(["sq", "sk", "sz"]):
                P.dma("sp", cols[:, 8 * i:8 * i + 8],
                      b_in[OFF[key]:OFF[key] + 1024].rearrange("(h p) -> p h", p=128), r=[], w=[colsB], slow=True)
            P.dma("sp", cols[:, 24:25], sqw.unsqueeze(1), r=[], w=[colsB], slow=True)
            P.dma("sp", cols[:, 25:26], skw.unsqueeze(1), r=[], w=[colsB], slow=True)
            for j in range(4):
                P.dma("sp", mcols[:, j * 16:(j + 1) * 16], conv_w[j].rearrange("(c p) -> p c", p=128),
                      r=[], w=[mcolsB], slow=True)
            P.dma("sp", mcols[:, 64:80], conv_b.rearrange("(c p) -> p c", p=128), r=[], w=[mcolsB], slow=True)
            P.dma("sp", mcols[:, 80:96], b_in[0:2048].rearrange("(c p) -> p c", p=128), r=[], w=[mcolsB], slow=True)
            P.dma("sp", gb[:, 0:1], b_in[OFF["mi"]:OFF["mi"] + 4].unsqueeze(1), r=[], w=[gbB], slow=True)
            P.dma("sp", gb[:, 1:2], b_in[OFF["mf"]:OFF["mf"] + 4].unsqueeze(1), r=[], w=[gbB], slow=True)
        P.v("dve", "tensor_scalar", cols[:, 25:26], cols[:, 25:26], math.sqrt(128.0), None, ALU.mult,
            r=[colsB], w=[colsB])
        P.v("dve", "tensor_scalar", cols[:, 32:40], cols[:, 0:8], cols[:, 24:25], None, ALU.mult,
            r=[colsB], w=[colsB])
        P.v("dve", "tensor_scalar", cols[:, 40:48], cols[:, 8:16], cols[:, 25:26], None, ALU.mult,
            r=[colsB], w=[colsB])
        P.v("dve", "tensor_scalar", gb[:, 2:3], gb[:, 1:2], -1.0, None, ALU.mult, r=[gbB], w=[gbB])

        def load_w(eng, tile, col0, ncols, bufs):
            return P.dma(eng, tile, w3[:, :, col0:col0 + ncols], r=[], w=bufs)

        for b in range(NSEQ):
            P.barrier()
            ar.reset()
            NXB = 4
            xb = [ar.alloc([1024], F32) for _ in range(NXB)]
            xbB = [Buf("xb") for _ in range(NXB)]
            junk = ar.alloc([1024], BF16)
            junkB = Buf("junk")
            ssq = [ar.alloc([2], F32) for _ in range(NXB)]
            ssqB = [Buf("ssq") for _ in range(NXB)]
            xn = [ar.alloc([1024], BF16) for _ in range(NXB)]
            xnB = [Buf("xn") for _ in range(NXB)]
            tps = [banks[6 + j_][:, :].bitcast(BF16).rearrange("p (a b) -> p a b", b=128) for j_ in range(2)]
            def p0A(i):
                k = i % NXB
                P.dma("sp", xb[k], x[b, i * 128:(i + 1) * 128, :], r=[], w=[xbB[k]])
                P.act(junk, xb[k], AF.Square, r=[xbB[k]], w=[junkB, ssqB[k]], accum_out=ssq[k][:, 0:1])
                P.v("dve", "tensor_scalar", ssq[k][:, 1:2], ssq[k][:, 0:1], 1.0 / D, EPS, ALU.mult, ALU.add,
                    r=[ssqB[k]], w=[ssqB[k]])
                P.v("pool", "tensor_tensor", ssq[k][:, 1:2], ssq[k][:, 1:2], mhalf[:, 0:1], ALU.pow,
                    r=[ssqB[k], mhB], w=[ssqB[k]])

            def p0B(i):
                k = i % NXB
                tp = tps[i % 2]
                tpB = bankB[6 + (i % 2)]
                P.v("dve", "scalar_tensor_tensor", xn[k], xb[k], ssq[k][:, 1:2], normw_bc, ALU.mult, ALU.mult,
                    r=[xbB[k], ssqB[k], nwB], w=[xnB[k]])
                for kc in range(8):
                    P.tr(tp[:, kc, :], xn[k][:, kc * 128:(kc + 1) * 128], ident,
                         r=[xnB[k], cbB], w=[tpB])

            def p0C(i):
                tp = tps[i % 2]
                tpB = bankB[6 + (i % 2)]
                P.act(hT[:, :, i * 128:(i + 1) * 128], tp, AF.Copy, r=[tpB], w=[hTB[i // 4]])

            for n in range(NCH + 2):
                if n < NCH:
                    p0A(n)
                if 0 <= n - 1 < NCH:
                    p0B(n - 1)
                if 0 <= n - 2 < NCH:
                    p0C(n - 2)
            if "hT" in dbg:
                P.dma("sp", dbg_t["hT"], hT, r=hTB, w=[Buf()])

            if "noA" not in dbg:
                P.barrier()
                ar.reset()
                Wt = [[ar.alloc([8, 128], BF16) for _ in range(4)] for _ in range(2)]
                WtB = [[Buf("Wt") for _ in range(4)] for _ in range(2)]
                qT = ar.alloc([S], BF16)
                kT = ar.alloc([S], BF16)
                szT = ar.alloc([S], BF16)
                vv = ar.alloc([NCH, 128], BF16)
                sqv = [ar.alloc([512], BF16) for _ in range(2)]
                qb = [ar.alloc([512], F32) for _ in range(2)]
                lnr = [ar.alloc([512], F32) for _ in range(2)]
                sg = [ar.alloc([512], F32) for _ in range(2)]
                ebuf = [ar.alloc([1024], F32) for _ in range(2)]
                Lb = [ar.alloc([1024], BF16) for _ in range(3)]
                Ab = [ar.alloc([1024], BF16) for _ in range(3)]
                Lacc = [ar.alloc([512], BF16) for _ in range(2)]
                yt = [ar.alloc([512], BF16) for _ in range(2)]
                sqvB = [Buf() for _ in range(2)]
                qbB = [Buf() for _ in range(2)]
                lnrB = [Buf() for _ in range(2)]
                sgB = [Buf() for _ in range(2)]
                eB = [Buf() for _ in range(2)]
                LB = [Buf() for _ in range(3)]
                AB = [Buf() for _ in range(3)]
                LaccB = [Buf() for _ in range(2)]
                ytB = [Buf() for _ in range(2)]
                SEG = ["sq", "sk", "sv", "sz"]
                brow = ar.alloc([1024], BF16, parts=1)
                browB = Buf("brow")
                P.dma("pool", brow, b_in[OFF["sv"]:OFF["sv"] + 1024].unsqueeze(0), r=[], w=[browB])

                def loadA(h):
                    for i_, key in enumerate(SEG):
                        load_w("pool", Wt[h % 2][i_], OFF[key] + h * 128, 128, [WtB[h % 2][i_]])

                loadA(0)
                cnt = 0
                NHA = 1 if "h1" in dbg else 8
                for h in range(NHA):
                    if h + 1 < NHA:
                        loadA(h + 1)
                    Wq, Wk, Wv, Wz = Wt[h % 2]
                    WqB, WkB, WvB, WzB = WtB[h % 2]
                    qTB = [Buf() for _ in range(NT)]
                    kTB = [Buf() for _ in range(NT)]
                    szB = [Buf() for _ in range(NT)]
                    vB = [Buf() for _ in range(NT)]

                    def proj_fm(Wtile, WB, tt, pb):
                        for kc in range(8):
                            P.mm(banks[pb][:, :], Wtile[:, kc, :], hT[:, kc, tt * 512:(tt + 1) * 512],
                                 kc == 0, kc == 7, r=[WB, hTB[tt]], w=[bankB[pb]])

                    for tt in range(NT if "A0" not in dbg else 0):
                        pb = 4 + (cnt % 2)
                        k2 = cnt % 2
                        cnt += 1
                        proj_fm(Wz, WzB, tt, pb)
                        bz = cols[:, 16 + h:17 + h]
                        P.act(sg[k2], banks[pb][:, :], AF.Sigmoid, bias=bz, r=[bankB[pb], colsB], w=[sgB[k2]])
                        P.v("dve", "scalar_tensor_tensor", szT[:, tt * 512:(tt + 1) * 512], banks[pb][:, :], bz,
                            sg[k2], ALU.add, ALU.mult, r=[bankB[pb], sgB[k2], colsB], w=[szB[tt]])
                    for g in range(NT if ("A0" not in dbg and "A1" not in dbg) else 0):
                        pb = 4 + (cnt % 2)
                        cnt += 1
                        for c4_ in range(4):
                            j = g * 4 + c4_
                            o_ = banks[pb][:, c4_ * 128:(c4_ + 1) * 128]
                            for kc in range(8):
                                P.mm(o_, hT[:, kc, j * 128:(j + 1) * 128], Wv[:, kc, :], kc == 0, False,
                                     r=[WvB, hTB[g]], w=[bankB[pb]])
                            P.mm(o_, ones_b[0:1, 0:128], brow[0:1, h * 128:(h + 1) * 128],
                                 False, True, r=[cbB, browB], w=[bankB[pb]])
                        P.act(vv[:, g * 4:(g + 1) * 4, :],
                              banks[pb][:, :].rearrange("p (a b) -> p a b", b=128), AF.Copy,
                              r=[bankB[pb]], w=[vB[g]])
                    for (Wx, WxB, dest, destB, wc, bwc, bc) in (
                            (Wq, WqB, qT, qTB, cols[:, 24:25], cols[:, 32 + h:33 + h], cols[:, h:h + 1]),
                            (Wk, WkB, kT, kTB, cols[:, 25:26], cols[:, 40 + h:41 + h], cols[:, 8 + h:9 + h])):
                        pbs = {}
                        for tt in range(NT if not (dbg & {"A0", "A1", "A2"}) else 0):
                            if tt not in pbs:
                                pbs[tt] = (4 + (cnt % 2), cnt % 2)
                                cnt += 1
                                proj_fm(Wx, WxB, tt, pbs[tt][0])
                            pb, k2 = pbs[tt]
                            if tt + 1 < NT:
                                pbs[tt + 1] = (4 + (cnt % 2), cnt % 2)
                                cnt += 1
                                proj_fm(Wx, WxB, tt + 1, pbs[tt + 1][0])
                            P.act(sqv[k2], banks[pb][:, :], AF.Square, bias=bc, r=[bankB[pb], colsB], w=[sqvB[k2]])
                            if "Q0" in dbg:
                                continue
                            P.v("dve", "tensor_scalar", qb[k2], banks[pb][:, :], wc, bwc, ALU.mult, ALU.add,
                                r=[bankB[pb], colsB] + ([sqvB[k2]] if "SER" in dbg else []), w=[qbB[k2]])
                            if "Q1" in dbg:
                                continue
                            P.mm(banks[3][:, :], ones_b, sqv[k2], True, True, r=[cbB, sqvB[k2]], w=[bankB[3]])
                            if "Q2" in dbg:
                                continue
                            P.act(lnr[k2], banks[3][:, :], AF.Ln, bias=epsq[:, 0:1], r=[bankB[3], mhB], w=[lnrB[k2]])
                            if "Q3" in dbg:
                                continue
                            P.act(lnr[k2], lnr[k2], AF.Exp, scale=-0.5, r=[lnrB[k2]], w=[lnrB[k2]])
                            if "Q4" in dbg:
                                continue
                            P.v("dve", "tensor_tensor", dest[:, tt * 512:(tt + 1) * 512], qb[k2], lnr[k2], ALU.mult,
                                r=[qbB[k2], lnrB[k2]], w=[destB[tt]])
                    steps = [(qt, kbH) for qt in range(NT) for kbH in range(4 * qt + 3, 0, -2)]
                    NS = len(steps)
                    cur = [0]

                    def mask_of(kb, qt):
                        i_ = kb - 4 * qt
                        if i_ < 0:
                            return None
                        return sbmask[:, 384 - 128 * i_:384 - 128 * i_ + 512]

                    def stA1(n):
                        qt, kbH = steps[n]
                        z = n % 3
                        for j_, kb in enumerate((kbH, kbH - 1)):
                            P.mm(banks[2 * z + j_][:, :], kT[:, kb * 128:(kb + 1) * 128],
                                 qT[:, qt * 512:(qt + 1) * 512], True, True,
                                 r=[kTB[kb // 4], qTB[qt]], w=[bankB[2 * z + j_]])
                        P.act(ebuf[n % 2], psum[:, 2 * z * 512:(2 * z + 2) * 512], AF.Exp,
                              r=[bankB[2 * z], bankB[2 * z + 1]], w=[eB[n % 2]])

                    def stA2(n):
                        qt, kbH = steps[n]
                        P.act(Lb[n % 3], ebuf[n % 2], AF.Ln, bias=onec[:, 0:1], r=[eB[n % 2], mhB], w=[LB[n % 3]])
                        for j_, kb in enumerate((kbH, kbH - 1)):
                            m = mask_of(kb, qt)
                            if m is not None:
                                hv = Lb[n % 3][:, j_ * 512:(j_ + 1) * 512]
                                P.v("dve", "tensor_tensor", hv, hv, m, ALU.mult, r=[LB[n % 3], cbB], w=[LB[n % 3]])

                    def stB(n):
                        qt, kbH = steps[n]
                        z = n % 3
                        first = (kbH == 4 * qt + 3)
                        last = (kbH == 1)
                        LH = Lb[n % 3][:, 0:512]
                        LL = Lb[n % 3][:, 512:1024]
                        c_ = cur[0]
                        bH, bL = 2 * z, 2 * z + 1
                        P.mm(banks[bH][:, :], negU, LH, False, True, r=[cbB, LB[n % 3]], w=[bankB[bH]])
                        if not first:
                            P.mm(banks[bH][:, :], negones, Lacc[c_], False, True, r=[cbB, LaccB[c_]], w=[bankB[bH]])
                        P.mm(banks[bL][:, :], negU, LL, False, True, r=[cbB, LB[n % 3]], w=[bankB[bL]])
                        P.mm(banks[bL][:, :], negones, LH, False, True, r=[cbB, LB[n % 3]], w=[bankB[bL]])
                        if not first:
                            P.mm(banks[bL][:, :], negones, Lacc[c_], False, True, r=[cbB, LaccB[c_]], w=[bankB[bL]])
                        if not last:
                            if first:
                                P.v("dve", "tensor_tensor", Lacc[0], LH, LL, ALU.add, r=[LB[n % 3]], w=[LaccB[0]])
                                cur[0] = 0
                            else:
                                P.v("dve", "tensor_tensor", Lacc[1 - c_], Lacc[c_], LH, ALU.add,
                                    r=[LaccB[c_], LB[n % 3]], w=[LaccB[1 - c_]])
                                P.v("dve", "tensor_tensor", Lacc[1 - c_], Lacc[1 - c_], LL, ALU.add,
                                    r=[LaccB[1 - c_], LB[n % 3]], w=[LaccB[1 - c_]])
                                cur[0] = 1 - c_

                    def stB2(n):
                        qt, kbH = steps[n]
                        z = n % 3
                        P.act(Ab[n % 3], psum[:, 2 * z * 512:(2 * z + 2) * 512], AF.Exp,
                              r=[bankB[2 * z], bankB[2 * z + 1]], w=[AB[n % 3]])
                        for j_, kb in enumerate((kbH, kbH - 1)):
                            m = mask_of(kb, qt)
                            if m is not None:
                                hv = Ab[n % 3][:, j_ * 512:(j_ + 1) * 512]
                                P.v("dve", "tensor_tensor", hv, hv, m, ALU.mult, r=[AB[n % 3], cbB], w=[AB[n % 3]])

                    def stC(n):
                        qt, kbH = steps[n]
                        first = (kbH == 4 * qt + 3)
                        last = (kbH == 1)
                        ob = 6 + (qt % 2)
                        P.mm(banks[ob][:, :], vv[:, kbH, :], Ab[n % 3][:, 0:512], first, False,
                             r=[vB[kbH // 4], AB[n % 3]], w=[bankB[ob]])
                        P.mm(banks[ob][:, :], vv[:, kbH - 1, :], Ab[n % 3][:, 512:1024], False, last,
                             r=[vB[(kbH - 1) // 4], AB[n % 3]], w=[bankB[ob]])
                        if last:
                            y2 = qt % 2
                            P.v("dve", "tensor_tensor", yt[y2], banks[ob][:, :], szT[:, qt * 512:(qt + 1) * 512],
                                ALU.mult, r=[bankB[ob], szB[qt]], w=[ytB[y2]])
                            P.dma("sp", ys_d[b, h, :, qt * 512:(qt + 1) * 512], yt[y2], r=[ytB[y2]],
                                  w=[ysB[b][h][qt]])

                    if "noattn" in dbg:
                        NS = -3
                    for n in range(NS + 3):
                        if n < NS:
                            stA1(n)
                        if 0 <= n - 1 < NS:
                            stB(n - 1)
                        if 0 <= n - 2 < NS:
                            stB2(n - 2)
                        if n < NS:
                            stA2(n)
                        if 0 <= n - 3 < NS:
                            stC(n - 3)
            if "ys" in dbg:
                P.barrier()
                P.dma("sp", dbg_t["ys"], ys_d[b], r=[], w=[Buf()])

            if "noB" not in dbg:
                P.barrier()
                ar.reset()
                Wg8 = ar.alloc([8, 8], BF16)
                Wg8B = Buf()
                load_w("pool", Wg8, OFF["mi"], 8, [Wg8B])
                A1 = ar.alloc([S], F32, parts=4)
                A2 = ar.alloc([S], F32, parts=4)
                A3 = ar.alloc([S], F32, parts=4)
                A1B, A2B, A3B = Buf(), Buf(), Buf()
                Pt = ar.alloc([NCH + 1], F32, parts=4)
                PtB = Buf()
                cr = ar.alloc([NCH], F32, parts=4)
                crB = Buf()
                one4 = ar.alloc([2], F32, parts=4)
                one4B = Buf()
                P.v("dve", "memset", one4, 1.0, r=[], w=[one4B])
                for tt in range(NT):
                    sl = slice(tt * 512, (tt + 1) * 512)
                    for kc in range(8):
                        P.mm(banks[5][0:4, :], Wg8[:, kc, 0:4], hT[:, kc, sl], kc == 0, kc == 7,
                             r=[Wg8B, hTB[tt]], w=[bankB[5]])
                    P.act(A1[:, sl], banks[5][0:4, :], AF.Identity, bias=gb[:, 0:1], r=[bankB[5], gbB], w=[A1B])
                    for kc in range(8):
                        P.mm(banks[6][0:4, :], Wg8[:, kc, 4:8], hT[:, kc, sl], kc == 0, kc == 7,
                             r=[Wg8B, hTB[tt]], w=[bankB[6]])
                    P.act(A2[:, sl], banks[6][0:4, :], AF.Exp, scale=-1.0, bias=gb[:, 2:3],
                          r=[bankB[6], gbB], w=[A2B])
                P.act(A2, A2, AF.Ln, bias=one4[:, 0:1], r=[A2B, one4B], w=[A2B])
                P.v("dve", "tensor_tensor_scan", A3, one4[:, 0:1].to_broadcast([4, S]), A2, 0.0,
                    ALU.mult, ALU.subtract, r=[A2B, one4B], w=[A3B])
                P.v("dve", "tensor_tensor", A1, A1, A3, ALU.subtract, r=[A1B, A3B], w=[A1B])
                P.v("dve", "tensor_tensor_scan", A2, A1, A1, 0.0, ALU.max, ALU.max, r=[A1B], w=[A2B])
                P.v("dve", "memset", Pt[:, 0:1], 0.0, r=[], w=[PtB])
                P.v("dve", "tensor_copy", Pt[:, 1:NCH + 1],
                    A2.rearrange("p (c t) -> p c t", t=128)[:, :, 127], r=[A2B], w=[PtB])
                A1v = A1.rearrange("p (c t) -> p c t", t=128)
                A2v = A2.rearrange("p (c t) -> p c t", t=128)
                A3v = A3.rearrange("p (c t) -> p c t", t=128)
                Plo = Pt[:, 0:NCH].unsqueeze(2).to_broadcast([4, NCH, 128])
                Phi = Pt[:, 1:NCH + 1].unsqueeze(2).to_broadcast([4, NCH, 128])
                colps = banks[7]
                for slot in range(3):
                    if slot == 0:
                        P.v("dve", "tensor_tensor", A2v, A1v, Plo, ALU.subtract, r=[A1B, PtB], w=[A2B])
                        P.act(A2, A2, AF.Exp, bias=ml16[0:4, 0:1], r=[A2B, gbB], w=[A2B])
                    elif slot == 1:
                        P.v("dve", "tensor_tensor", A2v, A1v, Phi, ALU.subtract, r=[A1B, PtB], w=[A2B])
                        P.act(A2, A2, AF.Exp, bias=ml16[0:4, 0:1], r=[A2B, gbB], w=[A2B])
                    else:
                        P.v("dve", "tensor_tensor", A2v, A3v, Plo, ALU.add, r=[A3B, PtB], w=[A2B])
                        P.act(A2, A2, AF.Exp, scale=-1.0, r=[A2B], w=[A2B])
                    for c in range(NCH):
                        o0 = slot * NCH * 4 + c * 4
                        P.mm(colps[:, o0:o0 + 4], A2[:, c * 128:(c + 1) * 128], ident4, True, True,
                             r=[A2B, c4B], w=[bankB[7]])
                P.act(colv, colps[:, 0:3 * NCH * 4], AF.Copy, r=[bankB[7]], w=[colvB])
                P.v("dve", "tensor_tensor", cr, Pt[:, 0:NCH], Pt[:, 1:NCH + 1], ALU.subtract, r=[PtB], w=[crB])
                P.act(cr, cr, AF.Exp, r=[crB], w=[crB])
                for h in range(4):
                    P.mm(banks[6][:, h * NCH:(h + 1) * NCH], c4[:, 8 + h * 128:8 + (h + 1) * 128], cr, True, True,
                         r=[c4B, crB], w=[bankB[6]])
                P.act(carry_bc, banks[6][:, 0:4 * NCH], AF.Copy, r=[bankB[6]], w=[carryB])
                if "gates" in dbg:
                    P.dma("sp", dbg_t["colv"], colv, r=[colvB], w=[Buf()])
                    P.dma("sp", dbg_t["carry"], carry_bc, r=[carryB], w=[Buf()])

                P.barrier()
                ar.reset()
                Wm = [[ar.alloc([8, 256], BF16) for _ in range(5)] for _ in range(2)]
                WmB = [[Buf() for _ in range(5)] for _ in range(2)]
                raw = [ar.alloc([516], BF16) for _ in range(4)]
                rawB = [Buf() for _ in range(4)]
                dwc = ar.alloc([16, 128], BF16)
                dwcB = Buf()
                acc = [ar.alloc([512], F32) for _ in range(2)]
                accB = [Buf() for _ in range(2)]
                sgm = [ar.alloc([512], F32) for _ in range(2)]
                sgmB = [Buf() for _ in range(2)]
                qTt = [ar.alloc([2, 512], BF16) for _ in range(2)]
                kTt = [ar.alloc([2, 512], BF16) for _ in range(2)]
                qTtB = [Buf() for _ in range(2)]
                kTtB = [Buf() for _ in range(2)]
                vaug = [ar.alloc([4, 264], BF16) for _ in range(2)]
                vaugB = [Buf() for _ in range(2)]
                ogt = [ar.alloc([4, 256], BF16) for _ in range(2)]
                ogtB = [Buf() for _ in range(2)]
                sgo = [ar.alloc([512], F32) for _ in range(2)]
                sgoB = [Buf() for _ in range(2)]
                t1 = [ar.alloc([256], F32) for _ in range(2)]
                t1B = [Buf() for _ in range(2)]
                Cf = ar.alloc([2, 257], F32)
                CfB = Buf()
                Cbf = ar.alloc([2, 258], BF16)
                CbfB = Buf()
                scT = [ar.alloc([128], BF16) for _ in range(2)]
                scTB = [Buf() for _ in range(2)]
                kwt = [ar.alloc([256], BF16) for _ in range(2)]
                kwtB = [Buf() for _ in range(2)]
                ymt = [ar.alloc([256], BF16) for _ in range(2)]
                ymtB = [Buf() for _ in range(2)]
                ymT = [ar.alloc([2, 512], BF16) for _ in range(2)]
                ymTB = [Buf() for _ in range(2)]
                sc = [ar.alloc([8], F32) for _ in range(2)]
                scB = [Buf() for _ in range(2)]
                junk2 = ar.alloc([256], BF16)
                junk2B = Buf()
                for k_ in range(2):
                    P.v("dve", "memset", vaug[k_][:, :, 256:257], 1.0, r=[], w=[vaugB[k_]])
                MSEG = ["mq", "mk", "mv", "mo", "mz"]
                brow = ar.alloc([3072], BF16, parts=1)
                browB = Buf("brow")
                for i_, key in enumerate(["mv", "mo", "mz"]):
                    P.dma("pool", brow[:, i_ * 1024:(i_ + 1) * 1024],
                          b_in[OFF[key]:OFF[key] + 1024].unsqueeze(0), r=[], w=[browB])

                def loadB(h):
                    for i_, key in enumerate(MSEG):
                        load_w("pool", Wm[h % 2][i_], OFF[key] + h * 256, 256, [WmB[h % 2][i_]])

                loadB(0)
                pcnt = [0]
                NHB = 1 if "h1" in dbg else 4
                tpm = banks[3][:, :].bitcast(BF16).rearrange("p (a b) -> p a b", b=128)
                tpk = banks[0][:, 256:512].bitcast(BF16).rearrange("p (a b) -> p a b", b=128)
                for h in range(NHB):
                    if h + 1 < NHB:
                        loadB(h + 1)
                    Wq_, Wk_, Wv_, Wo_, Wz_ = Wm[h % 2]
                    WqB_, WkB_, WvB_, WoB_, WzB_ = WmB[h % 2]
                    P.v("dve", "memset", Cf, 0.0, r=[], w=[CfB])
                    for i_ in range(4):
                        P.v("dve", "memset", raw[i_][:, 0:3], 0.0, r=[], w=[rawB[i_]])
                    for qk_ in range(2):
                        for dc_ in range(2):
                            ci_ = qk_ * 8 + h * 2 + dc_
                            for j_ in range(4):
                                P.v("dve", "tensor_scalar", dwc[:, (qk_ * 2 + dc_) * 4 + j_, :], ident,
                                    mcols[:, j_ * 16 + ci_:j_ * 16 + ci_ + 1], None, ALU.mult,
                                    r=[cbB, mcolsB], w=[dwcB])

                    def proj_pieces(tt, h=h, Wq_=Wq_, Wk_=Wk_, Wv_=Wv_, Wo_=Wo_, Wz_=Wz_,
                                    WqB_=WqB_, WkB_=WkB_, WvB_=WvB_, WoB_=WoB_, WzB_=WzB_):
                        k2 = tt % 2
                        sl = slice(tt * 512, (tt + 1) * 512)
                        pieces = []

                        def qk_piece(qk, dc):
                            ci = qk * 8 + h * 2 + dc
                            ri = qk * 2 + dc
                            Wx, WxB = (Wq_, WqB_) if qk == 0 else (Wk_, WkB_)
                            a2 = pcnt[0] % 2
                            pcnt[0] += 2
                            for kc in range(8):
                                P.mm(banks[6][:, :], Wx[:, kc, dc * 128:(dc + 1) * 128], hT[:, kc, sl],
                                     kc == 0, kc == 7, r=[WxB, hTB[tt]], w=[bankB[6]])
                            if tt > 0:
                                P.v("dve", "tensor_copy", raw[ri][:, 0:3], raw[ri][:, 512:515],
                                    r=[rawB[ri]], w=[rawB[ri]])
                            P.act(raw[ri][:, 3:515], banks[6][:, :], AF.Identity, bias=mcols[:, 80 + ci:81 + ci],
                                  r=[bankB[6], mcolsB], w=[rawB[ri]])
                            for j in range(4):
                                P.mm(banks[7][:, :], dwc[:, ri * 4 + j, :], raw[ri][:, j:j + 512], j == 0, j == 3,
                                     r=[dwcB, rawB[ri]], w=[bankB[7]])
                            P.act(sgm[a2], banks[7][:, :], AF.Sigmoid, bias=mcols[:, 64 + ci:65 + ci],
                                  r=[bankB[7], mcolsB], w=[sgmB[a2]])
                            dst, dstB = (qTt, qTtB) if qk == 0 else (kTt, kTtB)
                            P.v("dve", "scalar_tensor_tensor", dst[k2][:, dc, :], banks[7][:, :],
                                mcols[:, 64 + ci:65 + ci], sgm[a2], ALU.add, ALU.mult,
                                r=[bankB[7], sgmB[a2], mcolsB], w=[dstB[k2]])

                        def v_piece(g2):
                            pb = 6 + (pcnt[0] % 2)
                            pcnt[0] += 1
                            for c2 in range(2):
                                c4_ = g2 * 2 + c2
                                j = tt * 4 + c4_
                                o_ = banks[pb][:, c2 * 256:(c2 + 1) * 256]
                                for kc in range(8):
                                    P.mm(o_, hT[:, kc, j * 128:(j + 1) * 128], Wv_[:, kc, :], kc == 0, False,
                                         r=[WvB_, hTB[tt]], w=[bankB[pb]])
                                P.mm(o_, ones_b[0:1, 0:128], brow[0:1, h * 256:(h + 1) * 256], False, True,
                                     r=[cbB, browB], w=[bankB[pb]])
                            P.act(vaug[k2][:, g2 * 2:g2 * 2 + 2, 0:256],
                                  banks[pb][:, :].rearrange("p (a b) -> p a b", b=256), AF.Copy,
                                  r=[bankB[pb]], w=[vaugB[k2]])

                        def og_piece(c4_):
                            j = tt * 4 + c4_
                            pb = 6 + (pcnt[0] % 2)
                            a2 = pcnt[0] % 2
                            pcnt[0] += 1
                            for half, (Wx, WxB, boff) in enumerate(((Wo_, WoB_, 1024), (Wz_, WzB_, 2048))):
                                o_ = banks[pb][:, half * 256:(half + 1) * 256]
                                for kc in range(8):
                                    P.mm(o_, hT[:, kc, j * 128:(j + 1) * 128], Wx[:, kc, :], kc == 0, False,
                                         r=[WxB, hTB[tt]], w=[bankB[pb]])
                                P.mm(o_, ones_b[0:1, 0:128], brow[0:1, boff + h * 256:boff + (h + 1) * 256],
                                     False, True, r=[cbB, browB], w=[bankB[pb]])
                            P.act(sgo[a2], banks[pb][:, :], AF.Sigmoid, r=[bankB[pb]], w=[sgoB[a2]])
                            P.v("dve", "tensor_tensor", t1[a2], sgo[a2][:, 0:256], sgo[a2][:, 256:512], ALU.mult,
                                r=[sgoB[a2]], w=[t1B[a2]])
                            P.v("dve", "tensor_tensor", ogt[k2][:, c4_, :], t1[a2], banks[pb][:, 256:512], ALU.mult,
                                r=[t1B[a2], bankB[pb]], w=[ogtB[k2]])

                        for qk in range(2):
                            for dc in range(2):
                                pieces.append(lambda qk=qk, dc=dc: qk_piece(qk, dc))
                        for g2 in range(2):
                            pieces.append(lambda g2=g2: v_piece(g2))
                        for c4_ in range(4):
                            pieces.append(lambda c4_=c4_: og_piece(c4_))
                        return pieces

                    def chunk_parts(g, h=h):
                        tt, c4_ = divmod(g, 4)
                        c = g
                        k2 = tt % 2
                        bl = slice(c4_ * 128, (c4_ + 1) * 128)
                        s2 = g % 2
                        hb = 1 + (g % 2)
                        es = colv[:, c * 4 + h:c * 4 + h + 1]
                        es2 = colv[:, NCH * 4 + c * 4 + h:NCH * 4 + c * 4 + h + 1]
                        dnm = colv[:, 2 * NCH * 4 + c * 4 + h:2 * NCH * 4 + c * 4 + h + 1]
                        car = carry_bc[:, h * NCH + c:h * NCH + c + 1]
                        upd = c < NCH - 1

                        def early():
                            for dc in range(2):
                                P.mm(banks[0][:, 0:128], kTt[k2][:, dc, bl], qTt[k2][:, dc, bl], dc == 0, dc == 1,
                                     r=[kTtB[k2], qTtB[k2]], w=[bankB[0]])
                            P.v("dve", "scalar_tensor_tensor", scT[s2], banks[0][:, 0:128], es, trimask,
                                ALU.mult, ALU.mult, r=[bankB[0], colvB, cbB], w=[scTB[s2]])
                            if upd:
                                for dc in range(2):
                                    P.tr(tpk[:, dc, :], kTt[k2][:, dc, bl], ident, r=[kTtB[k2], cbB], w=[bankB[0]])
                                P.act(kwt[s2].rearrange("p (a b) -> p a b", b=128), tpk[:, 0:2, :], AF.Copy,
                                      scale=es2, r=[bankB[0], colvB], w=[kwtB[s2]])
                            P.mm(banks[hb][:, 0:257], scT[s2], vaug[k2][:, c4_, 0:257], True, c == 0,
                                 r=[scTB[s2], vaugB[k2]], w=[bankB[hb]])
                            if c > 0:
                                for dc in range(2):
                                    P.mm(banks[hb][:, 0:257], qTt[k2][:, dc, bl], Cbf[:, dc, 0:257], False, dc == 1,
                                         r=[qTtB[k2], CbfB], w=[bankB[hb]])
                            if upd:
                                for dc in range(2):
                                    P.mm(banks[4 + dc][:, 0:257], kwt[s2][:, dc * 128:(dc + 1) * 128],
                                         vaug[k2][:, c4_, 0:257], True, True, r=[kwtB[s2], vaugB[k2]],
                                         w=[bankB[4 + dc]])
                                    P.v("dve", "scalar_tensor_tensor", Cf[:, dc, :], Cf[:, dc, :], car,
                                        banks[4 + dc][:, 0:257], ALU.mult, ALU.add,
                                        r=[CfB, carryB, bankB[4 + dc]], w=[CfB])
                                P.act(Cbf[:, :, 0:257], Cf, AF.Copy, r=[CfB], w=[CbfB])

                        def epi1():
                            P.act(junk2, banks[hb][:, 0:256], AF.Square, r=[bankB[hb]], w=[junk2B, scB[s2]],
                                  accum_out=sc[s2][:, 0:1])
                            P.act(sc[s2][:, 6:7], banks[hb][:, 256:257], AF.Abs, r=[bankB[hb]], w=[scB[s2]])
                            P.v("dve", "tensor_scalar", sc[s2][:, 1:2], sc[s2][:, 6:7], dnm, None,
                                ALU.max, r=[scB[s2], colvB], w=[scB[s2]])
                            P.v("dve", "reciprocal", sc[s2][:, 2:3], sc[s2][:, 1:2], r=[scB[s2]], w=[scB[s2]])
                            P.v("dve", "scalar_tensor_tensor", sc[s2][:, 3:4], sc[s2][:, 0:1], sc[s2][:, 2:3],
                                sc[s2][:, 2:3], ALU.mult, ALU.mult, r=[scB[s2]], w=[scB[s2]])
                            P.v("dve", "tensor_scalar", sc[s2][:, 3:4], sc[s2][:, 3:4], 1.0 / 256.0, EPS,
                                ALU.mult, ALU.add, r=[scB[s2]], w=[scB[s2]])
                            P.v("pool", "tensor_tensor", sc[s2][:, 4:5], sc[s2][:, 3:4], mhalf[:, 0:1], ALU.pow,
                                r=[scB[s2], mhB], w=[scB[s2]])

                        def epi2():
                            P.v("dve", "tensor_tensor", sc[s2][:, 5:6], sc[s2][:, 4:5], sc[s2][:, 2:3], ALU.mult,
                                r=[scB[s2]], w=[scB[s2]])
                            P.v("dve", "scalar_tensor_tensor", ymt[s2], banks[hb][:, 0:256], sc[s2][:, 5:6],
                                ogt[k2][:, c4_, :], ALU.mult, ALU.mult, r=[bankB[hb], scB[s2], ogtB[k2]],
                                w=[ymtB[s2]])

                        def fin():
                            for ec in range(2):
                                P.tr(tpm[:, ec, :], ymt[s2][:, ec * 128:(ec + 1) * 128], ident,
                                     r=[ymtB[s2], cbB], w=[bankB[3]])
                            P.act(ymT[k2][:, :, bl], tpm[:, 0:2, :], AF.Copy, r=[bankB[3]], w=[ymTB[k2]])
                            if c4_ == 3:
                                sl = slice(tt * 512, (tt + 1) * 512)
                                P.dma("sp", ym_d[b, 2 * h:2 * h + 2, :, sl].rearrange("c p t -> p c t"), ymT[k2],
                                      r=[ymTB[k2]], w=[ymB[b][2 * h][tt], ymB[b][2 * h + 1][tt]])

                        return early, epi1, epi2, fin

                    for pc in proj_pieces(0):
                        pc()
                    parts = {}
                    nxt = []
                    for g in range(NCH + 2):
                        if g < NCH:
                            tt, c4_ = divmod(g, 4)
                            if c4_ == 0:
                                nxt = proj_pieces(tt + 1) if tt + 1 < NT else []
                            lo = (len(nxt) * c4_) // 4
                            hi = (len(nxt) * (c4_ + 1)) // 4
                            for pc in nxt[lo:hi]:
                                pc()
                            parts[g] = chunk_parts(g)
                            parts[g][0]()
                        if 0 <= g - 1 < NCH:
                            parts[g - 1][2]()
                        if g < NCH:
                            parts[g][1]()
                        if 0 <= g - 2 < NCH:
                            parts[g - 2][3]()
            if "ym" in dbg:
                P.barrier()
                P.dma("sp", dbg_t["ym"], ym_d[b], r=[], w=[Buf()])

            if "noC" not in dbg:
                P.barrier()
                ar.reset()
                TC = 256
                NTC = S // TC
                Wpm_t = ar.alloc([8, 1024], BF16)
                Wps_t = ar.alloc([8, 1024], BF16)
                Wo_t = ar.alloc([8, 1024], BF16)
                Wg_t = ar.alloc([8, 2048], BF16)
                WpmB, WpsB, WoB, WgB = Buf(), Buf(), Buf(), Buf()
                mnwc = ar.alloc([8], F32)
                bgc = ar.alloc([16], F32)
                smB = Buf()
                ymt_c = [ar.alloc([8, TC], BF16) for _ in range(2)]
                yst_c = [ar.alloc([8, TC], BF16) for _ in range(2)]
                ymcB = [Buf() for _ in range(2)]
                yscB = [Buf() for _ in range(2)]
                sga = [ar.alloc([TC], F32) for _ in range(2)]
                sgb = [ar.alloc([TC], F32) for _ in range(2)]
                m1 = [ar.alloc([TC], F32) for _ in range(2)]
                m2 = [ar.alloc([TC], F32) for _ in range(2)]
                sgaB = [Buf() for _ in range(2)]
                sgbB = [Buf() for _ in range(2)]
                m1B = [Buf() for _ in range(2)]
                m2B = [Buf() for _ in range(2)]
                mrg = [ar.alloc([8, TC], BF16) for _ in range(2)]
                mrgB = [Buf() for _ in range(2)]
                xt = [ar.alloc([1024], F32) for _ in range(2)]
                xtB = [Buf() for _ in range(2)]
                for wt_, wb_, src in ((Wpm_t, WpmB, w_pm), (Wps_t, WpsB, w_ps), (Wo_t, WoB, w_out)):
                    s3 = src.rearrange("(kc p) n -> p kc n", p=128)
                    for hf in range(2):
                        P.dma("pool", wt_[:, :, hf * 512:(hf + 1) * 512], s3[:, :, hf * 512:(hf + 1) * 512],
                              r=[], w=[wb_])
                for hf in range(4):
                    P.dma("pool", Wg_t[:, :, hf * 512:(hf + 1) * 512],
                          w3[:, :, OFF["g"] + hf * 512:OFF["g"] + (hf + 1) * 512], r=[], w=[WgB])
                P.dma("sp", mnwc, mnw.rearrange("(c p) -> p c", p=128), r=[], w=[smB], slow=True)
                P.dma("sp", bgc, b_in[OFF["g"]:OFF["g"] + 2048].rearrange("(c p) -> p c", p=128), r=[], w=[smB],
                      slow=True)
                for kc in range(8):
                    P.v("dve", "tensor_scalar", Wpm_t[:, kc, :], Wpm_t[:, kc, :], mnwc[:, kc:kc + 1], None, ALU.mult,
                        r=[WpmB, smB], w=[WpmB])
                ymv = ym_d[b].rearrange("c p t -> p c t")
                ysv = ys_d[b].rearrange("c p t -> p c t")
                gcnt = 0
                for tc in range(NTC):
                    k2 = tc % 2
                    sl = slice(tc * TC, (tc + 1) * TC)
                    t512 = (tc * TC) // 512
                    P.dma("sp", ymt_c[k2], ymv[:, :, sl], r=[ymB[b][f_][t512] for f_ in range(8)], w=[ymcB[k2]])
                    P.dma("sp", yst_c[k2], ysv[:, :, sl], r=[ysB[b][f_][t512] for f_ in range(8)], w=[yscB[k2]])
                    for cc in range(8):
                        g2 = gcnt % 2
                        gcnt += 1
                        cs = slice(cc * 128, (cc + 1) * 128)
                        cs2 = slice(1024 + cc * 128, 1024 + (cc + 1) * 128)
                        bPm, bPs, bgm, bgs = banks[0 + g2], banks[2 + g2], banks[4 + g2], banks[6 + g2]
                        BPm, BPs, Bgm, Bgs = bankB[0 + g2], bankB[2 + g2], bankB[4 + g2], bankB[6 + g2]
                        for kc in range(8):
                            P.mm(bgm[:, 0:TC], Wg_t[:, kc, cs], hT[:, kc, sl], kc == 0, kc == 7,
                                 r=[WgB, hTB[t512]], w=[Bgm])
                        for kc in range(8):
                            P.mm(bgs[:, 0:TC], Wg_t[:, kc, cs2], hT[:, kc, sl], kc == 0, kc == 7,
                                 r=[WgB, hTB[t512]], w=[Bgs])
                        for kc in range(8):
                            P.mm(bPm[:, 0:TC], Wpm_t[:, kc, cs], ymt_c[k2][:, kc, :], kc == 0, kc == 7,
                                 r=[WpmB, ymcB[k2]], w=[BPm])
                        for kc in range(8):
                            P.mm(bPs[:, 0:TC], Wps_t[:, kc, cs], yst_c[k2][:, kc, :], kc == 0, kc == 7,
                                 r=[WpsB, yscB[k2]], w=[BPs])
                        P.act(sga[g2], bgm[:, 0:TC], AF.Sigmoid, bias=bgc[:, cc:cc + 1], r=[Bgm, smB], w=[sgaB[g2]])
                        P.act(sgb[g2], bgs[:, 0:TC], AF.Sigmoid, bias=bgc[:, 8 + cc:9 + cc], r=[Bgs, smB],
                              w=[sgbB[g2]])
                        P.v("dve", "tensor_tensor", m1[g2], bPm[:, 0:TC], sga[g2], ALU.mult, r=[BPm, sgaB[g2]],
                            w=[m1B[g2]])
                        P.v("dve", "tensor_tensor", m2[g2], bPs[:, 0:TC], sgb[g2], ALU.mult, r=[BPs, sgbB[g2]],
                            w=[m2B[g2]])
                        P.v("pool", "tensor_tensor", mrg[k2][:, cc, :], m1[g2], m2[g2], ALU.add,
                            r=[m1B[g2], m2B[g2]], w=[mrgB[k2]])
                    for sub in range(TC // 128):
                        row0 = tc * TC + sub * 128
                        x2 = (tc * (TC // 128) + sub) % 2
                        P.dma("sp", xt[x2], x[b, row0:row0 + 128, :], r=[], w=[xtB[x2]])
                        for eh in range(2):
                            g2 = gcnt % 2
                            gcnt += 1
                            bo, BO = banks[0 + g2], bankB[0 + g2]
                            for cc in range(8):
                                P.mm(bo[:, :], mrg[k2][:, cc, sub * 128:(sub + 1) * 128],
                                     Wo_t[:, cc, eh * 512:(eh + 1) * 512], cc == 0, cc == 7,
                                     r=[mrgB[k2], WoB], w=[BO])
                            P.v("dve", "tensor_tensor", xt[x2][:, eh * 512:(eh + 1) * 512], bo[:, :],
                                xt[x2][:, eh * 512:(eh + 1) * 512], ALU.add, r=[BO, xtB[x2]], w=[xtB[x2]])
                        P.dma("sp", out[b, row0:row0 + 128, :], xt[x2], r=[xtB[x2]], w=[Buf()])

        P.emit()
    return nc


_NC_CACHE = {}


def kernel(x, norm_w, w_in, b_in, conv_w, conv_b, mlstm_norm_w, sb_q_norm_w, sb_k_norm_w,
           w_proj_m, w_proj_s, w_out):
    n = 8
    B, S, _ = x.shape
    nseq = B // n
    nc = build(S, nseq)
    c, c4 = _consts()
    f = lambda a: np.ascontiguousarray(np.asarray(a, dtype=np.float32))
    shared = dict(norm_w=f(norm_w), w_in=f(w_in), b_in=f(b_in), conv_w=f(conv_w), conv_b=f(conv_b),
                  mlstm_norm_w=f(mlstm_norm_w), sb_q_norm_w=f(sb_q_norm_w), sb_k_norm_w=f(sb_k_norm_w),
                  w_proj_m=f(w_proj_m), w_proj_s=f(w_proj_s), w_out=f(w_out), cst=c, cst4=c4)
    xs = f(x)
    in_maps = [dict(shared, x=xs[i * nseq:(i + 1) * nseq]) for i in range(n)]
    res = run_bass_kernel_spmd(nc, in_maps, core_ids=list(range(n)))
    return np.concatenate([r["out"] for r in res.results], axis=0)
```

```python
import math
import numpy as np
import concourse.bass as bass
import concourse.mybir as mybir
from concourse.bass_utils import run_bass_kernel_spmd

F32 = mybir.dt.float32
BF16 = mybir.dt.bfloat16
AF = mybir.ActivationFunctionType
ALU = mybir.AluOpType
AX = mybir.AxisListType

D = 1024
IN_COLS = 11272
EPS = 1e-6
OFF = dict(mq=0, mk=1024, mv=2048, mi=3072, mf=3076, mo=3080, mz=4104,
           sq=5128, sk=6152, sv=7176, sz=8200, g=9224)
LN16 = math.log(16.0)


class Buf:
    __slots__ = ("name", "w", "r", "excl")

    def __init__(self, name="", excl=False):
        self.name = name
        self.w = None
        self.r = {}
        self.excl = excl


class Op:
    __slots__ = ("eng", "tl", "idx", "fn", "waits", "signal", "clock", "dma", "semval")


ENGS = ["pe", "act", "dve", "pool", "sp"]


class Prog:
    def __init__(self, nc, ndma=8):
        self.nc = nc
        self.ops = {e: [] for e in ENGS}
        self.tl_ops = {}
        self.known = {e: {} for e in ENGS}
        self.ndma = ndma
        self.dma_count = {e: 0 for e in ENGS}
        self.bar_ops = []
        self.bar_gen = 0
        self.eng_gen = {e: 0 for e in ENGS}

    def barrier(self):
        self.bar_ops = [lst[-1] for lst in self.tl_ops.values() if lst]
        self.bar_gen += 1

    def add(self, eng, fn, r=(), w=(), dma=False):
        op = Op()
        op.eng = eng
        op.fn = fn
        op.dma = dma
        op.signal = dma
        deps = []
        for b in r:
            if b.w is not None:
                deps.append((b.w, True))
            if b.excl:
                for o in b.r.values():
                    deps.append((o, False))
        for b in w:
            if b.w is not None:
                deps.append((b.w, False))
            for o in b.r.values():
                deps.append((o, False))
        if self.eng_gen[eng] < self.bar_gen:
            self.eng_gen[eng] = self.bar_gen
            for o in self.bar_ops:
                deps.append((o, True))
        if dma:
            j = self.dma_count[eng]
            self.dma_count[eng] += 1
            tl = (eng, j % self.ndma)
            prev = self.tl_ops.get(tl)
            if prev:
                deps.append((prev[-1], True))
        else:
            tl = eng
        lst = self.tl_ops.setdefault(tl, [])
        op.tl = tl
        op.idx = len(lst)
        kn = self.known[eng]
        best = {}
        for d, raw in deps:
            if d.tl == eng:
                if eng == "pe" or not raw:
                    continue
            if kn.get(d.tl, -1) >= d.idx:
                continue
            if d.tl not in best or best[d.tl].idx < d.idx:
                best[d.tl] = d
        waits = []
        for d in best.values():
            if kn.get(d.tl, -1) >= d.idx:
                continue
            waits.append(d)
            d.signal = True
            for t, i in d.clock.items():
                if kn.get(t, -1) < i:
                    kn[t] = i
            if kn.get(d.tl, -1) < d.idx:
                kn[d.tl] = d.idx
        op.waits = waits
        op.clock = dict(kn)
        lst.append(op)
        self.ops[eng].append(op)
        for b in r:
            b.r[tl] = op
        for b in w:
            b.w = op
            b.r = {}
        return op

    def mm(self, out, lhsT, rhs, start, stop, r, w):
        return self.add("pe", lambda e: e.matmul(out, lhsT, rhs, start=start, stop=stop,
                                                 skip_group_check=True), r, w)

    def tr(self, out, in_, ident, r, w):
        return self.add("pe", lambda e: e.transpose(out, in_, ident), r, w)

    def act(self, out, in_, func, r, w, bias=None, scale=None, accum_out=None):
        kw = {}
        if bias is not None:
            kw["bias"] = bias
        if scale is not None:
            kw["scale"] = scale
        if accum_out is not None:
            kw["accum_out"] = accum_out
        return self.add("act", lambda e: e.activation(out, in_, func, **kw), r, w)

    def v(self, eng, name, *args, r=(), w=(), **kw):
        return self.add(eng, lambda e: getattr(e, name)(*args, **kw), r, w)

    def dma(self, eng, out, in_, r, w, slow=False):
        if slow:
            return self.add(eng, lambda e: e.dma_start(out=out, in_=in_, allow_slow_non_contiguous=True),
                            r, w, dma=True)
        return self.add(eng, lambda e: e.dma_start(out=out, in_=in_), r, w, dma=True)

    def emit(self):
        nc = self.nc
        for tl, lst in self.tl_ops.items():
            if isinstance(tl, tuple):
                for o in lst:
                    o.semval = 16 * (o.idx + 1)
            else:
                c = 0
                for o in lst:
                    if o.signal:
                        c += 1
                    o.semval = c
        sems = {}
        import contextlib
        with contextlib.ExitStack() as st:
            for tl in self.tl_ops:
                nm = tl if isinstance(tl, str) else "%s_d%d" % tl
                sems[tl] = st.enter_context(nc.semaphore("s_" + nm))
            block = st.enter_context(nc.Block())
            handles = {"pe": block.tensor, "act": block.scalar, "dve": block.vector,
                       "pool": block.gpsimd, "sp": block.sync}

            def make(engname):
                def body(e):
                    for o in self.ops[engname]:
                        for d in o.waits:
                            e.wait_ge(sems[d.tl], d.semval)
                        ins = o.fn(e)
                        if o.signal:
                            ins.then_inc(sems[o.tl], 16 if o.dma else 1)
                    for tl, lst in self.tl_ops.items():
                        if isinstance(tl, tuple) and tl[0] == engname and lst:
                            e.wait_ge(sems[tl], lst[-1].semval)
                return body

            for en in ENGS:
                if self.ops[en]:
                    handles[en](make(en))


def _consts():
    c = np.zeros((128, 1536), np.float32)
    j = np.arange(128)[:, None]
    s = np.arange(128)[None, :]
    c[:, 0:128] = (j == s)
    c[:, 128:256] = -(j >= s).astype(np.float32)
    c[:, 256:384] = -1.0
    c[:, 384:512] = 1.0
    c[:, 512:640] = (j <= s)
    cc = np.arange(896)[None, :]
    c[:, 640:1536] = ((cc - 384) > j)
    c4 = np.zeros((4, 520), np.float32)
    c4[:, 0:4] = np.eye(4)
    for h in range(4):
        c4[h, 8 + h * 128: 8 + (h + 1) * 128] = 1.0
    return c, c4


def build(S, NSEQ, dbg=None):
    dbg = dbg or set()
    nc = bass.Bass("TRN2", target_bir_lowering=False)
    NT = S // 512
    NCH = S // 128
    dt = nc.dram_tensor
    x = dt("x", [NSEQ, S, D], F32, kind="ExternalInput").ap()
    norm_w = dt("norm_w", [D], F32, kind="ExternalInput").ap()
    w_in = dt("w_in", [D, IN_COLS], F32, kind="ExternalInput").ap()
    b_in = dt("b_in", [IN_COLS], F32, kind="ExternalInput").ap()
    conv_w = dt("conv_w", [4, 2048], F32, kind="ExternalInput").ap()
    conv_b = dt("conv_b", [2048], F32, kind="ExternalInput").ap()
    mnw = dt("mlstm_norm_w", [1024], F32, kind="ExternalInput").ap()
    sqw = dt("sb_q_norm_w", [128], F32, kind="ExternalInput").ap()
    skw = dt("sb_k_norm_w", [128], F32, kind="ExternalInput").ap()
    w_pm = dt("w_proj_m", [D, D], F32, kind="ExternalInput").ap()
    w_ps = dt("w_proj_s", [D, D], F32, kind="ExternalInput").ap()
    w_out = dt("w_out", [D, D], F32, kind="ExternalInput").ap()
    cst = dt("cst", [128, 1536], F32, kind="ExternalInput").ap()
    cst4 = dt("cst4", [4, 520], F32, kind="ExternalInput").ap()
    out = dt("out", [NSEQ, S, D], F32, kind="ExternalOutput").ap()
    ys_d = dt("ys_scr", [NSEQ, 8, 128, S], BF16, kind="Internal").ap()
    ym_d = dt("ym_scr", [NSEQ, 8, 128, S], BF16, kind="Internal").ap()
    dbg_t = {}
    if "hT" in dbg:
        dbg_t["hT"] = dt("dbg_hT", [128, 8, S], BF16, kind="ExternalOutput").ap()
    if "ys" in dbg:
        dbg_t["ys"] = dt("dbg_ys", [8, 128, S], BF16, kind="ExternalOutput").ap()
    if "ym" in dbg:
        dbg_t["ym"] = dt("dbg_ym", [8, 128, S], BF16, kind="ExternalOutput").ap()
    if "gates" in dbg:
        dbg_t["colv"] = dt("dbg_colv", [128, 3 * NCH * 4], F32, kind="ExternalOutput").ap()
        dbg_t["carry"] = dt("dbg_carry", [128, 4 * NCH], F32, kind="ExternalOutput").ap()

    w3 = w_in.rearrange("(kc p) n -> p kc n", p=128)

    import contextlib
    with contextlib.ExitStack() as st:
        TOTAL = 104000
        big = st.enter_context(nc.sbuf_tensor("big", [128, TOTAL], BF16))
        psum = st.enter_context(nc.psum_tensor("psum_all", [128, 4096], F32))
        banks = [psum[:, i * 512:(i + 1) * 512] for i in range(8)]
        bankB = [Buf("bank%d" % i, excl=True) for i in range(8)]

        class Arena:
            def __init__(self, lo, hi):
                self.lo, self.hi, self.p = lo, hi, lo

            def reset(self):
                self.p = self.lo

            def alloc(self, shape, dtype, parts=128):
                n = 1
                for d_ in shape:
                    n *= d_
                nb = n * (2 if dtype == BF16 else 4)
                nb = (nb + 63) // 64 * 64
                ne = nb // 2
                assert self.p + ne <= self.hi, ("arena overflow", self.p, ne, self.hi)
                v = big[0:parts, self.p:self.p + ne]
                self.p += ne
                if dtype != BF16:
                    v = v.bitcast(dtype)
                v = v[:, 0:n]
                if len(shape) == 2:
                    v = v.rearrange("p (a b) -> p a b", b=shape[1])
                elif len(shape) == 3:
                    v = v.rearrange("p (a b c) -> p a b c", b=shape[1], c=shape[2])
                return v

        pers = Arena(0, 40000)
        ar = Arena(40000, TOTAL)

        P = Prog(nc)

        hT = pers.alloc([8, S], BF16)
        hTB = [Buf("hT%d" % i) for i in range(NT)]
        cb = pers.alloc([1536], BF16)
        cbB = Buf("cb")
        c4 = pers.alloc([520], F32, parts=4)
        c4B = Buf("c4")
        normw_bc = pers.alloc([1024], F32)
        nwB = Buf("normw")
        cols = pers.alloc([64], F32)
        colsB = Buf("cols")
        colv = pers.alloc([3 * NCH * 4], F32)
        colvB = Buf("colv")
        carry_bc = pers.alloc([4 * NCH], F32)
        carryB = Buf("carry")
        mcols = pers.alloc([16 * 6 + 8], F32)
        mcolsB = Buf("mcols")
        gb = pers.alloc([4], F32, parts=4)
        gbB = Buf("gb")

        mhalf = pers.alloc([2], F32)
        mhB = Buf("mhalf")
        P.v("pool", "memset", mhalf, -0.5, r=[], w=[mhB])
        epsq = pers.alloc([2], F32)
        P.v("pool", "memset", epsq, 128.0 * EPS, r=[], w=[mhB])
        onec = pers.alloc([2], F32)
        P.v("pool", "memset", onec, 1.0, r=[], w=[mhB])
        ml16 = pers.alloc([2], F32)
        P.v("pool", "memset", ml16, -LN16, r=[], w=[gbB])
        ysB = [[[Buf() for _ in range(NT)] for _ in range(8)] for _ in range(NSEQ)]
        ymB = [[[Buf() for _ in range(NT)] for _ in range(8)] for _ in range(NSEQ)]
        ident = cb[:, 0:128]
        negU = cb[:, 128:256]
        negones = cb[:, 256:384]
        ones_b = cb[:, 384:512]
        trimask = cb[:, 512:640]
        sbmask = cb[:, 640:1536]
        ident4 = c4[:, 0:4]

        P.dma("pool", cb, cst, r=[], w=[cbB])
        P.dma("sp", c4, cst4, r=[], w=[c4B])
        P.dma("sp", normw_bc, norm_w.partition_broadcast(128), r=[], w=[nwB])
        with nc.allow_non_contiguous_dma(reason="tiny one-time bias/gain column loads"):
            for i, key in enumerate(["sq", "sk", "sz"]):
                P.dma("sp", cols[:, 8 * i:8 * i + 8],
                      b_in[OFF[key]:OFF[key] + 1024].rearrange("(h p) -> p h", p=128), r=[], w=[colsB], slow=True)
            P.dma("sp", cols[:, 24:25], sqw.unsqueeze(1), r=[], w=[colsB], slow=True)
            P.dma("sp", cols[:, 25:26], skw.unsqueeze(1), r=[], w=[colsB], slow=True)
            for j in range(4):
                P.dma("sp", mcols[:, j * 16:(j + 1) * 16], conv_w[j].rearrange("(c p) -> p c", p=128),
                      r=[], w=[mcolsB], slow=True)
            P.dma("sp", mcols[:, 64:80], conv_b.rearrange("(c p) -> p c", p=128), r=[], w=[mcolsB], slow=True)
            P.dma("sp", mcols[:, 80:96], b_in[0:2048].rearrange("(c p) -> p c", p=128), r=[], w=[mcolsB], slow=True)
            P.dma("sp", gb[:, 0:1], b_in[OFF["mi"]:OFF["mi"] + 4].unsqueeze(1), r=[], w=[gbB], slow=True)
            P.dma("sp", gb[:, 1:2], b_in[OFF["mf"]:OFF["mf"] + 4].unsqueeze(1), r=[], w=[gbB], slow=True)
        P.v("dve", "tensor_scalar", cols[:, 25:26], cols[:, 25:26], math.sqrt(128.0), None, ALU.mult,
            r=[colsB], w=[colsB])
        P.v("dve", "tensor_scalar", cols[:, 32:40], cols[:, 0:8], cols[:, 24:25], None, ALU.mult,
            r=[colsB], w=[colsB])
        P.v("dve", "tensor_scalar", cols[:, 40:48], cols[:, 8:16], cols[:, 25:26], None, ALU.mult,
            r=[colsB], w=[colsB])
        P.v("dve", "tensor_scalar", gb[:, 2:3], gb[:, 1:2], -1.0, None, ALU.mult, r=[gbB], w=[gbB])

        def load_w(eng, tile, col0, ncols, bufs):
            return P.dma(eng, tile, w3[:, :, col0:col0 + ncols], r=[], w=bufs)

        for b in range(NSEQ):
            P.barrier()
            ar.reset()
            NXB = 4
            xb = [ar.alloc([1024], F32) for _ in range(NXB)]
            xbB = [Buf("xb") for _ in range(NXB)]
            junk = ar.alloc([1024], BF16)
            junkB = Buf("junk")
            ssq = [ar.alloc([2], F32) for _ in range(NXB)]
            ssqB = [Buf("ssq") for _ in range(NXB)]
            xn = [ar.alloc([1024], BF16) for _ in range(NXB)]
            xnB = [Buf("xn") for _ in range(NXB)]
            tps = [banks[6 + j_][:, :].bitcast(BF16).rearrange("p (a b) -> p a b", b=128) for j_ in range(2)]
            def p0A(i):
                k = i % NXB
                P.dma("sp", xb[k], x[b, i * 128:(i + 1) * 128, :], r=[], w=[xbB[k]])
                P.act(junk, xb[k], AF.Square, r=[xbB[k]], w=[junkB, ssqB[k]], accum_out=ssq[k][:, 0:1])
                P.v("dve", "tensor_scalar", ssq[k][:, 1:2], ssq[k][:, 0:1], 1.0 / D, EPS, ALU.mult, ALU.add,
                    r=[ssqB[k]], w=[ssqB[k]])
                P.v("pool", "tensor_tensor", ssq[k][:, 1:2], ssq[k][:, 1:2], mhalf[:, 0:1], ALU.pow,
                    r=[ssqB[k], mhB], w=[ssqB[k]])

            def p0B(i):
                k = i % NXB
                tp = tps[i % 2]
                tpB = bankB[6 + (i % 2)]
                P.v("dve", "scalar_tensor_tensor", xn[k], xb[k], ssq[k][:, 1:2], normw_bc, ALU.mult, ALU.mult,
                    r=[xbB[k], ssqB[k], nwB], w=[xnB[k]])
                for kc in range(8):
                    P.tr(tp[:, kc, :], xn[k][:, kc * 128:(kc + 1) * 128], ident,
                         r=[xnB[k], cbB], w=[tpB])

            def p0C(i):
                tp = tps[i % 2]
                tpB = bankB[6 + (i % 2)]
                P.act(hT[:, :, i * 128:(i + 1) * 128], tp, AF.Copy, r=[tpB], w=[hTB[i // 4]])

            for n in range(NCH + 2):
                if n < NCH:
                    p0A(n)
                if 0 <= n - 1 < NCH:
                    p0B(n - 1)
                if 0 <= n - 2 < NCH:
                    p0C(n - 2)
            if "hT" in dbg:
                P.dma("sp", dbg_t["hT"], hT, r=hTB, w=[Buf()])

            if "noA" not in dbg:
                P.barrier()
                ar.reset()
                Wt = [[ar.alloc([8, 128], BF16) for _ in range(4)] for _ in range(2)]
                WtB = [[Buf("Wt") for _ in range(4)] for _ in range(2)]
                qT = ar.alloc([S], BF16)
                kT = ar.alloc([S], BF16)
                szT = ar.alloc([S], BF16)
                vv = ar.alloc([NCH, 128], BF16)
                sqv = [ar.alloc([512], BF16) for _ in range(2)]
                qb = [ar.alloc([512], F32) for _ in range(2)]
                lnr = [ar.alloc([512], F32) for _ in range(2)]
                sg = [ar.alloc([512], F32) for _ in range(2)]
                ebuf = [ar.alloc([1024], F32) for _ in range(2)]
                Lb = [ar.alloc([1024], BF16) for _ in range(3)]
                Ab = [ar.alloc([1024], BF16) for _ in range(3)]
                Lacc = [ar.alloc([512], BF16) for _ in range(2)]
                yt = [ar.alloc([512], BF16) for _ in range(2)]
                sqvB = [Buf() for _ in range(2)]
                qbB = [Buf() for _ in range(2)]
                lnrB = [Buf() for _ in range(2)]
                sgB = [Buf() for _ in range(2)]
                eB = [Buf() for _ in range(2)]
                LB = [Buf() for _ in range(3)]
                AB = [Buf() for _ in range(3)]
                LaccB = [Buf() for _ in range(2)]
                ytB = [Buf() for _ in range(2)]
                SEG = ["sq", "sk", "sv", "sz"]
                brow = ar.alloc([1024], BF16, parts=1)
                browB = Buf("brow")
                P.dma("pool", brow, b_in[OFF["sv"]:OFF["sv"] + 1024].unsqueeze(0), r=[], w=[browB])

                def loadA(h):
                    for i_, key in enumerate(SEG):
                        load_w("pool", Wt[h % 2][i_], OFF[key] + h * 128, 128, [WtB[h % 2][i_]])

                loadA(0)
                cnt = 0
                NHA = 1 if "h1" in dbg else 8
                for h in range(NHA):
                    if h + 1 < NHA:
                        loadA(h + 1)
                    Wq, Wk, Wv, Wz = Wt[h % 2]
                    WqB, WkB, WvB, WzB = WtB[h % 2]
                    qTB = [Buf() for _ in range(NT)]
                    kTB = [Buf() for _ in range(NT)]
                    szB = [Buf() for _ in range(NT)]
                    vB = [Buf() for _ in range(NT)]

                    def proj_fm(Wtile, WB, tt, pb):
                        for kc in range(8):
                            P.mm(banks[pb][:, :], Wtile[:, kc, :], hT[:, kc, tt * 512:(tt + 1) * 512],
                                 kc == 0, kc == 7, r=[WB, hTB[tt]], w=[bankB[pb]])

                    for tt in range(NT if "A0" not in dbg else 0):
                        pb = 4 + (cnt % 2)
                        k2 = cnt % 2
                        cnt += 1
                        proj_fm(Wz, WzB, tt, pb)
                        bz = cols[:, 16 + h:17 + h]
                        P.act(sg[k2], banks[pb][:, :], AF.Sigmoid, bias=bz, r=[bankB[pb], colsB], w=[sgB[k2]])
                        P.v("dve", "scalar_tensor_tensor", szT[:, tt * 512:(tt + 1) * 512], banks[pb][:, :], bz,
                            sg[k2], ALU.add, ALU.mult, r=[bankB[pb], sgB[k2], colsB], w=[szB[tt]])
                    for g in range(NT if ("A0" not in dbg and "A1" not in dbg) else 0):
                        pb = 4 + (cnt % 2)
                        cnt += 1
                        for c4_ in range(4):
                            j = g * 4 + c4_
                            o_ = banks[pb][:, c4_ * 128:(c4_ + 1) * 128]
                            for kc in range(8):
                                P.mm(o_, hT[:, kc, j * 128:(j + 1) * 128], Wv[:, kc, :], kc == 0, False,
                                     r=[WvB, hTB[g]], w=[bankB[pb]])
                            P.mm(o_, ones_b[0:1, 0:128], brow[0:1, h * 128:(h + 1) * 128],
                                 False, True, r=[cbB, browB], w=[bankB[pb]])
                        P.act(vv[:, g * 4:(g + 1) * 4, :],
                              banks[pb][:, :].rearrange("p (a b) -> p a b", b=128), AF.Copy,
                              r=[bankB[pb]], w=[vB[g]])
                    for (Wx, WxB, dest, destB, wc, bwc, bc) in (
                            (Wq, WqB, qT, qTB, cols[:, 24:25], cols[:, 32 + h:33 + h], cols[:, h:h + 1]),
                            (Wk, WkB, kT, kTB, cols[:, 25:26], cols[:, 40 + h:41 + h], cols[:, 8 + h:9 + h])):
                        def SA(tt, Wx=Wx, WxB=WxB, wc=wc, bwc=bwc, bc=bc):
                            pb = 4 + (tt % 2)
                            k2 = tt % 2
                            proj_fm(Wx, WxB, tt, pb)
                            P.act(sqv[k2], banks[pb][:, :], AF.Square, bias=bc, r=[bankB[pb], colsB], w=[sqvB[k2]])
                            P.v("dve", "tensor_scalar", qb[k2], banks[pb][:, :], wc, bwc, ALU.mult, ALU.add,
                                r=[bankB[pb], colsB], w=[qbB[k2]])

                        def SB(tt, dest=dest, destB=destB):
                            k2 = tt % 2
                            P.mm(banks[3][:, :], ones_b, sqv[k2], True, True, r=[cbB, sqvB[k2]], w=[bankB[3]])
                            P.act(lnr[k2], banks[3][:, :], AF.Ln, bias=epsq[:, 0:1], r=[bankB[3], mhB], w=[lnrB[k2]])
                            P.act(lnr[k2], lnr[k2], AF.Exp, scale=-0.5, r=[lnrB[k2]], w=[lnrB[k2]])
                            P.v("dve", "tensor_tensor", dest[:, tt * 512:(tt + 1) * 512], qb[k2], lnr[k2], ALU.mult,
                                r=[qbB[k2], lnrB[k2]], w=[destB[tt]])

                        NTQ = NT if not (dbg & {"A0", "A1", "A2"}) else 0
                        for tt in range(NTQ + 1):
                            if tt < NTQ:
                                SA(tt)
                            if 0 <= tt - 1 < NTQ:
                                SB(tt - 1)
                    steps = [(qt, kbH) for qt in range(NT) for kbH in range(4 * qt + 3, 0, -2)]
                    NS = len(steps)
                    cur = [0]

                    def mask_of(kb, qt):
                        i_ = kb - 4 * qt
                        if i_ < 0:
                            return None
                        return sbmask[:, 384 - 128 * i_:384 - 128 * i_ + 512]

                    def stA1(n):
                        qt, kbH = steps[n]
                        z = n % 3
                        for j_, kb in enumerate((kbH, kbH - 1)):
                            P.mm(banks[2 * z + j_][:, :], kT[:, kb * 128:(kb + 1) * 128],
                                 qT[:, qt * 512:(qt + 1) * 512], True, True,
                                 r=[kTB[kb // 4], qTB[qt]], w=[bankB[2 * z + j_]])
                        P.act(ebuf[n % 2], psum[:, 2 * z * 512:(2 * z + 2) * 512], AF.Exp,
                              r=[bankB[2 * z], bankB[2 * z + 1]], w=[eB[n % 2]])

                    def stA2(n):
                        qt, kbH = steps[n]
                        P.act(Lb[n % 3], ebuf[n % 2], AF.Ln, bias=onec[:, 0:1], r=[eB[n % 2], mhB], w=[LB[n % 3]])
                        for j_, kb in enumerate((kbH, kbH - 1)):
                            m = mask_of(kb, qt)
                            if m is not None:
                                hv = Lb[n % 3][:, j_ * 512:(j_ + 1) * 512]
                                P.v("dve", "tensor_tensor", hv, hv, m, ALU.mult, r=[LB[n % 3], cbB], w=[LB[n % 3]])

                    def stB(n):
                        qt, kbH = steps[n]
                        z = n % 3
                        first = (kbH == 4 * qt + 3)
                        last = (kbH == 1)
                        LH = Lb[n % 3][:, 0:512]
                        LL = Lb[n % 3][:, 512:1024]
                        c_ = cur[0]
                        bH, bL = 2 * z, 2 * z + 1
                        P.mm(banks[bH][:, :], negU, LH, False, True, r=[cbB, LB[n % 3]], w=[bankB[bH]])
                        if not first:
                            P.mm(banks[bH][:, :], negones, Lacc[c_], False, True, r=[cbB, LaccB[c_]], w=[bankB[bH]])
                        P.mm(banks[bL][:, :], negU, LL, False, True, r=[cbB, LB[n % 3]], w=[bankB[bL]])
                        P.mm(banks[bL][:, :], negones, LH, False, True, r=[cbB, LB[n % 3]], w=[bankB[bL]])
                        if not first:
                            P.mm(banks[bL][:, :], negones, Lacc[c_], False, True, r=[cbB, LaccB[c_]], w=[bankB[bL]])
                        if not last:
                            if first:
                                P.v("dve", "tensor_tensor", Lacc[0], LH, LL, ALU.add, r=[LB[n % 3]], w=[LaccB[0]])
                                cur[0] = 0
                            else:
                                P.v("dve", "tensor_tensor", Lacc[1 - c_], Lacc[c_], LH, ALU.add,
                                    r=[LaccB[c_], LB[n % 3]], w=[LaccB[1 - c_]])
                                P.v("dve", "tensor_tensor", Lacc[1 - c_], Lacc[1 - c_], LL, ALU.add,
                                    r=[LaccB[1 - c_], LB[n % 3]], w=[LaccB[1 - c_]])
                                cur[0] = 1 - c_

                    def stB2(n):
                        qt, kbH = steps[n]
                        z = n % 3
                        P.act(Ab[n % 3], psum[:, 2 * z * 512:(2 * z + 2) * 512], AF.Exp,
                              r=[bankB[2 * z], bankB[2 * z + 1]], w=[AB[n % 3]])
                        for j_, kb in enumerate((kbH, kbH - 1)):
                            m = mask_of(kb, qt)
                            if m is not None:
                                hv = Ab[n % 3][:, j_ * 512:(j_ + 1) * 512]
                                P.v("dve", "tensor_tensor", hv, hv, m, ALU.mult, r=[AB[n % 3], cbB], w=[AB[n % 3]])

                    def stC(n):
                        qt, kbH = steps[n]
                        first = (kbH == 4 * qt + 3)
                        last = (kbH == 1)
                        ob = 6 + (qt % 2)
                        P.mm(banks[ob][:, :], vv[:, kbH, :], Ab[n % 3][:, 0:512], first, False,
                             r=[vB[kbH // 4], AB[n % 3]], w=[bankB[ob]])
                        P.mm(banks[ob][:, :], vv[:, kbH - 1, :], Ab[n % 3][:, 512:1024], False, last,
                             r=[vB[(kbH - 1) // 4], AB[n % 3]], w=[bankB[ob]])
                        if last:
                            y2 = qt % 2
                            P.v("dve", "tensor_tensor", yt[y2], banks[ob][:, :], szT[:, qt * 512:(qt + 1) * 512],
                                ALU.mult, r=[bankB[ob], szB[qt]], w=[ytB[y2]])
                            P.dma("sp", ys_d[b, h, :, qt * 512:(qt + 1) * 512], yt[y2], r=[ytB[y2]],
                                  w=[ysB[b][h][qt]])

                    if "noattn" in dbg:
                        NS = -3
                    for n in range(NS + 3):
                        if n < NS:
                            stA1(n)
                        if 0 <= n - 1 < NS:
                            stB(n - 1)
                        if 0 <= n - 2 < NS:
                            stB2(n - 2)
                        if n < NS:
                            stA2(n)
                        if 0 <= n - 3 < NS:
                            stC(n - 3)
            if "ys" in dbg:
                P.barrier()
                P.dma("sp", dbg_t["ys"], ys_d[b], r=[], w=[Buf()])

            if "noB" not in dbg:
                P.barrier()
                ar.reset()
                Wg8 = ar.alloc([8, 8], BF16)
                Wg8B = Buf()
                load_w("pool", Wg8, OFF["mi"], 8, [Wg8B])
                A1 = ar.alloc([S], F32, parts=4)
                A2 = ar.alloc([S], F32, parts=4)
                A3 = ar.alloc([S], F32, parts=4)
                A1B, A2B, A3B = Buf(), Buf(), Buf()
                Pt = ar.alloc([NCH + 1], F32, parts=4)
                PtB = Buf()
                cr = ar.alloc([NCH], F32, parts=4)
                crB = Buf()
                one4 = ar.alloc([2], F32, parts=4)
                one4B = Buf()
                P.v("dve", "memset", one4, 1.0, r=[], w=[one4B])
                for tt in range(NT):
                    sl = slice(tt * 512, (tt + 1) * 512)
                    for kc in range(8):
                        P.mm(banks[5][0:4, :], Wg8[:, kc, 0:4], hT[:, kc, sl], kc == 0, kc == 7,
                             r=[Wg8B, hTB[tt]], w=[bankB[5]])
                    P.act(A1[:, sl], banks[5][0:4, :], AF.Identity, bias=gb[:, 0:1], r=[bankB[5], gbB], w=[A1B])
                    for kc in range(8):
                        P.mm(banks[6][0:4, :], Wg8[:, kc, 4:8], hT[:, kc, sl], kc == 0, kc == 7,
                             r=[Wg8B, hTB[tt]], w=[bankB[6]])
                    P.act(A2[:, sl], banks[6][0:4, :], AF.Exp, scale=-1.0, bias=gb[:, 2:3],
                          r=[bankB[6], gbB], w=[A2B])
                P.act(A2, A2, AF.Ln, bias=one4[:, 0:1], r=[A2B, one4B], w=[A2B])
                P.v("dve", "tensor_tensor_scan", A3, one4[:, 0:1].to_broadcast([4, S]), A2, 0.0,
                    ALU.mult, ALU.subtract, r=[A2B, one4B], w=[A3B])
                P.v("dve", "tensor_tensor", A1, A1, A3, ALU.subtract, r=[A1B, A3B], w=[A1B])
                P.v("dve", "tensor_tensor_scan", A2, A1, A1, 0.0, ALU.max, ALU.max, r=[A1B], w=[A2B])
                P.v("dve", "memset", Pt[:, 0:1], 0.0, r=[], w=[PtB])
                P.v("dve", "tensor_copy", Pt[:, 1:NCH + 1],
                    A2.rearrange("p (c t) -> p c t", t=128)[:, :, 127], r=[A2B], w=[PtB])
                A1v = A1.rearrange("p (c t) -> p c t", t=128)
                A2v = A2.rearrange("p (c t) -> p c t", t=128)
                A3v = A3.rearrange("p (c t) -> p c t", t=128)
                Plo = Pt[:, 0:NCH].unsqueeze(2).to_broadcast([4, NCH, 128])
                Phi = Pt[:, 1:NCH + 1].unsqueeze(2).to_broadcast([4, NCH, 128])
                colps = banks[7]
                for slot in range(3):
                    if slot == 0:
                        P.v("dve", "tensor_tensor", A2v, A1v, Plo, ALU.subtract, r=[A1B, PtB], w=[A2B])
                        P.act(A2, A2, AF.Exp, bias=ml16[0:4, 0:1], r=[A2B, gbB], w=[A2B])
                    elif slot == 1:
                        P.v("dve", "tensor_tensor", A2v, A1v, Phi, ALU.subtract, r=[A1B, PtB], w=[A2B])
                        P.act(A2, A2, AF.Exp, bias=ml16[0:4, 0:1], r=[A2B, gbB], w=[A2B])
                    else:
                        P.v("dve", "tensor_tensor", A2v, A3v, Plo, ALU.add, r=[A3B, PtB], w=[A2B])
                        P.act(A2, A2, AF.Exp, scale=-1.0, r=[A2B], w=[A2B])
                    for c in range(NCH):
                        o0 = slot * NCH * 4 + c * 4
                        P.mm(colps[:, o0:o0 + 4], A2[:, c * 128:(c + 1) * 128], ident4, True, True,
                             r=[A2B, c4B], w=[bankB[7]])
                P.act(colv, colps[:, 0:3 * NCH * 4], AF.Copy, r=[bankB[7]], w=[colvB])
                P.v("dve", "tensor_tensor", cr, Pt[:, 0:NCH], Pt[:, 1:NCH + 1], ALU.subtract, r=[PtB], w=[crB])
                P.act(cr, cr, AF.Exp, r=[crB], w=[crB])
                for h in range(4):
                    P.mm(banks[6][:, h * NCH:(h + 1) * NCH], c4[:, 8 + h * 128:8 + (h + 1) * 128], cr, True, True,
                         r=[c4B, crB], w=[bankB[6]])
                P.act(carry_bc, banks[6][:, 0:4 * NCH], AF.Copy, r=[bankB[6]], w=[carryB])
                if "gates" in dbg:
                    P.dma("sp", dbg_t["colv"], colv, r=[colvB], w=[Buf()])
                    P.dma("sp", dbg_t["carry"], carry_bc, r=[carryB], w=[Buf()])

                P.barrier()
                ar.reset()
                Wm = [[ar.alloc([8, 256], BF16) for _ in range(5)] for _ in range(2)]
                WmB = [[Buf() for _ in range(5)] for _ in range(2)]
                raw = [ar.alloc([516], BF16) for _ in range(4)]
                rawB = [Buf() for _ in range(4)]
                dwc = ar.alloc([16, 128], BF16)
                dwcB = Buf()
                acc = [ar.alloc([512], F32) for _ in range(2)]
                accB = [Buf() for _ in range(2)]
                sgm = [ar.alloc([512], F32) for _ in range(2)]
                sgmB = [Buf() for _ in range(2)]
                qTt = [ar.alloc([2, 512], BF16) for _ in range(2)]
                kTt = [ar.alloc([2, 512], BF16) for _ in range(2)]
                qTtB = [Buf() for _ in range(2)]
                kTtB = [Buf() for _ in range(2)]
                vaug = [ar.alloc([4, 264], BF16) for _ in range(2)]
                vaugB = [Buf() for _ in range(2)]
                ogt = [ar.alloc([4, 256], BF16) for _ in range(2)]
                ogtB = [Buf() for _ in range(2)]
                sgo = [ar.alloc([512], F32) for _ in range(2)]
                sgoB = [Buf() for _ in range(2)]
                t1 = [ar.alloc([256], F32) for _ in range(2)]
                t1B = [Buf() for _ in range(2)]
                Cf = ar.alloc([2, 257], F32)
                CfB = Buf()
                Cbf = ar.alloc([2, 258], BF16)
                CbfB = Buf()
                scT = [ar.alloc([128], BF16) for _ in range(2)]
                scTB = [Buf() for _ in range(2)]
                kwt = [ar.alloc([256], BF16) for _ in range(2)]
                kwtB = [Buf() for _ in range(2)]
                ymt = [ar.alloc([256], BF16) for _ in range(2)]
                ymtB = [Buf() for _ in range(2)]
                ymT = [ar.alloc([2, 512], BF16) for _ in range(2)]
                ymTB = [Buf() for _ in range(2)]
                sc = [ar.alloc([8], F32) for _ in range(2)]
                scB = [Buf() for _ in range(2)]
                junk2 = ar.alloc([256], BF16)
                junk2B = Buf()
                for k_ in range(2):
                    P.v("dve", "memset", vaug[k_][:, :, 256:257], 1.0, r=[], w=[vaugB[k_]])
                MSEG = ["mq", "mk", "mv", "mo", "mz"]
                brow = ar.alloc([3072], BF16, parts=1)
                browB = Buf("brow")
                for i_, key in enumerate(["mv", "mo", "mz"]):
                    P.dma("pool", brow[:, i_ * 1024:(i_ + 1) * 1024],
                          b_in[OFF[key]:OFF[key] + 1024].unsqueeze(0), r=[], w=[browB])

                def loadB(h):
                    for i_, key in enumerate(MSEG):
                        load_w("pool", Wm[h % 2][i_], OFF[key] + h * 256, 256, [WmB[h % 2][i_]])

                loadB(0)
                pcnt = [0]
                NHB = 1 if "h1" in dbg else 4
                tpm = banks[3][:, :].bitcast(BF16).rearrange("p (a b) -> p a b", b=128)
                tpk = banks[0][:, 256:512].bitcast(BF16).rearrange("p (a b) -> p a b", b=128)
                for h in range(NHB):
                    if h + 1 < NHB:
                        loadB(h + 1)
                    Wq_, Wk_, Wv_, Wo_, Wz_ = Wm[h % 2]
                    WqB_, WkB_, WvB_, WoB_, WzB_ = WmB[h % 2]
                    P.v("dve", "memset", Cf, 0.0, r=[], w=[CfB])
                    for i_ in range(4):
                        P.v("dve", "memset", raw[i_][:, 0:3], 0.0, r=[], w=[rawB[i_]])
                    for qk_ in range(2):
                        for dc_ in range(2):
                            ci_ = qk_ * 8 + h * 2 + dc_
                            for j_ in range(4):
                                P.v("dve", "tensor_scalar", dwc[:, (qk_ * 2 + dc_) * 4 + j_, :], ident,
                                    mcols[:, j_ * 16 + ci_:j_ * 16 + ci_ + 1], None, ALU.mult,
                                    r=[cbB, mcolsB], w=[dwcB])

                    def proj_pieces(tt, h=h, Wq_=Wq_, Wk_=Wk_, Wv_=Wv_, Wo_=Wo_, Wz_=Wz_,
                                    WqB_=WqB_, WkB_=WkB_, WvB_=WvB_, WoB_=WoB_, WzB_=WzB_):
                        k2 = tt % 2
                        sl = slice(tt * 512, (tt + 1) * 512)
                        pieces = []

                        def qk_piece(qk, dc):
                            ci = qk * 8 + h * 2 + dc
                            ri = qk * 2 + dc
                            Wx, WxB = (Wq_, WqB_) if qk == 0 else (Wk_, WkB_)
                            a2 = pcnt[0] % 2
                            pcnt[0] += 2
                            for kc in range(8):
                                P.mm(banks[6][:, :], Wx[:, kc, dc * 128:(dc + 1) * 128], hT[:, kc, sl],
                                     kc == 0, kc == 7, r=[WxB, hTB[tt]], w=[bankB[6]])
                            if tt > 0:
                                P.v("dve", "tensor_copy", raw[ri][:, 0:3], raw[ri][:, 512:515],
                                    r=[rawB[ri]], w=[rawB[ri]])
                            P.act(raw[ri][:, 3:515], banks[6][:, :], AF.Identity, bias=mcols[:, 80 + ci:81 + ci],
                                  r=[bankB[6], mcolsB], w=[rawB[ri]])
                            for j in range(4):
                                P.mm(banks[7][:, :], dwc[:, ri * 4 + j, :], raw[ri][:, j:j + 512], j == 0, j == 3,
                                     r=[dwcB, rawB[ri]], w=[bankB[7]])
                            P.act(sgm[a2], banks[7][:, :], AF.Sigmoid, bias=mcols[:, 64 + ci:65 + ci],
                                  r=[bankB[7], mcolsB], w=[sgmB[a2]])
                            dst, dstB = (qTt, qTtB) if qk == 0 else (kTt, kTtB)
                            P.v("dve", "scalar_tensor_tensor", dst[k2][:, dc, :], banks[7][:, :],
                                mcols[:, 64 + ci:65 + ci], sgm[a2], ALU.add, ALU.mult,
                                r=[bankB[7], sgmB[a2], mcolsB], w=[dstB[k2]])

                        def v_piece(g2):
                            pb = 6 + (pcnt[0] % 2)
                            pcnt[0] += 1
                            for c2 in range(2):
                                c4_ = g2 * 2 + c2
                                j = tt * 4 + c4_
                                o_ = banks[pb][:, c2 * 256:(c2 + 1) * 256]
                                for kc in range(8):
                                    P.mm(o_, hT[:, kc, j * 128:(j + 1) * 128], Wv_[:, kc, :], kc == 0, False,
                                         r=[WvB_, hTB[tt]], w=[bankB[pb]])
                                P.mm(o_, ones_b[0:1, 0:128], brow[0:1, h * 256:(h + 1) * 256], False, True,
                                     r=[cbB, browB], w=[bankB[pb]])
                            P.act(vaug[k2][:, g2 * 2:g2 * 2 + 2, 0:256],
                                  banks[pb][:, :].rearrange("p (a b) -> p a b", b=256), AF.Copy,
                                  r=[bankB[pb]], w=[vaugB[k2]])

                        def og_piece(c4_):
                            j = tt * 4 + c4_
                            pb = 6 + (pcnt[0] % 2)
                            a2 = pcnt[0] % 2
                            pcnt[0] += 1
                            for half, (Wx, WxB, boff) in enumerate(((Wo_, WoB_, 1024), (Wz_, WzB_, 2048))):
                                o_ = banks[pb][:, half * 256:(half + 1) * 256]
                                for kc in range(8):
                                    P.mm(o_, hT[:, kc, j * 128:(j + 1) * 128], Wx[:, kc, :], kc == 0, False,
                                         r=[WxB, hTB[tt]], w=[bankB[pb]])
                                P.mm(o_, ones_b[0:1, 0:128], brow[0:1, boff + h * 256:boff + (h + 1) * 256],
                                     False, True, r=[cbB, browB], w=[bankB[pb]])
                            P.act(sgo[a2], banks[pb][:, :], AF.Sigmoid, r=[bankB[pb]], w=[sgoB[a2]])
                            P.v("dve", "tensor_tensor", t1[a2], sgo[a2][:, 0:256], sgo[a2][:, 256:512], ALU.mult,
                                r=[sgoB[a2]], w=[t1B[a2]])
                            P.v("dve", "tensor_tensor", ogt[k2][:, c4_, :], t1[a2], banks[pb][:, 256:512], ALU.mult,
                                r=[t1B[a2], bankB[pb]], w=[ogtB[k2]])

                        for qk in range(2):
                            for dc in range(2):
                                pieces.append(lambda qk=qk, dc=dc: qk_piece(qk, dc))
                        for g2 in range(2):
                            pieces.append(lambda g2=g2: v_piece(g2))
                        for c4_ in range(4):
                            pieces.append(lambda c4_=c4_: og_piece(c4_))
                        return pieces

                    def chunk_parts(g, h=h):
                        tt, c4_ = divmod(g, 4)
                        c = g
                        k2 = tt % 2
                        bl = slice(c4_ * 128, (c4_ + 1) * 128)
                        s2 = g % 2
                        hb = 1 + (g % 2)
                        es = colv[:, c * 4 + h:c * 4 + h + 1]
                        es2 = colv[:, NCH * 4 + c * 4 + h:NCH * 4 + c * 4 + h + 1]
                        dnm = colv[:, 2 * NCH * 4 + c * 4 + h:2 * NCH * 4 + c * 4 + h + 1]
                        car = carry_bc[:, h * NCH + c:h * NCH + c + 1]
                        upd = c < NCH - 1

                        def early():
                            for dc in range(2):
                                P.mm(banks[0][:, 0:128], kTt[k2][:, dc, bl], qTt[k2][:, dc, bl], dc == 0, dc == 1,
                                     r=[kTtB[k2], qTtB[k2]], w=[bankB[0]])
                            P.v("dve", "scalar_tensor_tensor", scT[s2], banks[0][:, 0:128], es, trimask,
                                ALU.mult, ALU.mult, r=[bankB[0], colvB, cbB], w=[scTB[s2]])
                            if upd:
                                for dc in range(2):
                                    P.tr(tpk[:, dc, :], kTt[k2][:, dc, bl], ident, r=[kTtB[k2], cbB], w=[bankB[0]])
                                P.act(kwt[s2].rearrange("p (a b) -> p a b", b=128), tpk[:, 0:2, :], AF.Copy,
                                      scale=es2, r=[bankB[0], colvB], w=[kwtB[s2]])
                            P.mm(banks[hb][:, 0:257], scT[s2], vaug[k2][:, c4_, 0:257], True, c == 0,
                                 r=[scTB[s2], vaugB[k2]], w=[bankB[hb]])
                            if c > 0:
                                for dc in range(2):
                                    P.mm(banks[hb][:, 0:257], qTt[k2][:, dc, bl], Cbf[:, dc, 0:257], False, dc == 1,
                                         r=[qTtB[k2], CbfB], w=[bankB[hb]])
                            if upd:
                                for dc in range(2):
                                    P.mm(banks[4 + dc][:, 0:257], kwt[s2][:, dc * 128:(dc + 1) * 128],
                                         vaug[k2][:, c4_, 0:257], True, True, r=[kwtB[s2], vaugB[k2]],
                                         w=[bankB[4 + dc]])
                                    P.v("dve", "scalar_tensor_tensor", Cf[:, dc, :], Cf[:, dc, :], car,
                                        banks[4 + dc][:, 0:257], ALU.mult, ALU.add,
                                        r=[CfB, carryB, bankB[4 + dc]], w=[CfB])
                                P.act(Cbf[:, :, 0:257], Cf, AF.Copy, r=[CfB], w=[CbfB])

                        def epi1():
                            P.act(junk2, banks[hb][:, 0:256], AF.Square, r=[bankB[hb]], w=[junk2B, scB[s2]],
                                  accum_out=sc[s2][:, 0:1])
                            P.act(sc[s2][:, 6:7], banks[hb][:, 256:257], AF.Abs, r=[bankB[hb]], w=[scB[s2]])
                            P.v("dve", "tensor_scalar", sc[s2][:, 1:2], sc[s2][:, 6:7], dnm, None,
                                ALU.max, r=[scB[s2], colvB], w=[scB[s2]])
                            P.v("dve", "reciprocal", sc[s2][:, 2:3], sc[s2][:, 1:2], r=[scB[s2]], w=[scB[s2]])
                            P.v("dve", "scalar_tensor_tensor", sc[s2][:, 3:4], sc[s2][:, 0:1], sc[s2][:, 2:3],
                                sc[s2][:, 2:3], ALU.mult, ALU.mult, r=[scB[s2]], w=[scB[s2]])
                            P.v("dve", "tensor_scalar", sc[s2][:, 3:4], sc[s2][:, 3:4], 1.0 / 256.0, EPS,
                                ALU.mult, ALU.add, r=[scB[s2]], w=[scB[s2]])
                            P.v("pool", "tensor_tensor", sc[s2][:, 4:5], sc[s2][:, 3:4], mhalf[:, 0:1], ALU.pow,
                                r=[scB[s2], mhB], w=[scB[s2]])

                        def epi2():
                            P.v("dve", "tensor_tensor", sc[s2][:, 5:6], sc[s2][:, 4:5], sc[s2][:, 2:3], ALU.mult,
                                r=[scB[s2]], w=[scB[s2]])
                            P.v("dve", "scalar_tensor_tensor", ymt[s2], banks[hb][:, 0:256], sc[s2][:, 5:6],
                                ogt[k2][:, c4_, :], ALU.mult, ALU.mult, r=[bankB[hb], scB[s2], ogtB[k2]],
                                w=[ymtB[s2]])

                        def fin():
                            for ec in range(2):
                                P.tr(tpm[:, ec, :], ymt[s2][:, ec * 128:(ec + 1) * 128], ident,
                                     r=[ymtB[s2], cbB], w=[bankB[3]])
                            P.act(ymT[k2][:, :, bl], tpm[:, 0:2, :], AF.Copy, r=[bankB[3]], w=[ymTB[k2]])
                            if c4_ == 3:
                                sl = slice(tt * 512, (tt + 1) * 512)
                                P.dma("sp", ym_d[b, 2 * h:2 * h + 2, :, sl].rearrange("c p t -> p c t"), ymT[k2],
                                      r=[ymTB[k2]], w=[ymB[b][2 * h][tt], ymB[b][2 * h + 1][tt]])

                        return early, epi1, epi2, fin

                    for pc in proj_pieces(0):
                        pc()
                    parts = {}
                    nxt = []
                    for g in range(NCH + 2):
                        if g < NCH:
                            tt, c4_ = divmod(g, 4)
                            if c4_ == 0:
                                nxt = proj_pieces(tt + 1) if tt + 1 < NT else []
                            lo = (len(nxt) * c4_) // 4
                            hi = (len(nxt) * (c4_ + 1)) // 4
                            for pc in nxt[lo:hi]:
                                pc()
                            parts[g] = chunk_parts(g)
                            parts[g][0]()
                        if 0 <= g - 1 < NCH:
                            parts[g - 1][2]()
                        if g < NCH:
                            parts[g][1]()
                        if 0 <= g - 2 < NCH:
                            parts[g - 2][3]()
            if "ym" in dbg:
                P.barrier()
                P.dma("sp", dbg_t["ym"], ym_d[b], r=[], w=[Buf()])

            if "noC" not in dbg:
                P.barrier()
                ar.reset()
                TC = 256
                NTC = S // TC
                Wpm_t = ar.alloc([8, 1024], BF16)
                Wps_t = ar.alloc([8, 1024], BF16)
                Wo_t = ar.alloc([8, 1024], BF16)
                Wg_t = ar.alloc([8, 2048], BF16)
                WpmB, WpsB, WoB, WgB = Buf(), Buf(), Buf(), Buf()
                mnwc = ar.alloc([8], F32)
                bgc = ar.alloc([16], F32)
                smB = Buf()
                ymt_c = [ar.alloc([8, TC], BF16) for _ in range(2)]
                yst_c = [ar.alloc([8, TC], BF16) for _ in range(2)]
                ymcB = [Buf() for _ in range(2)]
                yscB = [Buf() for _ in range(2)]
                sga = [ar.alloc([TC], F32) for _ in range(2)]
                sgb = [ar.alloc([TC], F32) for _ in range(2)]
                m1 = [ar.alloc([TC], F32) for _ in range(2)]
                m2 = [ar.alloc([TC], F32) for _ in range(2)]
                sgaB = [Buf() for _ in range(2)]
                sgbB = [Buf() for _ in range(2)]
                m1B = [Buf() for _ in range(2)]
                m2B = [Buf() for _ in range(2)]
                mrg = [ar.alloc([8, TC], BF16) for _ in range(2)]
                mrgB = [Buf() for _ in range(2)]
                xt = [ar.alloc([1024], F32) for _ in range(2)]
                xtB = [Buf() for _ in range(2)]
                for wt_, wb_, src in ((Wpm_t, WpmB, w_pm), (Wps_t, WpsB, w_ps), (Wo_t, WoB, w_out)):
                    s3 = src.rearrange("(kc p) n -> p kc n", p=128)
                    for hf in range(2):
                        P.dma("pool", wt_[:, :, hf * 512:(hf + 1) * 512], s3[:, :, hf * 512:(hf + 1) * 512],
                              r=[], w=[wb_])
                for hf in range(4):
                    P.dma("pool", Wg_t[:, :, hf * 512:(hf + 1) * 512],
                          w3[:, :, OFF["g"] + hf * 512:OFF["g"] + (hf + 1) * 512], r=[], w=[WgB])
                P.dma("sp", mnwc, mnw.rearrange("(c p) -> p c", p=128), r=[], w=[smB], slow=True)
                P.dma("sp", bgc, b_in[OFF["g"]:OFF["g"] + 2048].rearrange("(c p) -> p c", p=128), r=[], w=[smB],
                      slow=True)
                for kc in range(8):
                    P.v("dve", "tensor_scalar", Wpm_t[:, kc, :], Wpm_t[:, kc, :], mnwc[:, kc:kc + 1], None, ALU.mult,
                        r=[WpmB, smB], w=[WpmB])
                ymv = ym_d[b].rearrange("c p t -> p c t")
                ysv = ys_d[b].rearrange("c p t -> p c t")
                gcnt = 0
                for tc in range(NTC):
                    k2 = tc % 2
                    sl = slice(tc * TC, (tc + 1) * TC)
                    t512 = (tc * TC) // 512
                    P.dma("sp", ymt_c[k2], ymv[:, :, sl], r=[ymB[b][f_][t512] for f_ in range(8)], w=[ymcB[k2]])
                    P.dma("sp", yst_c[k2], ysv[:, :, sl], r=[ysB[b][f_][t512] for f_ in range(8)], w=[yscB[k2]])
                    for cc in range(8):
                        g2 = gcnt % 2
                        gcnt += 1
                        cs = slice(cc * 128, (cc + 1) * 128)
                        cs2 = slice(1024 + cc * 128, 1024 + (cc + 1) * 128)
                        bPm, bPs, bgm, bgs = banks[0 + g2], banks[2 + g2], banks[4 + g2], banks[6 + g2]
                        BPm, BPs, Bgm, Bgs = bankB[0 + g2], bankB[2 + g2], bankB[4 + g2], bankB[6 + g2]
                        for kc in range(8):
                            P.mm(bgm[:, 0:TC], Wg_t[:, kc, cs], hT[:, kc, sl], kc == 0, kc == 7,
                                 r=[WgB, hTB[t512]], w=[Bgm])
                        for kc in range(8):
                            P.mm(bgs[:, 0:TC], Wg_t[:, kc, cs2], hT[:, kc, sl], kc == 0, kc == 7,
                                 r=[WgB, hTB[t512]], w=[Bgs])
                        for kc in range(8):
                            P.mm(bPm[:, 0:TC], Wpm_t[:, kc, cs], ymt_c[k2][:, kc, :], kc == 0, kc == 7,
                                 r=[WpmB, ymcB[k2]], w=[BPm])
                        for kc in range(8):
                            P.mm(bPs[:, 0:TC], Wps_t[:, kc, cs], yst_c[k2][:, kc, :], kc == 0, kc == 7,
                                 r=[WpsB, yscB[k2]], w=[BPs])
                        P.act(sga[g2], bgm[:, 0:TC], AF.Sigmoid, bias=bgc[:, cc:cc + 1], r=[Bgm, smB], w=[sgaB[g2]])
                        P.act(sgb[g2], bgs[:, 0:TC], AF.Sigmoid, bias=bgc[:, 8 + cc:9 + cc], r=[Bgs, smB],
                              w=[sgbB[g2]])
                        P.v("dve", "tensor_tensor", m1[g2], bPm[:, 0:TC], sga[g2], ALU.mult, r=[BPm, sgaB[g2]],
                            w=[m1B[g2]])
                        P.v("dve", "tensor_tensor", m2[g2], bPs[:, 0:TC], sgb[g2], ALU.mult, r=[BPs, sgbB[g2]],
                            w=[m2B[g2]])
                        P.v("pool", "tensor_tensor", mrg[k2][:, cc, :], m1[g2], m2[g2], ALU.add,
                            r=[m1B[g2], m2B[g2]], w=[mrgB[k2]])
                    for sub in range(TC // 128):
                        row0 = tc * TC + sub * 128
                        x2 = (tc * (TC // 128) + sub) % 2
                        P.dma("sp", xt[x2], x[b, row0:row0 + 128, :], r=[], w=[xtB[x2]])
                        for eh in range(2):
                            g2 = gcnt % 2
                            gcnt += 1
                            bo, BO = banks[0 + g2], bankB[0 + g2]
                            for cc in range(8):
                                P.mm(bo[:, :], mrg[k2][:, cc, sub * 128:(sub + 1) * 128],
                                     Wo_t[:, cc, eh * 512:(eh + 1) * 512], cc == 0, cc == 7,
                                     r=[mrgB[k2], WoB], w=[BO])
                            P.v("dve", "tensor_tensor", xt[x2][:, eh * 512:(eh + 1) * 512], bo[:, :],
                                xt[x2][:, eh * 512:(eh + 1) * 512], ALU.add, r=[BO, xtB[x2]], w=[xtB[x2]])
                        P.dma("sp", out[b, row0:row0 + 128, :], xt[x2], r=[xtB[x2]], w=[Buf()])

        P.emit()
    return nc


_NC_CACHE = {}


def kernel(x, norm_w, w_in, b_in, conv_w, conv_b, mlstm_norm_w, sb_q_norm_w, sb_k_norm_w,
           w_proj_m, w_proj_s, w_out):
    n = 8
    B, S, _ = x.shape
    nseq = B // n
    nc = build(S, nseq)
    c, c4 = _consts()
    f = lambda a: np.ascontiguousarray(np.asarray(a, dtype=np.float32))
    shared = dict(norm_w=f(norm_w), w_in=f(w_in), b_in=f(b_in), conv_w=f(conv_w), conv_b=f(conv_b),
                  mlstm_norm_w=f(mlstm_norm_w), sb_q_norm_w=f(sb_q_norm_w), sb_k_norm_w=f(sb_k_norm_w),
                  w_proj_m=f(w_proj_m), w_proj_s=f(w_proj_s), w_out=f(w_out), cst=c, cst4=c4)
    xs = f(x)
    in_maps = [dict(shared, x=xs[i * nseq:(i + 1) * nseq]) for i in range(n)]
    res = run_bass_kernel_spmd(nc, in_maps, core_ids=list(range(n)))
    return np.concatenate([r["out"] for r in res.results], axis=0)
```
